# Optimizing a Trainium2 kernel written in Bass

```python
import jax
import jax.numpy as jnp
from jax import lax
import numpy as np

D_MODEL = 1024
BATCH = 32
SEQ = 2048
DEPTH = 2

CTX_LEN = 256
GRID_W = 64
HEAD_DIM = 64
N_GROUPS = 4
GROUP_W = D_MODEL // N_GROUPS
RET_HEADS = GROUP_W // HEAD_DIM
GQA_HEADS = GROUP_W // HEAD_DIM
GQA_KV_HEADS = GQA_HEADS // 2
SWA_HEADS = GROUP_W // HEAD_DIM
SWA_KV_HEADS = SWA_HEADS // 2
HGRN_HEADS = GROUP_W // HEAD_DIM
HGRN_EXPAND = 64
HGRN_FDIM = HGRN_HEADS * HGRN_EXPAND
D_FF = 4 * D_MODEL
Q_BLOCK = 128
WINDOW = 128
RET_CHUNK = 128
HGRN_CHUNK = 64
ROPE_THETA = 10000.0
ROPE_FREQS = HEAD_DIM // 4
ALPHA = (2 * DEPTH) ** 0.25
BETA = (8 * DEPTH) ** -0.25
EPS = 1e-6
NEG = -1e30
TINY = 1e-30
IN_SIZES = (GROUP_W, GROUP_W, GROUP_W, GROUP_W,
            GQA_HEADS * HEAD_DIM, GQA_KV_HEADS * HEAD_DIM, GQA_KV_HEADS * HEAD_DIM,
            SWA_HEADS * HEAD_DIM, SWA_KV_HEADS * HEAD_DIM, SWA_KV_HEADS * HEAD_DIM,
            HGRN_FDIM, HGRN_FDIM, HGRN_FDIM, GROUP_W, GROUP_W)
N_IN = sum(IN_SIZES)
F32 = jnp.float32

kernel_name = "hybrid_dit_retention_gqa_swa_hgrn2"


def _ln(x):
    x32 = x.astype(F32)
    mu = jnp.mean(x32, axis=-1, keepdims=True)
    var = jnp.mean(jnp.square(x32 - mu), axis=-1, keepdims=True)
    return ((x32 - mu) * lax.rsqrt(var + EPS)).astype(x.dtype)


def _rms(x):
    x32 = x.astype(F32)
    return (x32 * lax.rsqrt(jnp.mean(jnp.square(x32), axis=-1, keepdims=True) + EPS)).astype(x.dtype)


def _rope_tables(n):
    rows = n // GRID_W
    row = jnp.repeat(jnp.arange(rows), GRID_W).astype(F32)
    col = (jnp.arange(rows * GRID_W) % GRID_W).astype(F32)
    inv = ROPE_THETA ** (-jnp.arange(ROPE_FREQS, dtype=F32) / ROPE_FREQS)
    ang = jnp.stack([row[:, None] * inv, col[:, None] * inv], axis=0)
    return jnp.cos(ang), jnp.sin(ang)


def _rope(x, cos, sin):
    xs = x.reshape(*x.shape[:-1], 2, 2, ROPE_FREQS)
    c = jnp.moveaxis(cos, 0, 1)[None, :, None].astype(x.dtype)
    s = jnp.moveaxis(sin, 0, 1)[None, :, None].astype(x.dtype)
    x1, x2 = xs[..., 0, :], xs[..., 1, :]
    return jnp.stack([x1 * c - x2 * s, x2 * c + x1 * s], axis=-2).reshape(x.shape)


def _attend(q, k, v, mask=None, sink=None):
    s = jnp.einsum('btkgd,bskd->bkgts', q, k).astype(F32) * (HEAD_DIM ** -0.5)
    if mask is not None:
        s = jnp.where(mask, s, NEG)
    if sink is None:
        p = jax.nn.softmax(s, axis=-1)
    else:
        sk = sink.astype(F32)[None, :, :, None, None]
        m = jnp.maximum(jnp.max(s, axis=-1, keepdims=True), sk)
        e = jnp.exp(s - m)
        p = e / (jnp.sum(e, axis=-1, keepdims=True) + jnp.exp(sk - m))
    return jnp.einsum('bkgts,bskd->btkgd', p.astype(v.dtype), v)


def _chunk_recurrence(q, k, v, log_a, s0, chunk, with_output):
    B, H, L, _ = q.shape
    dv = v.shape[-1]
    n = L // chunk

    def split(t):
        return jnp.moveaxis(t.astype(F32).reshape(t.shape[0], t.shape[1], n, chunk, t.shape[-1]), 2, 0)

    causal = jnp.tril(jnp.ones((chunk, chunk), dtype=bool))[:, :, None]

    def step(S, xs):
        qc, kc, vc, gc = xs
        b = jnp.cumsum(gc, axis=-2)
        b_last = b[..., -1:, :]
        S_new = jnp.exp(jnp.swapaxes(b_last, -1, -2)) * S + jnp.einsum(
            'bhsd,bhse->bhde', kc * jnp.exp(b_last - b), vc)
        if not with_output:
            return S_new, None
        o_inter = jnp.einsum('bhtd,bhde->bhte', qc * jnp.exp(b), S)
        diff = b[..., :, None, :] - b[..., None, :, :]
        decay = jnp.where(causal, jnp.exp(jnp.where(causal, diff, 0.0)), 0.0)
        if gc.shape[-1] == 1:
            att = jnp.einsum('bhtd,bhsd->bhts', qc, kc) * decay[..., 0]
        else:
            att = jnp.einsum('bhtd,bhsd,bhtsd->bhts', qc, kc, decay)
        return S_new, o_inter + jnp.einsum('bhts,bhse->bhte', att, vc)

    S, os = lax.scan(step, s0.astype(F32), tuple(split(t) for t in (q, k, v, log_a)))
    if not with_output:
        return None, S
    return jnp.moveaxis(os, 0, 2).reshape(B, H, L, dv).astype(v.dtype), S


def _bidir_scan(q, ks, v, las, qc, kcs, vc, lacs, chunk, ctx_out):
    B, H, _, dk = q.shape
    dv = v.shape[-1]
    o_lat, o_ctx = 0.0, 0.0
    for d in range(2):
        fl = (lambda t: jnp.flip(t, axis=2)) if d == 1 else (lambda t: t)
        s0 = jnp.zeros((B, H, dk, dv), F32)
        oc, s_ctx = _chunk_recurrence(fl(qc), fl(kcs[d]), fl(vc), fl(lacs[d]), s0, chunk, ctx_out)
        o, _ = _chunk_recurrence(fl(q), fl(ks[d]), fl(v), fl(las[d]), s_ctx, chunk, True)
        o_lat = o_lat + fl(o)
        if ctx_out:
            o_ctx = o_ctx + fl(oc)
    return o_lat, (o_ctx if ctx_out else None)


def _retention(q, k, v, g, qc, kc, vc, gc, decay_logit, cos, sin, ctx_out):
    H = RET_HEADS

    def prep(t, rot):
        t = t.reshape(t.shape[0], t.shape[1], H, HEAD_DIM)
        if rot:
            t = _rope(t, cos, sin)
        return t.transpose(0, 2, 1, 3)

    kscale = HEAD_DIM ** -0.5
    q_, k_, v_ = prep(q, True), prep(k, True) * kscale, prep(v, False)
    qc_, kc_, vc_ = prep(qc, False), prep(kc, False) * kscale, prep(vc, False)
    L, Lc = q.shape[1], qc.shape[1]
    log_g = jax.nn.log_sigmoid(decay_logit.astype(F32))
    las = [jnp.broadcast_to(log_g[d][None, :, None, None], (1, H, L, 1)) for d in range(2)]
    lacs = [jnp.broadcast_to(log_g[d][None, :, None, None], (1, H, Lc, 1)) for d in range(2)]
    o, oc = _bidir_scan(q_, [k_, k_], v_, las, qc_, [kc_, kc_], vc_, lacs, RET_CHUNK, ctx_out)

    def finish(o, gate):
        on = _ln(o).transpose(0, 2, 1, 3).reshape(gate.shape).astype(gate.dtype)
        return on * jax.nn.silu(gate)

    return finish(o, g), (finish(oc, gc) if ctx_out else None)


def _global_gqa(q, k, v, qc, kc, vc, q_gain, k_gain, cos, sin, ctx_out):
    B, L, _ = q.shape
    Lc = qc.shape[1]
    KV, G = GQA_KV_HEADS, GQA_HEADS // GQA_KV_HEADS

    def qk(t, nh, gain):
        return _rms(t.reshape(t.shape[0], t.shape[1], nh, HEAD_DIM)) * gain

    q_ = _rope(qk(q, GQA_HEADS, q_gain), cos, sin)
    k_ = _rope(qk(k, KV, k_gain), cos, sin)
    v_ = v.reshape(B, L, KV, HEAD_DIM)
    kc_ = qk(kc, KV, k_gain)
    vc_ = vc.reshape(B, Lc, KV, HEAD_DIM)
    k_all = jnp.concatenate([kc_, k_], axis=1)
    v_all = jnp.concatenate([vc_, v_], axis=1)
    nb = L // Q_BLOCK
    qb = jnp.moveaxis(q_.reshape(B, nb, Q_BLOCK, KV, G, HEAD_DIM), 1, 0)
    out = lax.map(lambda qi: _attend(qi, k_all, v_all), qb)
    out = jnp.moveaxis(out, 0, 1).reshape(B, L, GQA_HEADS * HEAD_DIM)
    if not ctx_out:
        return out, None
    qc_ = qk(qc, GQA_HEADS, q_gain).reshape(B, Lc, KV, G, HEAD_DIM)
    return out, _attend(qc_, kc_, vc_).reshape(B, Lc, GQA_HEADS * HEAD_DIM)


def _window_gqa(q, k, v, qc, kc, vc, sink, cos, sin, ctx_out):
    B, L, _ = q.shape
    Lc = qc.shape[1]
    KV, G = SWA_KV_HEADS, SWA_HEADS // SWA_KV_HEADS
    nb = L // Q_BLOCK
    sink = sink.reshape(KV, G)
    q_ = _rope(q.reshape(B, L, SWA_HEADS, HEAD_DIM), cos, sin).reshape(B, nb, Q_BLOCK, KV, G, HEAD_DIM)
    k_ = _rope(k.reshape(B, L, KV, HEAD_DIM), cos, sin)
    v_ = v.reshape(B, L, KV, HEAD_DIM)
    kc_ = kc.reshape(B, Lc, KV, HEAD_DIM)
    vc_ = vc.reshape(B, Lc, KV, HEAD_DIM)

    def band(t):
        tp = jnp.pad(t, ((0, 0), (Q_BLOCK, Q_BLOCK), (0, 0), (0, 0))).reshape(B, nb + 2, Q_BLOCK, KV, HEAD_DIM)
        return jnp.concatenate([tp[:, :-2], tp[:, 1:-1], tp[:, 2:]], axis=2)

    blk = jnp.arange(nb)[:, None, None]
    t_pos = blk * Q_BLOCK + jnp.arange(Q_BLOCK)[None, :, None]
    s_pos = (blk - 1) * Q_BLOCK + jnp.arange(3 * Q_BLOCK)[None, None, :]
    valid = (jnp.abs(t_pos - s_pos) <= WINDOW) & (s_pos >= 0) & (s_pos < L)
    mask = jnp.concatenate([jnp.ones((nb, Q_BLOCK, Lc), dtype=bool), valid], axis=-1)

    def block(args):
        qi, ki, vi, mi = args
        return _attend(qi, jnp.concatenate([kc_, ki], axis=1), jnp.concatenate([vc_, vi], axis=1), mi, sink)

    xs = (jnp.moveaxis(q_, 1, 0), jnp.moveaxis(band(k_), 1, 0), jnp.moveaxis(band(v_), 1, 0), mask)
    out = jnp.moveaxis(lax.map(block, xs), 0, 1).reshape(B, L, SWA_HEADS * HEAD_DIM)
    if not ctx_out:
        return out, None
    qc_ = qc.reshape(B, Lc, KV, G, HEAD_DIM)
    return out, _attend(qc_, kc_, vc_, None, sink).reshape(B, Lc, SWA_HEADS * HEAD_DIM)


def _hgrn2(q, zf, zb, i, g, qc, zfc, zbc, ic, gc, lb, ctx_out):
    H, E = HGRN_HEADS, HGRN_EXPAND
    lb = lb.reshape(H, E)

    def heads(t, dh):
        return t.reshape(t.shape[0], t.shape[1], H, dh).transpose(0, 2, 1, 3)

    def gate(z):
        z = z.reshape(z.shape[0], z.shape[1], H, E).astype(F32)
        f = lb + (1.0 - lb) * jax.nn.sigmoid(z)
        log_f = jnp.log(jnp.maximum(f, TINY))
        key = (1.0 - lb) * jax.nn.sigmoid(-z)
        return key.transpose(0, 2, 1, 3), log_f.transpose(0, 2, 1, 3)

    q_, v_ = heads(jax.nn.silu(q), E), heads(i, HEAD_DIM)
    qc_, vc_ = heads(jax.nn.silu(qc), E), heads(ic, HEAD_DIM)
    kf, laf = gate(zf)
    kb, lab = gate(zb)
    kfc, lafc = gate(zfc)
    kbc, labc = gate(zbc)
    o, oc = _bidir_scan(q_, [kf, kb], v_, [laf, lab], qc_, [kfc, kbc], vc_, [lafc, labc], HGRN_CHUNK, ctx_out)

    def finish(o, gt):
        return _rms(o).transpose(0, 2, 1, 3).reshape(gt.shape).astype(gt.dtype) * jax.nn.silu(gt)

    return finish(o, g), (finish(oc, gc) if ctx_out else None)


def _modulation(cvec, w_ada, b_ada):
    return jnp.split(jax.nn.silu(cvec) @ w_ada + b_ada, 6, axis=-1)


def _modulate(x, shift, scale):
    return _ln(x) * (1.0 + scale) + shift


def _post(x, y, gate, gain, bias):
    return _ln(ALPHA * x + gate * y) * gain + bias


def _sq_relu_mlp(h, w_up, w_down):
    return jnp.square(jax.nn.relu(h @ w_up)) @ w_down


def setup_inputs(seed: int = 0) -> dict:
    key = jax.random.key(seed)
    ks = jax.random.split(key, 20)

    def nrm(k, shape, s=1.0):
        return jax.random.normal(k, shape, F32) * s

    base_logit = jnp.log(2.0 ** (5.0 + jnp.arange(RET_HEADS, dtype=F32)) - 1.0)
    return {
        "x": nrm(ks[0], (BATCH, SEQ, D_MODEL)),
        "c": nrm(ks[1], (BATCH, D_MODEL)),
        "ctx": nrm(ks[2], (BATCH, CTX_LEN, D_MODEL)),
        "c_ctx": nrm(ks[3], (D_MODEL,)),
        "w_ada": nrm(ks[4], (DEPTH, D_MODEL, 6 * D_MODEL), 0.5 * D_MODEL ** -0.5),
        "b_ada": nrm(ks[5], (DEPTH, 6 * D_MODEL), 0.02),
        "w_in": nrm(ks[6], (DEPTH, D_MODEL, N_IN), D_MODEL ** -0.5),
        "ret_decay_logit": base_logit + nrm(ks[7], (DEPTH, 2, RET_HEADS), 0.1),
        "gqa_q_gain": 1.0 + nrm(ks[8], (DEPTH, HEAD_DIM), 0.02),
        "gqa_k_gain": 1.0 + nrm(ks[9], (DEPTH, HEAD_DIM), 0.02),
        "swa_sink": nrm(ks[10], (DEPTH, SWA_HEADS)),
        "hgrn_lb": nrm(ks[11], (DEPTH, HGRN_FDIM)),
        "w_out": nrm(ks[12], (DEPTH, D_MODEL, D_MODEL), BETA * D_MODEL ** -0.5),
        "ln1_g": 1.0 + nrm(ks[13], (DEPTH, D_MODEL), 0.02),
        "ln1_b": nrm(ks[14], (DEPTH, D_MODEL), 0.02),
        "w_up": nrm(ks[15], (DEPTH, D_MODEL, D_FF), D_MODEL ** -0.5),
        "w_down": nrm(ks[16], (DEPTH, D_FF, D_MODEL), BETA * D_FF ** -0.5),
        "ln2_g": 1.0 + nrm(ks[17], (DEPTH, D_MODEL), 0.02),
        "ln2_b": nrm(ks[18], (DEPTH, D_MODEL), 0.02),
    }


def reference(x, c, ctx, c_ctx, w_ada, b_ada, w_in, ret_decay_logit, gqa_q_gain, gqa_k_gain,
              swa_sink, hgrn_lb, w_out, ln1_g, ln1_b, w_up, w_down, ln2_g, ln2_b):
    L = x.shape[1]
    cos, sin = _rope_tables(L)
    p_lb = jax.nn.softmax(hgrn_lb.astype(F32), axis=0)
    lower_bounds = jnp.cumsum(p_lb, axis=0) - p_lb[0]
    split_at = np.cumsum(IN_SIZES)[:-1].tolist()
    xc = ctx
    for l in range(DEPTH):
        ctx_out = l < DEPTH - 1
        m_lat = [m[:, None, :] for m in _modulation(c, w_ada[l], b_ada[l])]
        m_ctx = _modulation(c_ctx, w_ada[l], b_ada[l])
        h = _modulate(x, m_lat[0], m_lat[1])
        hc = _modulate(xc, m_ctx[0], m_ctx[1])
        p = jnp.split(h @ w_in[l], split_at, axis=-1)
        pc = jnp.split(hc @ w_in[l], split_at, axis=-1)
        y_ret, yc_ret = _retention(*p[0:4], *pc[0:4], ret_decay_logit[l], cos, sin, ctx_out)
        y_glb, yc_glb = _global_gqa(*p[4:7], *pc[4:7], gqa_q_gain[l], gqa_k_gain[l], cos, sin, ctx_out)
        y_win, yc_win = _window_gqa(*p[7:10], *pc[7:10], swa_sink[l], cos, sin, ctx_out)
        y_hg, yc_hg = _hgrn2(*p[10:15], *pc[10:15], lower_bounds[l], ctx_out)
        y = jnp.concatenate([y_ret, y_glb, y_win, y_hg], axis=-1) @ w_out[l]
        x = _post(x, y, m_lat[2], ln1_g[l], ln1_b[l])
        x = _post(x, _sq_relu_mlp(_modulate(x, m_lat[3], m_lat[4]), w_up[l], w_down[l]), m_lat[5], ln2_g[l], ln2_b[l])
        if ctx_out:
            yc = jnp.concatenate([yc_ret, yc_glb, yc_win, yc_hg], axis=-1) @ w_out[l]
            xc = _post(xc, yc, m_ctx[2], ln1_g[l], ln1_b[l])
            xc = _post(xc, _sq_relu_mlp(_modulate(xc, m_ctx[3], m_ctx[4]), w_up[l], w_down[l]), m_ctx[5], ln2_g[l], ln2_b[l])
    return x
```

```python
from contextlib import ExitStack
import numpy as np
import concourse.bass as bass
import concourse.mybir as mybir
from concourse.bass_utils import run_bass_kernel_spmd

F32 = mybir.dt.float32
BF16 = mybir.dt.bfloat16
AF = mybir.ActivationFunctionType
ALU = mybir.AluOpType
AX = mybir.AxisListType

D = 1024
SEQ = 2048
LC = 256
NT = 18
TT = NT * 128
NIN = 3328
DFF = 4096
ALPHA = 4.0 ** 0.25
EPS = 1e-6
TINY = 1e-30
SM_SHIFT = 12.0


class T:
    __slots__ = ("ap", "name", "w", "r", "dsem", "dcnt", "psum", "is_dram")

    def __init__(self, ap, name, psum=False, is_dram=False):
        self.is_dram = is_dram
        self.ap = ap
        self.name = name
        self.w = None
        self.r = []
        self.dsem = None
        self.dcnt = 0
        self.psum = psum

    def __getitem__(self, k):
        return self.ap[k]


class KB:
    def __init__(self, nc):
        self.nc = nc
        self.es = ExitStack()
        self.eng = {"pe": nc.tensor, "act": nc.scalar, "dve": nc.vector, "pool": nc.gpsimd, "sp": nc.sync}
        self.sem = {}
        self.cnt = {}
        self.known = {}
        for e in self.eng:
            self.sem[e] = self.es.enter_context(nc.semaphore("s_" + e))
            self.cnt[e] = 0
            self.known[e] = {}
        self.out_events = []
        self.n_ins = 0
        self.n_wait = 0
        self.sb_bytes = 0
        self.dma_owners = []

    SB_BASE = 16512
    SB_END = 229376 - 1024

    def sbuf(self, name, shape, dtype=F32):
        n = 1
        for s in shape[1:]:
            n *= s
        nbytes = n * (4 if dtype == F32 else 2)
        nbytes = (nbytes + 31) // 32 * 32
        if not hasattr(self, "top"):
            arena = self.nc.alloc_sbuf_tensor("arena", [128, (self.SB_END - self.SB_BASE) // 4], F32)
            self.SB_BASE = self.nc.lookup_mloc(arena).addr
            self.SB_END = self.SB_BASE + (self.SB_END - 16512)
            self.top = self.SB_BASE
            self.nalloc = 0
        off = self.top
        self.top += nbytes
        assert self.top <= self.SB_END, ("SBUF overflow", name, self.top)
        self.nalloc += 1
        t = self.nc.alloc_sbuf_tensor_at("%s_%d" % (name, self.nalloc), list(shape), dtype, offset=off)
        return T(t, name)

    def barrier(self):
        evs = [(o.dsem, o.dcnt) for o in self.dma_owners if o.dcnt > 0]
        evs += [(self.sem[e], self.cnt[e]) for e in ("pe", "act", "dve", "pool") if self.cnt[e] > 0]
        self._wait("sp", evs)
        ins = self.nc.sync.nop()
        self.cnt["sp"] += 1
        ins.then_inc(self.sem["sp"], 1)
        self.n_ins += 1
        for e in ("pe", "act", "dve", "pool"):
            self._wait(e, [(self.sem["sp"], self.cnt["sp"])])

    def psum_bank(self, name, dtype=F32):
        n = 512 if dtype == F32 else 1024
        t = self.es.enter_context(self.nc.psum_tensor(name, [128, n], dtype))
        return T(t, name, psum=True)

    def dram(self, name, shape, dtype, kind):
        t = self.nc.dram_tensor(name, list(shape), dtype, kind=kind)
        return T(t.ap(), name, is_dram=True)

    def tok(self, name, like=None, is_dram=False):
        return T(like.ap if like is not None else None, name, psum=(like.psum if like is not None else False),
                 is_dram=(like.is_dram if like is not None else is_dram))

    def _wait(self, e, deps, skip_self=False):
        best = {}
        for (s, v) in deps:
            kk = id(s)
            if kk not in best or best[kk][1] < v:
                best[kk] = (s, v)
        kn = self.known[e]
        for kk, (s, v) in best.items():
            if skip_self and s is self.sem[e]:
                continue
            if kn.get(kk, 0) >= v:
                continue
            self.eng[e].wait_ge(s, v)
            self.n_wait += 1
            kn[kk] = v

    def _deps(self, reads, writes):
        deps = []
        for t in reads:
            if t.w is not None:
                deps.append(t.w)
        for t in writes:
            if t.w is not None:
                deps.append(t.w)
            deps.extend(t.r)
        return deps

    def _mark(self, ev, reads, writes):
        for t in reads:
            if t.psum:
                t.w = ev
                t.r = []
            else:
                t.r.append(ev)
                if len(t.r) > 16:
                    best = {}
                    for (s, v) in t.r:
                        kk = id(s)
                        if kk not in best or best[kk][1] < v:
                            best[kk] = (s, v)
                    t.r = list(best.values())
        for t in writes:
            t.w = ev
            t.r = []

    def op(self, e, fn, reads=(), writes=()):
        self._wait(e, self._deps(reads, writes), skip_self=(e == "pe"))
        ins = fn(self.eng[e])
        self.cnt[e] += 1
        ins.then_inc(self.sem[e], 1)
        self.n_ins += 1
        self._mark((self.sem[e], self.cnt[e]), reads, writes)
        return ins

    def mm(self, out_t, groups, reads, start=True, stop=True, extra_writes=(), stop_last_only=False):
        writes = [out_t] + list(extra_writes)
        self._wait("pe", self._deps(reads, writes), skip_self=True)
        ins = None
        ng = len(groups)
        for gi_, (out_ap, pairs) in enumerate(groups):
            n = len(pairs)
            for i, (l, r) in enumerate(pairs):
                st_ = stop and i == n - 1 and (not stop_last_only or gi_ == ng - 1)
                ins = self.nc.tensor.matmul(out_ap, l, r, start=(start and i == 0), stop=st_)
                self.n_ins += 1
        self.cnt["pe"] += 1
        ins.then_inc(self.sem["pe"], 1)
        self._mark((self.sem["pe"], self.cnt["pe"]), reads, writes)

    def transposes(self, out_t, items, ident_ap, reads):
        writes = [out_t]
        self._wait("pe", self._deps(reads, writes), skip_self=True)
        ins = None
        for (o, i) in items:
            ins = self.nc.tensor.transpose(o, i, ident_ap)
            self.n_ins += 1
        self.cnt["pe"] += 1
        ins.then_inc(self.sem["pe"], 1)
        self._mark((self.sem["pe"], self.cnt["pe"]), reads, writes)

    def dma(self, q, out_t, out_ap, in_t, in_ap, is_output=False, owner=None):
        if owner is None:
            owner = out_t
            if out_t.is_dram and not in_t.is_dram:
                owner = in_t
        if owner.dsem is None:
            owner.dsem = self.es.enter_context(self.nc.semaphore("d_" + owner.name))
            self.dma_owners.append(owner)
        self._wait(q, self._deps([in_t], [out_t]))
        ins = self.eng[q].dma_start(out=out_ap, in_=in_ap)
        owner.dcnt += 16
        ins.then_inc(owner.dsem, 16)
        self.n_ins += 1
        ev = (owner.dsem, owner.dcnt)
        self._mark(ev, [in_t], [out_t])
        if is_output:
            self.out_events.append(ev)
        return ins

    def finish(self):
        evs = list(self.out_events)
        evs += [(o.dsem, o.dcnt) for o in self.dma_owners]
        evs += [(self.sem[e], self.cnt[e]) for e in ("pe", "act", "dve", "pool") if self.cnt[e] > 0]
        self._wait("sp", evs)


def bc(ap, shape):
    return ap.broadcast_to(list(shape))


CST = {}
_off = 0
for _n, _w in (("ident", 128), ("lincl", 128), ("ustrict", 128), ("uincl", 128), ("lstrict", 128),
               ("chunkind", 2), ("maskP", 128), ("maskN", 128), ("pos", 128), ("neg", 128),
               ("indge", 128), ("indle", 128), ("tp1", 128), ("t128m", 128), ("pcol", 2),
               ("c64", 16 * 64), ("s32", 16 * 32), ("mhalf", 8),
               ("l32incl", 128), ("u32incl", 128), ("u32strict", 128), ("l32strict", 128), ("m2f", 128), ("m2b", 128)):
    CST[_n] = (_off, _w)
    _off += _w
NCST = _off


def make_consts():
    c = np.zeros((128, NCST), np.float32)
    p = np.arange(128)[:, None]
    t = np.arange(128)[None, :]

    def put(n, a):
        o, w = CST[n]
        c[:, o:o + w] = a

    same = (p // 64) == (t // 64)
    put("ident", (p == t))
    put("lincl", same & (p <= t))
    put("ustrict", same & (p > t))
    put("uincl", same & (p >= t))
    put("lstrict", same & (p < t))
    same32 = (p // 32) == (t // 32)
    put("l32incl", same32 & (p <= t))
    put("u32incl", same32 & (p >= t))
    put("u32strict", same32 & (p > t))
    put("l32strict", same32 & (p < t))
    put("m2f", same & ((p % 64) < 32) & ((t % 64) >= 32))
    put("m2b", same & ((p % 64) >= 32) & ((t % 64) < 32))
    put("chunkind", np.stack([(np.arange(128) < 64), (np.arange(128) >= 64)], axis=1))
    put("maskP", (p >= t))
    put("maskN", (t >= p))
    put("pos", np.maximum(t - p, 0))
    put("neg", np.maximum(p - t, 0))
    put("indge", (t - p >= 0))
    put("indle", (t - p <= 0))
    put("tp1", np.broadcast_to(t + 1, (128, 128)))
    put("t128m", np.broadcast_to(128 - t, (128, 128)))
    put("pcol", np.stack([127 - np.arange(128), np.arange(128)], axis=1))
    put("mhalf", np.full((128, 8), -0.5))
    inv = (np.float32(10000.0) ** (-np.arange(16, dtype=np.float32) / np.float32(16))).astype(np.float32)
    tok = (np.arange(16)[None, :] * 128 + np.arange(128)[:, None]).astype(np.int64)
    row = (tok // 64).astype(np.float32)
    col = (tok % 64).astype(np.float32)
    ang = np.stack([row[..., None] * inv, col[..., None] * inv], axis=2).astype(np.float32)
    cos = np.cos(ang).astype(np.float32)
    sin = np.sin(ang).astype(np.float32)
    c64 = np.broadcast_to(cos[:, :, :, None, :], (128, 16, 2, 2, 16)).reshape(128, 16 * 64)
    put("c64", c64)
    put("s32", sin.reshape(128, 16 * 32))
    return c


def build(NB=4, LAYERS=2, dbg=False, stages="SARGWHCD"):
    nc = bass.Bass("TRN2", target_bir_lowering=False)
    k = KB(nc)
    IN, INT, OUT = "ExternalInput", "Internal", "ExternalOutput"

    x_d = k.dram("x", [NB, SEQ, D], F32, IN)
    ctx_d = k.dram("ctx", [NB, LC, D], F32, IN)
    cT_d = k.dram("cT", [128, 8 * 5], F32, IN)
    cst_d = k.dram("cst", [128, NCST], F32, IN)
    wada_d = k.dram("wada", [2, 128, 8 * 6144], F32, IN)
    bada_d = k.dram("bada", [2, 1, 6144], F32, IN)
    win_d = k.dram("win", [2, 128, 8 * NIN], F32, IN)
    wout_d = k.dram("wout", [2, 128, 8 * D], F32, IN)
    wup_d = k.dram("wup", [2, 8, 128, 8 * 512], F32, IN)
    wdn_d = k.dram("wdn", [2, 8, 128, 4 * D], F32, IN)
    rdl_d = k.dram("rdl", [2, 1, 8], F32, IN)
    gain_d = k.dram("gain", [2, 1, 384], F32, IN)
    sink_d = k.dram("sink", [2, 1, 4], F32, IN)
    hlb_d = k.dram("hlb", [2, 1, 256], F32, IN)
    lnp_d = k.dram("lnp", [2, 4, 1, D], F32, IN)
    out_d = k.dram("out", [NB, SEQ, D], F32, OUT)
    dbg_d = k.dram("dbg", [128, 8 * TT], F32, OUT) if dbg else None
    dbg2_d = k.dram("dbg2", [128, 8192], F32, OUT) if dbg else None
    dd = {"off": 0, "names": []}

    def dump(name, t, ap, n):
        if not dbg:
            return
        k.dma("sp", dbg2_d, dbg2_d[:, dd["off"]:dd["off"] + n], t, ap, is_output=True)
        dd["names"].append((name, dd["off"], n))
        dd["off"] += n
    k.dd = dd

    wada_b = [k.dram("wada_b%d" % l, [128, 8 * 6144], BF16, INT) for l in range(2)]
    win_b = [k.dram("win_b%d" % l, [128, 8 * NIN], BF16, INT) for l in range(2)]
    wout_b = [k.dram("wout_b%d" % l, [128, 8 * D], BF16, INT) for l in range(2)]
    wup_b = [k.dram("wup_b%d" % l, [8, 128, 8 * 512], BF16, INT) for l in range(2)]
    wdn_b = [k.dram("wdn_b%d" % l, [8, 128, 4 * D], BF16, INT) for l in range(2)]
    modrows = [k.dram("modrows%d" % l, [5, 6144], F32, INT) for l in range(2)]
    xs_d = k.dram("xs", [NB, TT, D], F32, INT)
    xm_d = k.dram("xm", [NB, TT, D], F32, INT)
    xs_tok = [[k.tok("xs%d_%d" % (j, t), is_dram=True) for t in range(NT)] for j in range(NB)]
    xm_tok = [[k.tok("xm%d_%d" % (j, t), is_dram=True) for t in range(NT)] for j in range(NB)]

    cst = k.sbuf("cst_sb", [128, NCST])
    ident = k.sbuf("ident_bf", [128, 128], BF16)
    HT_OFF = k.top
    hT = k.sbuf("hT", [128, 8, TT], BF16)
    ycT = k.sbuf("ycT", [128, 8, TT], BF16)
    hT_tok = [k.tok("hT%d" % t) for t in range(NT)]
    ycT_tok = [[k.tok("ycT%d_%d" % (m, t)) for t in range(NT)] for m in range(4)]
    gaintab = k.sbuf("gaintab", [128, 384])
    rdl = k.sbuf("rdl_sb", [128, 8])
    lgt = k.sbuf("lgt", [128, 8])
    lgcol = k.sbuf("lgcol", [128, 4])
    a128c = k.sbuf("a128c", [128, 4])
    dk = k.sbuf("dk", [128, 8])
    dq = k.sbuf("dq", [128, 4, 128])
    mT = k.sbuf("mT", [128, 4, 128])
    sinkE = k.sbuf("sinkE", [128, 4])
    hA = k.sbuf("hA", [128, 256])
    hB = k.sbuf("hB", [128, 256])
    small = k.sbuf("small", [128, 64])
    small2 = k.sbuf("small2", [128, 64])

    P = [k.psum_bank("ps%d" % i) for i in range(6)]
    PT = [k.psum_bank("pst%d" % i, BF16) for i in range(2)]
    prot = [0]
    ptrot = [0]

    def nb():
        prot[0] = (prot[0] + 1) % 6
        return P[prot[0]]

    def nbt():
        ptrot[0] = (ptrot[0] + 1) % 2
        return PT[ptrot[0]]

    def C(name, lo=0, hi=None):
        o, w = CST[name]
        return cst[:, o + lo:o + (w if hi is None else hi)]

    k.dma("sp", cst, cst[:], cst_d, cst_d[:])
    k.op("dve", lambda e: e.tensor_copy(ident[:], C("ident")), reads=[cst], writes=[ident])

    def cast_dram(dst, dst_ap2d, src, src_ap2d, n):
        step = 8192
        for c0 in range(0, n, step):
            c1 = min(n, c0 + step)
            k.dma("pool", dst, dst_ap2d[:, c0:c1], src, src_ap2d[:, c0:c1])

    for l in range(LAYERS):
        cast_dram(wada_b[l], wada_b[l][:, :], wada_d, wada_d[l], 8 * 6144)
        cast_dram(win_b[l], win_b[l][:, :], win_d, win_d[l], 8 * NIN)
        cast_dram(wout_b[l], wout_b[l][:, :], wout_d, wout_d[l], 8 * D)
        for c in range(8):
            cast_dram(wup_b[l], wup_b[l][c], wup_d, wup_d[l, c], 8 * 512)
        for c in range(8):
            cast_dram(wdn_b[l], wdn_b[l][c], wdn_d, wdn_d[l, c], 4 * D)

    def layer_norm_stats(xt, xt_tok, st, st_tok):
        k.op("dve", lambda e: e.bn_stats(st[:, 8:14], xt[:, 0:512]), reads=[xt_tok], writes=[st_tok])
        k.op("dve", lambda e: e.bn_stats(st[:, 14:20], xt[:, 512:1024]), reads=[xt_tok], writes=[st_tok])
        k.op("dve", lambda e: e.bn_aggr(st[:, 0:2], st[:, 8:20]), reads=[st_tok], writes=[st_tok])
        k.op("dve", lambda e: e.tensor_scalar(st[:, 3:4], st[:, 1:2], EPS, None, ALU.add), reads=[st_tok], writes=[st_tok])
        k.op("pool", lambda e: e.tensor_tensor(st[:, 1:2], st[:, 3:4], C("mhalf", 0, 1), ALU.pow),
             reads=[st_tok, cst], writes=[st_tok])
        k.op("dve", lambda e: e.scalar_tensor_tensor(st[:, 2:3], st[:, 0:1], -1.0, st[:, 1:2], ALU.mult, ALU.mult),
             reads=[st_tok], writes=[st_tok])

    cT = k.sbuf("cT_sb", [128, 40])
    cTb = k.sbuf("cT_bf", [128, 40], BF16)
    M0 = k.top
    slots = [k.sbuf("tabslot%d" % i, [128, D]) for i in range(5)]
    xbuf = [k.sbuf("xbuf%d" % i, [128, D]) for i in range(2)]
    wkA = [k.sbuf("wkA%d" % i, [128, D]) for i in range(2)]
    wkB = [k.sbuf("wkB%d" % i, [128, D], BF16) for i in range(2)]
    stt = [k.sbuf("stt%d" % i, [128, 24]) for i in range(2)]
    fr_x = [k.sbuf("fr_x%d" % i, [128, 512]) for i in range(2)]
    modsb, badasb = fr_x[0], fr_x[1]
    M1 = k.top
    wout_sb = k.sbuf("wout_sb", [128, 8, D], BF16)
    xbuf3 = xbuf + [k.sbuf("xbuf2", [128, D])]
    wkA3 = wkA + [k.sbuf("wkA2", [128, D])]
    wkB3 = wkB + [k.sbuf("wkB2", [128, D], BF16)]
    stt3 = stt + [k.sbuf("stt2", [128, 24])]
    k.top = M1
    hidT = k.sbuf("hidT", [128, 32, 512], BF16)
    wup_sb = [k.sbuf("wup_sb%d" % i, [128, 8, 512], BF16) for i in range(2)]
    wdn_sb = [k.sbuf("wdn_sb%d" % i, [128, 4, D], BF16) for i in range(2)]
    print("SBUF set X top", k.top)
    tabs = {}
    k.dma("sp", cT, cT[:], cT_d, cT_d[:])
    k.op("act", lambda e: e.activation(small[:, 0:40], cT[:], AF.Exp, scale=-1.0), reads=[cT], writes=[small])
    k.op("dve", lambda e: e.tensor_scalar(small[:, 0:40], small[:, 0:40], 1.0, None, ALU.add), reads=[small], writes=[small])
    k.op("dve", lambda e: e.reciprocal(small[:, 0:40], small[:, 0:40]), reads=[small], writes=[small])
    k.op("dve", lambda e: e.tensor_tensor(cTb[:], small[:, 0:40], cT[:], ALU.mult), reads=[small, cT], writes=[cTb])

    rot = {"x": 0, "a": 0, "b": 0, "s": 0}

    def nxt(lst, key):
        rot[key] = (rot[key] + 1) % len(lst)
        return lst[rot[key]]

    def modulate_to_hT(xt, sc, sh, tt):
        st = nxt(stt, "s")
        layer_norm_stats(xt, xt, st, st)
        wa = nxt(wkA, "a")
        wb = nxt(wkB, "b")
        k.op("act", lambda e: e.activation(wa[:], xt[:], AF.Identity, bias=st[:, 2:3], scale=st[:, 1:2]),
             reads=[xt, st], writes=[wa])
        k.op("pool", lambda e: e.tensor_tensor(wa[:], wa[:], sc[:], ALU.mult), reads=[wa, sc], writes=[wa])
        k.op("dve", lambda e: e.tensor_tensor(wb[:], wa[:], sh[:], ALU.add), reads=[wa, sh], writes=[wb])
        pt = nbt()
        k.transposes(pt, [(pt[:, kc * 128:(kc + 1) * 128], wb[:, kc * 128:(kc + 1) * 128]) for kc in range(8)],
                     ident[:], reads=[wb, ident])
        k.op("act", lambda e: e.activation(hT[:, :, tt * 128:(tt + 1) * 128],
                                           pt[:, :].rearrange("p (c t) -> p c t", c=8), AF.Copy),
             reads=[pt], writes=[hT_tok[tt]])

    def ln_stats_gen(xt, st):
        k.op("dve", lambda e: e.bn_stats(st[:, 8:14], xt[:, 0:512]), reads=[xt], writes=[st])
        k.op("dve", lambda e: e.bn_stats(st[:, 14:20], xt[:, 512:1024]), reads=[xt], writes=[st])
        k.op("dve", lambda e: e.bn_aggr(st[:, 0:2], st[:, 8:20]), reads=[st], writes=[st])
        k.op("dve", lambda e: e.tensor_scalar(st[:, 3:4], st[:, 1:2], EPS, None, ALU.add), reads=[st], writes=[st])
        yield
        k.op("pool", lambda e: e.tensor_tensor(st[:, 1:2], st[:, 3:4], C("mhalf", 0, 1), ALU.pow), reads=[st, cst], writes=[st])
        yield
        k.op("dve", lambda e: e.scalar_tensor_tensor(st[:, 2:3], st[:, 0:1], -1.0, st[:, 1:2], ALU.mult, ALU.mult), reads=[st], writes=[st])

    def modulate_gen(xt, sc, sh, tt, par, pt, mul_eng="pool"):
        st, wa, wb = stt3[par], wkA3[par], wkB3[par]
        yield from ln_stats_gen(xt, st)
        yield
        k.op("act", lambda e: e.activation(wa[:], xt[:], AF.Identity, bias=st[:, 2:3], scale=st[:, 1:2]), reads=[xt, st], writes=[wa])
        yield
        k.op(mul_eng, lambda e: e.tensor_tensor(wa[:], wa[:], sc[:], ALU.mult), reads=[wa, sc], writes=[wa])
        yield
        k.op("dve", lambda e: e.tensor_tensor(wb[:], wa[:], sh[:], ALU.add), reads=[wa, sh], writes=[wb])
        yield
        k.transposes(pt, [(pt[:, kc * 128:(kc + 1) * 128], wb[:, kc * 128:(kc + 1) * 128]) for kc in range(8)], ident[:], reads=[wb, ident])
        yield
        k.op("act", lambda e: e.activation(hT[:, :, tt * 128:(tt + 1) * 128], pt[:, :].rearrange("p (c t) -> p c t", c=8), AF.Copy),
             reads=[pt], writes=[hT_tok[tt]])

    def load_tab(name, slot, l, row, col0):
        t = slots[slot]
        tabs[name] = t
        k.dma("sp", t, t[:], modrows[l], modrows[l][row:row + 1, col0:col0 + D].partition_broadcast(128))

    def load_ln(name, slot, l, i):
        t = slots[slot]
        tabs[name] = t
        k.dma("sp", t, t[:], lnp_d, lnp_d[l, i].partition_broadcast(128))

    wada_sb = wup_sb

    def layer_setup(l):
        for ct in range(12):
            wsb = wada_sb[ct % 2]
            k.dma("sp", badasb, badasb[0:5, :], bada_d, bada_d[l][:, ct * 512:(ct + 1) * 512].partition_broadcast(5))
            k.dma("sp", wsb, wsb[:], wada_b[l],
                  wada_b[l][:, :].rearrange("p (c n) -> p c n", c=8)[:, :, ct * 512:(ct + 1) * 512])
            pb = nb()
            k.mm(pb, [(pb[0:5, 0:512], [(cTb[:, kc * 5:(kc + 1) * 5], wsb[:, kc, :]) for kc in range(8)])],
                 reads=[cTb, wsb])
            k.op("dve", lambda e: e.tensor_tensor(modsb[0:5, :], pb[0:5, 0:512], badasb[0:5, :], ALU.add),
                 reads=[pb, badasb], writes=[modsb])
            if ct in (2, 3, 8, 9):
                k.op("dve", lambda e: e.tensor_scalar(modsb[0:5, :], modsb[0:5, :], 1.0, None, ALU.add), reads=[modsb], writes=[modsb])
            k.dma("sp", modrows[l], modrows[l][:, ct * 512:(ct + 1) * 512], modsb, modsb[0:5, :])
        k.dma("sp", gaintab, gaintab[:], gain_d, gain_d[l].partition_broadcast(128))
        k.dma("sp", rdl, rdl[:], rdl_d, rdl_d[l].partition_broadcast(128))
        k.op("act", lambda e: e.activation(small[:, 0:8], rdl[:], AF.Exp, scale=-1.0), reads=[rdl], writes=[small])
        k.op("dve", lambda e: e.tensor_scalar(small[:, 0:8], small[:, 0:8], 1.0, None, ALU.add), reads=[small], writes=[small])
        k.op("act", lambda e: e.activation(small[:, 8:16], small[:, 0:8], AF.Ln), reads=[small], writes=[small])
        k.op("dve", lambda e: e.tensor_scalar(lgt[:], small[:, 8:16], -1.0, None, ALU.mult), reads=[small], writes=[lgt])
        for d_ in range(2):
            for g in range(2):
                j = d_ * 2 + g
                k.op("dve", lambda e: e.tensor_copy(lgcol[0:64, j:j + 1], lgt[0:64, d_ * 4 + 2 * g:d_ * 4 + 2 * g + 1]),
                     reads=[lgt], writes=[lgcol])
                k.op("dve", lambda e: e.tensor_copy(lgcol[64:128, j:j + 1], lgt[64:128, d_ * 4 + 2 * g + 1:d_ * 4 + 2 * g + 2]),
                     reads=[lgt], writes=[lgcol])
        k.op("act", lambda e: e.activation(a128c[:], lgcol[:], AF.Exp, scale=128.0), reads=[lgcol], writes=[a128c])
        for d_ in range(2):
            k.op("dve", lambda e: e.tensor_scalar(small[:, 16 + 4 * d_:20 + 4 * d_], lgt[:, 4 * d_:4 * d_ + 4],
                                                  C("pcol", d_, d_ + 1), None, ALU.mult),
                 reads=[lgt, cst], writes=[small])
        k.op("act", lambda e: e.activation(small[:, 24:32], small[:, 16:24], AF.Exp), reads=[small], writes=[small])
        k.op("dve", lambda e: e.tensor_scalar(dk[:], small[:, 24:32], 0.125, None, ALU.mult), reads=[small], writes=[dk])
        for d_ in range(2):
            for g in range(2):
                j = d_ * 2 + g
                src = C("tp1") if d_ == 0 else C("t128m")
                k.op("act", lambda e: e.activation(dq[:, j, :], src, AF.Exp, scale=lgcol[:, j:j + 1]),
                     reads=[cst, lgcol], writes=[dq])
        for h in range(4):
            k.op("act", lambda e: e.activation(hA[:, 0:128], C("pos"), AF.Exp, scale=lgt[:, h:h + 1]),
                 reads=[cst, lgt], writes=[hA])
            k.op("dve", lambda e: e.tensor_tensor(hA[:, 0:128], hA[:, 0:128], C("indge"), ALU.mult), reads=[hA, cst], writes=[hA])
            k.op("act", lambda e: e.activation(hA[:, 128:256], C("neg"), AF.Exp, scale=lgt[:, 4 + h:5 + h]),
                 reads=[cst, lgt], writes=[hA])
            k.op("dve", lambda e: e.tensor_tensor(hA[:, 128:256], hA[:, 128:256], C("indle"), ALU.mult), reads=[hA, cst], writes=[hA])
            k.op("dve", lambda e: e.tensor_tensor(hA[:, 0:128], hA[:, 0:128], hA[:, 128:256], ALU.add), reads=[hA], writes=[hA])
            k.op("dve", lambda e: e.tensor_scalar(mT[:, h, :], hA[:, 0:128], 0.125, None, ALU.mult), reads=[hA], writes=[mT])
        k.dma("sp", small2, small2[:, 0:4], sink_d, sink_d[l].partition_broadcast(128))
        k.op("act", lambda e: e.activation(sinkE[:], small2[:, 0:4], AF.Exp, bias=-SM_SHIFT), reads=[small2], writes=[sinkE])
        if l == 0:
            k.op("dve", lambda e: e.memset(hA[:], 0.0), writes=[hA])
            k.op("dve", lambda e: e.memset(hB[:], 1.0), writes=[hB])
        else:
            k.dma("sp", hA, hA[:], hlb_d, hlb_d[1].partition_broadcast(128))
            k.dma("sp", hB, hB[:], hlb_d, hlb_d[0].partition_broadcast(128))
            k.op("dve", lambda e: e.tensor_tensor(hB[:], hB[:], hA[:], ALU.subtract), reads=[hA, hB], writes=[hB])
            k.op("act", lambda e: e.activation(hB[:], hB[:], AF.Exp), reads=[hB], writes=[hB])
            k.op("dve", lambda e: e.tensor_scalar(hB[:], hB[:], 1.0, None, ALU.add), reads=[hB], writes=[hB])
            k.op("dve", lambda e: e.reciprocal(hA[:], hB[:]), reads=[hB], writes=[hA])
            k.op("dve", lambda e: e.tensor_scalar(hB[:], hA[:], -1.0, 1.0, ALU.mult, ALU.add), reads=[hA], writes=[hB])

    qkT_tok = [k.tok("qkT%d" % t) for t in range(NT)]
    tm_tok = [k.tok("tm%d" % t) for t in range(NT)]

    def alloc_set(pfx, base, spec):
        k.top = base
        S = {}
        for (key, shape, dtype, cnt) in spec:
            if cnt == 0:
                S[key] = k.sbuf(pfx + key, shape, dtype)
            else:
                S[key] = [k.sbuf(pfx + key + str(i), shape, dtype) for i in range(cnt)]
        return S

    fspec = lambda names: [(n, [128, 512], F32, 0) for n in names]
    small_spec = [("b2", [128, 512], BF16, 2), ("b3", [128, 512], BF16, 2), ("yc", [128, 256], BF16, 2),
                  ("sfst", [128, 2, 64], F32, 0), ("sfbf", [128, 2, 64], BF16, 2)]
    tm_spec = [(n, [128, NT, 256], BF16, 0) for n in ("tmA", "tmB", "tmC", "tmD")]
    SET_RET = alloc_set("r_", M0, [("qkT", [128, 8, TT], BF16, 0)] + tm_spec + [("sball", [128, 18, 2, 64], BF16, 0),
                        ("win_sb", [128, 8, 1024], BF16, 0), ("f1", [128, 512], F32, 2)] + fspec(["f2", "f3", "f4", "f5"]) + small_spec)
    print("SBUF set RET top", k.top)
    _save_top = k.top
    k.top = k.nc.lookup_mloc(SET_RET["sball"].ap).addr
    SET_RET["f3b"] = k.sbuf("r_f3b", [128, 512])
    SET_RET["f4b"] = k.sbuf("r_f4b", [128, 512])
    assert k.top <= k.nc.lookup_mloc(SET_RET["win_sb"].ap).addr, "ret overlay overflow"
    k.top = k.nc.lookup_mloc(SET_RET["f4"].ap).addr
    SET_RET["sfbf2"] = [k.sbuf("r_sfbf_x%d" % i, [128, 2, 64], BF16) for i in range(2)]
    k.top = _save_top
    SET_ATT = alloc_set("a_", M0, [("qkT", [128, 2, TT], BF16, 0), ("kz", [128, 2, TT], BF16, 0), ("vaug2", [128, NT, 2, 2, 128], BF16, 0),
                        ("win_sb", [128, 8, 512], BF16, 0), ("fa", [128, 512], F32, 3), ("f2", [128, 512], F32, 3),
                        ("f3", [128, 512], F32, 3), ("f4", [128, 512], F32, 3), ("qb", [128, 512], BF16, 3),
                        ("pT", [128, 512], BF16, 8), ("rec", [128, 512], F32, 4), ("rec2", [128, 512], F32, 4),
                        ("sm", [128, 16], F32, 3), ("msk", [128, 6, 512], BF16, 0)])
    print("SBUF set ATT top", k.top)
    SET_HG = alloc_set("h_", M0, [("win_sb", [128, 8, 1280], BF16, 0), ("thz", [128, 512], F32, 2), ("sq", [128, 256], F32, 2),
                       ("tmp", [128, 256], F32, 2), ("key", [128, 512], F32, 2), ("lg", [128, 512], F32, 2), ("sqq", [128, 256], F32, 2),
                       ("fo", [128, 256], F32, 2), ("fe", [128, 256], F32, 20), ("b1", [128, 1024], BF16, 4), ("kh", [128, 256], BF16, 4),
                       ("vt", [128, 256], BF16, 4), ("sgt", [128, 256], BF16, 2), ("fm", [128, 16, 128], BF16, 2),
                       ("am", [128, 1024], BF16, 2), ("yc", [128, 256], BF16, 2), ("sfst", [128, 2, 64], F32, 0),
                       ("sfbf", [128, 2, 64], BF16, 3), ("sball", [128, 36, 2, 64], BF16, 0), ("aTsb", [128, 16], F32, 4),
                       ("sm", [128, 16], F32, 2)])
    print("SBUF set HG top", k.top)
    qkT = tmA = tmB = tmC = tmD = vaug = sball = win_sb = f1 = f2 = f3 = f4 = f5 = f6 = f7 = None
    b1 = b2 = b3 = yc = sfst = sfbf = aTsb = None

    def use_set(S):
        nonlocal qkT, tmA, tmB, tmC, tmD, vaug, sball, win_sb, f1, f2, f3, f4, f5, f6, f7, b1, b2, b3, yc, sfst, sfbf, aTsb
        qkT = S.get("qkT"); tmA = S.get("tmA"); tmB = S.get("tmB"); tmC = S.get("tmC"); tmD = S.get("tmD")
        vaug = S.get("vaug"); sball = S.get("sball"); win_sb = S.get("win_sb")
        f1 = S.get("f1"); f2 = S.get("f2"); f3 = S.get("f3"); f4 = S.get("f4"); f5 = S.get("f5"); f6 = S.get("f6"); f7 = S.get("f7")
        b1 = S.get("b1"); b2 = S.get("b2"); b3 = S.get("b3"); yc = S.get("yc")
        sfst = S.get("sfst"); sfbf = S.get("sfbf"); aTsb = S.get("aTsb")
    rr = {"f1": 0, "b1": 0, "b2": 0, "b3": 0, "yc": 0, "sf": 0}

    def rn(lst, key):
        rr[key] = (rr[key] + 1) % len(lst)
        return lst[rr[key]]

    MIX_COLS = {"ret": (0, 1024), "gqa": (1024, 512), "swa": (1536, 512), "hgrn": (2048, 1280)}

    def load_win(l, mixer):
        c0, n = MIX_COLS[mixer]
        k.dma("sp", win_sb, win_sb[:, :, 0:n], win_b[l],
              win_b[l][:, :].rearrange("p (c n) -> p c n", c=8)[:, :, c0:c0 + n])

    def project(tt, n):
        res = []
        for c0 in range(0, n, 512):
            w = min(512, n - c0)
            pb = nb()
            k.mm(pb, [(pb[:, 0:w], [(hT[:, kc, tt * 128:(tt + 1) * 128], win_sb[:, kc, c0:c0 + w]) for kc in range(8)])],
                 reads=[hT_tok[tt], win_sb])
            res.append((pb, w))
        return res

    def rope(dst_bf, src, nh, tt, tmp1, tmp2):
        n = nh * 64
        if tt < 2:
            k.op("pool", lambda e: e.tensor_copy(dst_bf[:, 0:n], src[:, 0:n]), reads=[src], writes=[dst_bf])
            return
        li = tt - 2
        c64 = C("c64", li * 64, (li + 1) * 64)
        s32 = C("s32", li * 32, (li + 1) * 32)
        xv = src[:, 0:n].rearrange("p (h a f e) -> p h a f e", h=nh, a=2, f=2, e=16)
        t1 = tmp1[:, 0:n]
        k.op("dve", lambda e: e.tensor_tensor(t1.rearrange("p (h c) -> p h c", h=nh),
                                              src[:, 0:n].rearrange("p (h c) -> p h c", h=nh),
                                              bc(c64.unsqueeze(1), [128, nh, 64]), ALU.mult),
             reads=[src, cst], writes=[tmp1])
        sv = bc(s32.rearrange("p (a e) -> p a e", a=2).unsqueeze(1), [128, nh, 2, 16])
        u = tmp2[:, 0:n // 2].rearrange("p (h a e) -> p h a e", h=nh, a=2, e=16)
        w_ = tmp2[:, n // 2:n].rearrange("p (h a e) -> p h a e", h=nh, a=2, e=16)
        k.op("pool", lambda e: e.tensor_tensor(u, xv[:, :, :, 1, :], sv, ALU.mult), reads=[src, cst], writes=[tmp2])
        k.op("dve", lambda e: e.tensor_tensor(w_, xv[:, :, :, 0, :], sv, ALU.mult), reads=[src, cst], writes=[tmp2])
        t1v = t1.rearrange("p (h a f e) -> p h a f e", h=nh, a=2, f=2, e=16)
        dv = dst_bf[:, 0:n].rearrange("p (h a f e) -> p h a f e", h=nh, a=2, f=2, e=16)
        k.op("dve", lambda e: e.tensor_tensor(dv[:, :, :, 0, :], t1v[:, :, :, 0, :], u, ALU.subtract),
             reads=[tmp1, tmp2], writes=[dst_bf])
        k.op("dve", lambda e: e.tensor_tensor(dv[:, :, :, 1, :], t1v[:, :, :, 1, :], w_, ALU.add),
             reads=[tmp1, tmp2], writes=[dst_bf])

    def to_fm(src_bf, nblk, tt, slot0, reads):
        pt = nbt()
        k.transposes(pt, [(pt[:, i * 128:(i + 1) * 128], src_bf[:, i * 128:(i + 1) * 128]) for i in range(nblk)],
                     ident[:], reads=reads + [ident])
        k.op("act", lambda e: e.activation(qkT[:, slot0:slot0 + nblk, tt * 128:(tt + 1) * 128],
                                           pt[:, 0:nblk * 128].rearrange("p (c t) -> p c t", c=nblk), AF.Copy),
             reads=[pt], writes=[qkT_tok[tt]])

    def emit_y(ycb, m, tt, pt=None):
        if pt is None:
            pt = nbt()
        k.transposes(pt, [(pt[:, i * 128:(i + 1) * 128], ycb[:, i * 128:(i + 1) * 128]) for i in range(2)],
                     ident[:], reads=[ycb, ident])
        k.op("act", lambda e: e.activation(ycT[:, 2 * m:2 * m + 2, tt * 128:(tt + 1) * 128],
                                           pt[:, 0:256].rearrange("p (c t) -> p c t", c=2), AF.Copy),
             reads=[pt], writes=[ycT_tok[m][tt]])

    def silu_to(dst_bf_ap, dst_tok, src_ap, src_tok, tmp, n):
        k.op("act", lambda e: e.activation(tmp[:, 0:n], src_ap, AF.Exp, scale=-1.0), reads=[src_tok], writes=[tmp])
        k.op("dve", lambda e: e.tensor_scalar(tmp[:, 0:n], tmp[:, 0:n], 1.0, None, ALU.add), reads=[tmp], writes=[tmp])
        k.op("dve", lambda e: e.reciprocal(tmp[:, 0:n], tmp[:, 0:n]), reads=[tmp], writes=[tmp])
        k.op("dve", lambda e: e.tensor_tensor(dst_bf_ap, tmp[:, 0:n], src_ap, ALU.mult), reads=[tmp, src_tok], writes=[dst_tok])

    def interleave(gens, depth=2):
        active = []
        it_ = iter(gens)
        while True:
            if len(active) < depth:
                g_ = next(it_, None)
                if g_ is not None:
                    active.append(g_)
            if not active:
                break
            for g_ in list(active):
                try:
                    next(g_)
                except StopIteration:
                    active.remove(g_)

    class Rot:
        def __init__(self, lst):
            self.lst = list(lst)
            self.i = -1

        def __call__(self):
            self.i = (self.i + 1) % len(self.lst)
            return self.lst[self.i]

    def attn_mixer(l, which):
        m = 1 if which == "gqa" else 2
        k.barrier()
        S = SET_ATT
        qkT_, vaug2, win = S["qkT"], S["vaug2"], S["win_sb"]
        k.op("pool", lambda e: e.memset(vaug2[:], 1.0), writes=[vaug2] + tm_tok)
        kz = S["kz"]
        k.op("pool", lambda e: e.memset(kz[:], 0.0), writes=[kz] + qkT_tok)
        c0, n = MIX_COLS[which]
        k.dma("sp", win, win[:, :, 0:n], win_b[l], win_b[l][:, :].rearrange("p (c n) -> p c n", c=8)[:, :, c0:c0 + n])
        msk = S["msk"]
        if which == "swa":
            k.op("pool", lambda e: e.memset(msk[:], 0.0), writes=[msk])
            for r in range(-1, 5):
                for b in range(4):
                    dlt = r - b
                    if dlt not in (-1, 0, 1):
                        continue
                    src = {-1: C("maskP"), 0: None, 1: C("maskN")}[dlt]
                    if src is None:
                        k.op("pool", lambda e: e.memset(msk[:, r + 1, b * 128:(b + 1) * 128], 1.0), writes=[msk])
                    else:
                        k.op("dve", lambda e: e.tensor_copy(msk[:, r + 1, b * 128:(b + 1) * 128], src), reads=[cst], writes=[msk])
        PB = Rot([P[0], P[1], P[2]])

        def prep(tt, par):
            pb = PB()
            k.mm(pb, [(pb[:, 0:512], [(hT[:, kc, tt * 128:(tt + 1) * 128], win[:, kc, 0:512]) for kc in range(8)])],
                 reads=[hT_tok[tt], win])
            fa, f2_, f3_, f4_, qb, sm = S["fa"][par], S["f2"][par], S["f3"][par], S["f4"][par], S["qb"][par], S["sm"][par]
            yield
            k.op("act", lambda e: e.activation(fa[:, 0:384], pb[:, 0:384], AF.Copy), reads=[pb], writes=[fa])
            for kv in range(2):
                k.op("act", lambda e: e.activation(vaug2[:, tt, kv, 0, 0:64], pb[:, 384 + kv * 64:448 + kv * 64], AF.Copy),
                     reads=[pb], writes=[tm_tok[tt]])
                k.op("dve", lambda e: e.tensor_copy(vaug2[:, tt, kv, 1, 64:128], pb[:, 384 + kv * 64:448 + kv * 64]),
                     reads=[pb], writes=[tm_tok[tt]])
            yield
            if which == "gqa":
                k.op("dve", lambda e: e.tensor_tensor(f2_[:, 0:384], fa[:, 0:384], fa[:, 0:384], ALU.mult), reads=[fa], writes=[f2_])
                k.op("dve", lambda e: e.tensor_reduce(sm[:, 0:6], f2_[:, 0:384].rearrange("p (h d) -> p h d", h=6), AX.X, ALU.add),
                     reads=[f2_], writes=[sm])
                k.op("dve", lambda e: e.tensor_scalar(sm[:, 0:6], sm[:, 0:6], 1.0 / 64, EPS, ALU.mult, ALU.add), reads=[sm], writes=[sm])
                yield
                k.op("pool", lambda e: e.tensor_tensor(sm[:, 8:14], sm[:, 0:6], C("mhalf", 0, 6), ALU.pow), reads=[sm, cst], writes=[sm])
                yield
                k.op("dve", lambda e: e.tensor_tensor(fa[:, 0:384].rearrange("p (h d) -> p h d", h=6),
                                                      fa[:, 0:384].rearrange("p (h d) -> p h d", h=6),
                                                      bc(sm[:, 8:14].unsqueeze(2), [128, 6, 64]), ALU.mult), reads=[fa, sm], writes=[fa])
                k.op("dve", lambda e: e.tensor_tensor(fa[:, 0:384], fa[:, 0:384], gaintab[:], ALU.mult), reads=[fa, gaintab], writes=[fa])
                yield
            if tt < 2:
                k.op("dve", lambda e: e.tensor_copy(qb[:, 0:384], fa[:, 0:384]), reads=[fa], writes=[qb])
            else:
                li = tt - 2
                c64 = C("c64", li * 64, (li + 1) * 64)
                s32 = C("s32", li * 32, (li + 1) * 32)
                nh = 6
                nn = 384
                xv = fa[:, 0:nn].rearrange("p (h a f e) -> p h a f e", h=nh, a=2, f=2, e=16)
                t1 = f3_[:, 0:nn]
                k.op("dve", lambda e: e.tensor_tensor(t1.rearrange("p (h c) -> p h c", h=nh), fa[:, 0:nn].rearrange("p (h c) -> p h c", h=nh),
                                                      bc(c64.unsqueeze(1), [128, nh, 64]), ALU.mult), reads=[fa, cst], writes=[f3_])
                sv = bc(s32.rearrange("p (a e) -> p a e", a=2).unsqueeze(1), [128, nh, 2, 16])
                u = f4_[:, 0:nn // 2].rearrange("p (h a e) -> p h a e", h=nh, a=2, e=16)
                w_ = f4_[:, nn // 2:nn].rearrange("p (h a e) -> p h a e", h=nh, a=2, e=16)
                k.op("pool", lambda e: e.tensor_tensor(u, xv[:, :, :, 1, :], sv, ALU.mult), reads=[fa, cst], writes=[f4_])
                k.op("dve", lambda e: e.tensor_tensor(w_, xv[:, :, :, 0, :], sv, ALU.mult), reads=[fa, cst], writes=[f4_])
                yield
                t1v = t1.rearrange("p (h a f e) -> p h a f e", h=nh, a=2, f=2, e=16)
                dv = qb[:, 0:nn].rearrange("p (h a f e) -> p h a f e", h=nh, a=2, f=2, e=16)
                k.op("dve", lambda e: e.tensor_tensor(dv[:, :, :, 0, :], t1v[:, :, :, 0, :], u, ALU.subtract), reads=[f3_, f4_], writes=[qb])
                k.op("dve", lambda e: e.tensor_tensor(dv[:, :, :, 1, :], t1v[:, :, :, 1, :], w_, ALU.add), reads=[f3_, f4_], writes=[qb])
            yield
            pt = nbt()
            k.transposes(pt, [(pt[:, i * 128:(i + 1) * 128], qb[:, i * 128:(i + 1) * 128]) for i in range(3)], ident[:], reads=[qb, ident])
            yield
            k.op("act", lambda e: e.activation(qkT_[:, 0:2, tt * 128:(tt + 1) * 128],
                                               pt[:, 0:256].rearrange("p (c t) -> p c t", c=2), AF.Copy), reads=[pt], writes=[qkT_tok[tt]])
            k.op("dve", lambda e: e.tensor_copy(kz[0:64, 0, tt * 128:(tt + 1) * 128], pt[0:64, 256:384]), reads=[pt], writes=[qkT_tok[tt]])
            k.op("act", lambda e: e.activation(kz[64:128, 1, tt * 128:(tt + 1) * 128], pt[64:128, 256:384], AF.Copy), reads=[pt], writes=[qkT_tok[tt]])

        interleave((prep(tt, i % 3) for i, tt in enumerate(range(NT))), depth=3)

        ob_free = [P[5], P[0], P[1], PT[0]]
        sb_free = [P[2], P[3], P[4], PT[1]]

        def BA(bk):
            return bk[:, :].bitcast(F32) if bk in (PT[0], PT[1]) else bk[:, :]
        pt_free = list(S["pT"])
        rec_free = list(zip(S["rec"], S["rec2"]))

        def core(q0, nq, keys, g, hp, par):
            head = 2 * hp + g
            pr = slice(hp * 64, hp * 64 + 64)
            nqt = nq * 128
            qs = slice(q0 * 128, q0 * 128 + nqt)
            ob = ob_free.pop(0)
            qtoks = [qkT_tok[t] for t in range(q0, q0 + nq)]
            for ki, (kt, mi) in enumerate(keys):
                sbk = sb_free.pop(0)
                k.mm(sbk, [(BA(sbk)[:, 0:nqt], [(kz[:, hp, kt * 128:(kt + 1) * 128], qkT_[:, g, qs])])], reads=[qkT_tok[kt]] + qtoks)
                yield
                pT = pt_free.pop(0)
                k.op("act", lambda e: e.activation(pT[:, 0:nqt], BA(sbk)[:, 0:nqt], AF.Exp, scale=0.125, bias=-SM_SHIFT), reads=[sbk], writes=[pT])
                sb_free.append(sbk)
                if mi is not None:
                    k.op("dve", lambda e: e.tensor_tensor(pT[:, 0:nqt], pT[:, 0:nqt], msk[:, mi, 0:nqt], ALU.mult), reads=[pT, msk], writes=[pT])
                yield
                k.mm(ob, [(BA(ob)[:, 0:nqt], [(vaug2[:, kt, hp, g, :], pT[:, 0:nqt])])], reads=[pT, tm_tok[kt]],
                     start=(ki == 0), stop=(ki == len(keys) - 1))
                pt_free.append(pT)
            yield
            orow = slice(g * 64, g * 64 + 64)
            drow = slice((1 - g) * 64, (1 - g) * 64 + 64)
            rec, rec2 = rec_free.pop(0)
            if which == "swa":
                k.op("dve", lambda e: e.tensor_scalar(rec[drow, 0:nqt], BA(ob)[drow, 0:nqt], sinkE[drow, head:head + 1], None, ALU.add),
                     reads=[ob, sinkE], writes=[rec])
                k.op("dve", lambda e: e.reciprocal(rec[drow, 0:nqt], rec[drow, 0:nqt]), reads=[rec], writes=[rec])
            else:
                k.op("dve", lambda e: e.reciprocal(rec[drow, 0:nqt], BA(ob)[drow, 0:nqt]), reads=[ob], writes=[rec])
            yield
            k.op("act", lambda e: e.activation(rec2[orow, 0:nqt], rec[drow, 0:nqt], AF.Copy), reads=[rec], writes=[rec2])
            yield
            kc = 2 * m + hp
            k.op("dve", lambda e: e.tensor_tensor(ycT[orow, kc, qs], BA(ob)[orow, 0:nqt], rec2[orow, 0:nqt], ALU.mult),
                 reads=[ob, rec2], writes=[ycT_tok[m][t] for t in range(q0, q0 + nq)])
            ob_free.append(ob)
            rec_free.append((rec, rec2))

        units = []
        ui = 0
        qgroups = [(2 + 4 * i, 4) for i in range(4)]
        if l < LAYERS - 1 or LAYERS == 1:
            qgroups = qgroups + [(0, 2)]
        for (q0, nq) in qgroups:
            if q0 == 0:
                keys = [(0, None), (1, None)]
            elif which == "gqa":
                keys = [(kt, None) for kt in range(NT)]
            else:
                keys = [(0, None), (1, None)]
                for r in range(-1, 5):
                    j = q0 + r
                    if 2 <= j < NT:
                        keys.append((j, r + 1))
            for g in range(2):
                for hp in range(2):
                    units.append(core(q0, nq, keys, g, hp, ui % 2))
                    ui += 1
        interleave(units, depth=4)

    def ret_mixer(l):
        k.barrier()
        use_set(SET_RET)
        load_win(l, "ret")
        ktf, ktb, vr, sg = tmA, tmB, tmC, tmD
        RB = [[P[0], P[1]], [P[2], P[3]]]
        rtmp = [(f2, f3, f4), (f5, SET_RET["f3b"], SET_RET["f4b"])]

        def prep(tt, par):
            p0, p1 = RB[par]
            for (pb, c0) in ((p0, 0), (p1, 512)):
                k.mm(pb, [(pb[:, 0:512], [(hT[:, kc, tt * 128:(tt + 1) * 128], win_sb[:, kc, c0:c0 + 512]) for kc in range(8)])],
                     reads=[hT_tok[tt], win_sb])
            fa, qb = f1[par], b2[par]
            t2, t3, t4 = rtmp[par]
            yield
            k.op("act", lambda e: e.activation(fa[:], p0[:, 0:512], AF.Copy), reads=[p0], writes=[fa])
            k.op("act", lambda e: e.activation(vr[:, tt, :], p1[:, 0:256], AF.Copy), reads=[p1], writes=[tm_tok[tt]])
            k.op("act", lambda e: e.activation(t2[:, 0:256], p1[:, 256:512], AF.Exp, scale=-1.0), reads=[p1], writes=[t2])
            yield
            k.op("act", lambda e: e.activation(t2[:, 0:256], t2[:, 0:256], AF.Ln, bias=1.0), reads=[t2], writes=[t2])
            k.op("act", lambda e: e.activation(t2[:, 0:256], t2[:, 0:256], AF.Exp, scale=-1.0), reads=[t2], writes=[t2])
            k.op("dve", lambda e: e.tensor_tensor(sg[:, tt, :], t2[:, 0:256], p1[:, 256:512], ALU.mult), reads=[t2, p1], writes=[tm_tok[tt]])
            yield
            rope(qb, fa, 8, tt, t3, t4)
            yield
            k.op("dve", lambda e: e.tensor_tensor(ktf[:, tt, :].rearrange("p (h d) -> p h d", h=4),
                                                  qb[:, 256:512].rearrange("p (h d) -> p h d", h=4),
                                                  bc(dk[:, 0:4].unsqueeze(2), [128, 4, 64]), ALU.mult), reads=[qb, dk], writes=[tm_tok[tt]])
            k.op("pool", lambda e: e.tensor_tensor(ktb[:, tt, :].rearrange("p (h d) -> p h d", h=4),
                                                   qb[:, 256:512].rearrange("p (h d) -> p h d", h=4),
                                                   bc(dk[:, 4:8].unsqueeze(2), [128, 4, 64]), ALU.mult), reads=[qb, dk], writes=[tm_tok[tt]])
            pt = PT[par]
            k.transposes(pt, [(pt[:, i * 128:(i + 1) * 128], qb[:, i * 128:(i + 1) * 128]) for i in range(4)], ident[:], reads=[qb, ident])
            yield
            k.op("act", lambda e: e.activation(qkT[:, 0:4, tt * 128:(tt + 1) * 128],
                                               pt[:, 0:512].rearrange("p (c t) -> p c t", c=4), AF.Copy), reads=[pt], writes=[qkT_tok[tt]])

        interleave((prep(tt, tt % 2) for tt in range(NT)), depth=2)
        k.barrier()
        allq = qkT_tok
        for d_ in range(2):
            for g in range(2):
                j = d_ * 2 + g
                k.op("dve" if g == 0 else "pool",
                     lambda e: e.tensor_tensor(qkT[:, 4 + j, :].rearrange("p (c t) -> p c t", c=NT),
                                               qkT[:, g, :].rearrange("p (c t) -> p c t", c=NT),
                                               bc(dq[:, j, :].unsqueeze(1), [128, NT, 128]), ALU.mult),
                     reads=allq + [dq], writes=allq)

        def u_mm(src, tt):
            ub = nb()
            k.mm(ub, [(ub[hp * 64:hp * 64 + 64, g * 64:g * 64 + 64],
                       [(src[:, tt, (2 * g + hp) * 64:(2 * g + hp) * 64 + 64], vr[:, tt, (2 * g + hp) * 64:(2 * g + hp) * 64 + 64])])
                      for g in range(2) for hp in range(2)], reads=[tm_tok[tt]])
            return ub

        def s_update(ub, d_):
            for g in range(2):
                k.op("dve", lambda e: e.scalar_tensor_tensor(sfst[:, g, :], sfst[:, g, :], a128c[:, d_ * 2 + g:d_ * 2 + g + 1],
                                                             ub[:, g * 64:g * 64 + 64], ALU.mult, ALU.add),
                     reads=[sfst, a128c, ub], writes=[sfst])

        chain = [1, 0] + list(range(NT - 1, 1, -1))
        k.op("dve", lambda e: e.memset(sfst[:], 0.0), writes=[sfst])
        for ci in range(len(chain) - 1):
            cur, nx = chain[ci], chain[ci + 1]
            ub = u_mm(ktb, cur)
            s_update(ub, 1)
            k.op("act", lambda e: e.activation(sball[:, nx, :, :], sfst[:], AF.Copy), reads=[sfst], writes=[sball])
        k.op("dve", lambda e: e.memset(sfst[:], 0.0), writes=[sfst])
        sf_free = list(sfbf) + list(SET_RET["sfbf2"])
        st_ = {"cur": None}
        am_sets = [[b3[0], b3[1]], [b2[0], b2[1]]]
        sq_tmp = [f5, f2]
        smalls = [small, small2]

        def r3(tt, par):
            need_out = not (tt < 2 and l == LAYERS - 1 and LAYERS > 1)
            incoming = st_["cur"]
            if tt < NT - 1:
                ubT = PT[par]
                ub = ubT[:, :].bitcast(F32)
                k.mm(ubT, [(ub[hp * 64:hp * 64 + 64, g * 64:g * 64 + 64],
                            [(ktf[:, tt, (2 * g + hp) * 64:(2 * g + hp) * 64 + 64], vr[:, tt, (2 * g + hp) * 64:(2 * g + hp) * 64 + 64])])
                           for g in range(2) for hp in range(2)], reads=[tm_tok[tt]])
                for g in range(2):
                    k.op("dve", lambda e: e.scalar_tensor_tensor(sfst[:, g, :], sfst[:, g, :], a128c[:, g:g + 1],
                                                                 ub[:, g * 64:g * 64 + 64], ALU.mult, ALU.add),
                         reads=[sfst, a128c, ubT], writes=[sfst])
                outb = sf_free.pop(0)
                k.op("act", lambda e: e.activation(outb[:], sfst[:], AF.Copy), reads=[sfst], writes=[outb])
                st_["cur"] = outb
            yield
            if not need_out:
                if incoming is not None:
                    sf_free.append(incoming)
                return
            sm = smalls[par]
            fo = f1[par]
            ab = P[par * 3 + 0]
            for hp in range(2):
                pr = slice(hp * 64, hp * 64 + 64)
                k.mm(ab, [(ab[:, g * 128:(g + 1) * 128],
                           [(qkT[pr, 2 + g, tt * 128:(tt + 1) * 128], qkT[pr, g, tt * 128:(tt + 1) * 128])]) for g in range(2)],
                     reads=[qkT_tok[tt]])
                yield
                am = am_sets[par][hp]
                k.op("dve", lambda e: e.tensor_tensor(am[:, 0:256].rearrange("p (g t) -> p g t", g=2),
                                                      ab[:, 0:256].rearrange("p (g t) -> p g t", g=2),
                                                      mT[:, hp::2, :], ALU.mult), reads=[ab, mT], writes=[am])
                yield
                ob = P[par * 3 + 1 + hp]
                groups = []
                rd = [am, tm_tok[tt], qkT_tok[tt]]
                for g in range(2):
                    h = 2 * g + hp
                    pairs = [(am[:, g * 128:(g + 1) * 128], vr[:, tt, h * 64:(h + 1) * 64])]
                    if tt != 0:
                        pairs.append((qkT[pr, 4 + g, tt * 128:(tt + 1) * 128], incoming[pr, g, :]))
                    if tt != 1:
                        pairs.append((qkT[pr, 6 + g, tt * 128:(tt + 1) * 128], sball[pr, tt, g, :]))
                    groups.append((ob[:, g * 64:(g + 1) * 64], pairs))
                if tt != 0:
                    rd.append(incoming)
                if tt != 1:
                    rd.append(sball)
                k.mm(ob, groups, reads=rd)
                yield
                k.op("act", lambda e: e.activation(fo[:, 0:256].rearrange("p (g hp d) -> p g hp d", g=2, hp=2)[:, :, hp, :],
                                                   ob[:, 0:128].rearrange("p (g d) -> p g d", g=2), AF.Copy), reads=[ob], writes=[fo])
            if incoming is not None:
                sf_free.append(incoming)
            yield
            tq = sq_tmp[par]
            k.op("dve", lambda e: e.tensor_reduce(sm[:, 0:4], fo[:, 0:256].rearrange("p (h d) -> p h d", h=4), AX.X, ALU.add), reads=[fo], writes=[sm])
            k.op("dve", lambda e: e.tensor_tensor(tq[:, 0:256], fo[:, 0:256], fo[:, 0:256], ALU.mult), reads=[fo], writes=[tq])
            k.op("dve", lambda e: e.tensor_reduce(sm[:, 4:8], tq[:, 0:256].rearrange("p (h d) -> p h d", h=4), AX.X, ALU.add), reads=[tq], writes=[sm])
            k.op("dve", lambda e: e.tensor_scalar(sm[:, 0:8], sm[:, 0:8], 1.0 / 64, None, ALU.mult), reads=[sm], writes=[sm])
            k.op("dve", lambda e: e.tensor_tensor(sm[:, 8:12], sm[:, 0:4], sm[:, 0:4], ALU.mult), reads=[sm], writes=[sm])
            k.op("dve", lambda e: e.tensor_tensor(sm[:, 8:12], sm[:, 4:8], sm[:, 8:12], ALU.subtract), reads=[sm], writes=[sm])
            k.op("dve", lambda e: e.tensor_scalar(sm[:, 8:12], sm[:, 8:12], EPS, None, ALU.add), reads=[sm], writes=[sm])
            yield
            k.op("pool", lambda e: e.tensor_tensor(sm[:, 12:16], sm[:, 8:12], C("mhalf", 0, 4), ALU.pow), reads=[sm, cst], writes=[sm])
            yield
            fv = fo[:, 0:256].rearrange("p (h d) -> p h d", h=4)
            k.op("dve", lambda e: e.tensor_tensor(fv, fv, bc(sm[:, 0:4].unsqueeze(2), [128, 4, 64]), ALU.subtract), reads=[fo, sm], writes=[fo])
            k.op("dve", lambda e: e.tensor_tensor(fv, fv, bc(sm[:, 12:16].unsqueeze(2), [128, 4, 64]), ALU.mult), reads=[fo, sm], writes=[fo])
            ycb = yc[par]
            k.op("dve", lambda e: e.tensor_tensor(ycb[:], fo[:, 0:256], sg[:, tt, :], ALU.mult), reads=[fo, tm_tok[tt]], writes=[ycb])
            yield
            emit_y(ycb, 0, tt, PT[par])

        interleave((r3(tt, tt % 2) for tt in range(NT)), depth=2)

    def hgrn_mixer(l):
        k.barrier()
        S = SET_HG
        win = S["win_sb"]
        sfst_, sfbf_, sball_ = S["sfst"], S["sfbf"], S["sball"]
        c0, n = MIX_COLS["hgrn"]
        k.dma("sp", win, win[:, :, 0:n], win_b[l], win_b[l][:, :].rearrange("p (c n) -> p c n", c=8)[:, :, c0:c0 + n])
        hrot = {}

        def hr(lst):
            key_ = id(lst[0])
            hrot[key_] = hrot.get(key_, -1) + 1
            return lst[hrot[key_] % len(lst)]

        def proj(pb, tt, lo, w):
            k.mm(pb, [(pb[:, 0:w], [(hT[:, kc, tt * 128:(tt + 1) * 128], win[:, kc, lo:lo + w]) for kc in range(8)])],
                 reads=[hT_tok[tt], win])

        def gate(th_ap, th_t, key_ap, key_t, logf_ap, logf_t, ncol):
            nd = ncol // 256
            fv = th_ap.rearrange("p (a d) -> p a d", a=nd)
            k.op("dve", lambda e: e.tensor_tensor(fv, fv, bc(hB[:, :].unsqueeze(1), [128, nd, 256]), ALU.mult), reads=[th_t, hB], writes=[th_t])
            k.op("pool", lambda e: e.tensor_tensor(fv, fv, bc(hA[:, :].unsqueeze(1), [128, nd, 256]), ALU.add), reads=[th_t, hA], writes=[th_t])
            k.op("pool", lambda e: e.tensor_scalar(th_ap, th_ap, TINY, None, ALU.max), reads=[th_t], writes=[th_t])
            k.op("dve", lambda e: e.tensor_scalar(key_ap, th_ap, -1.0, 1.0, ALU.mult, ALU.add), reads=[th_t], writes=[key_t])
            k.op("act", lambda e: e.activation(logf_ap, th_ap, AF.Ln), reads=[th_t], writes=[logf_t])

        def u_mm(kh_t, v_t, ci):
            ub = nb()
            rows = slice(ci * 64, ci * 64 + 64)
            k.mm(ub, [(ub[hp * 64:hp * 64 + 64, g * 64:g * 64 + 64],
                       [(kh_t[rows, (2 * g + hp) * 64:(2 * g + hp) * 64 + 64], v_t[rows, (2 * g + hp) * 64:(2 * g + hp) * 64 + 64])])
                      for g in range(2) for hp in range(2)], reads=[kh_t, v_t])
            return ub

        def s_update(ub, a_t, ci):
            for g in range(2):
                k.op("dve", lambda e: e.scalar_tensor_tensor(sfst_[:, g, :], sfst_[:, g, :], a_t[:, g * 2 + ci:g * 2 + ci + 1],
                                                             ub[:, g * 64:g * 64 + 64], ALU.mult, ALU.add),
                     reads=[sfst_, a_t, ub], writes=[sfst_])

        def chunk_decay(logf_ap, logf_t, a_t):
            pa = nb()
            k.mm(pa, [(pa[:, g * 2:g * 2 + 2], [(logf_ap[:, g * 128:(g + 1) * 128], C("chunkind"))]) for g in range(2)],
                 reads=[logf_t, cst])
            k.op("act", lambda e: e.activation(a_t[:, 0:4], pa[:, 0:4], AF.Exp), reads=[pa], writes=[a_t])

        k.op("dve", lambda e: e.memset(sfst_[:], 0.0), writes=[sfst_])
        order = [1, 0] + list(range(NT - 1, 1, -1))
        B0, B1, B2, B3, B4, B5 = P

        def chunk_decay2(logf_ap, logf_t, a_t, pa, c0_):
            k.mm(pa, [(pa[:, c0_ + g * 2:c0_ + g * 2 + 2], [(logf_ap[:, g * 128:(g + 1) * 128], C("chunkind"))]) for g in range(2)],
                 reads=[logf_t, cst])
            k.op("act", lambda e: e.activation(a_t[:, 0:4], pa[:, c0_:c0_ + 4], AF.Exp), reads=[pa], writes=[a_t])

        def u_mm2(kh_t, v_t, ci):
            ub = B5
            rows = slice(ci * 64, ci * 64 + 64)
            uo = 256 + ci * 128
            k.mm(ub, [(ub[hp * 64:hp * 64 + 64, uo + g * 64:uo + g * 64 + 64],
                       [(kh_t[rows, (2 * g + hp) * 64:(2 * g + hp) * 64 + 64], v_t[rows, (2 * g + hp) * 64:(2 * g + hp) * 64 + 64])])
                      for g in range(2) for hp in range(2)], reads=[kh_t, v_t])
            return ub, uo

        def s_update2(ubo, a_t, ci):
            ub, uo = ubo
            for g in range(2):
                k.op("dve", lambda e: e.scalar_tensor_tensor(sfst_[:, g, :], sfst_[:, g, :], a_t[:, g * 2 + ci:g * 2 + ci + 1],
                                                             ub[:, uo + g * 64:uo + g * 64 + 64], ALU.mult, ALU.add),
                     reads=[sfst_, a_t, ub], writes=[sfst_])

        def gate2(th_ap, th_t, key_ap, key_t, logf_ap, logf_t, ncol):
            nd = ncol // 256
            fv = th_ap.rearrange("p (a d) -> p a d", a=nd)
            k.op("dve", lambda e: e.tensor_scalar(th_ap, th_ap, 1e18, None, ALU.min), reads=[th_t], writes=[th_t])
            k.op("act", lambda e: e.activation(logf_ap, th_ap, AF.Ln, bias=1.0), reads=[th_t], writes=[logf_t])
            k.op("act", lambda e: e.activation(th_ap, logf_ap, AF.Exp, scale=-1.0), reads=[logf_t], writes=[th_t])
            if l == 0:
                k.op("dve", lambda e: e.tensor_scalar(logf_ap, logf_ap, -1.0, -69.07755278982137, ALU.mult, ALU.max), reads=[logf_t], writes=[logf_t])
                k.op("dve", lambda e: e.tensor_scalar(key_ap, th_ap, -1.0, 1.0, ALU.mult, ALU.add), reads=[th_t], writes=[key_t])
            else:
                k.op("dve", lambda e: e.tensor_tensor(fv, fv, bc(hB[:, :].unsqueeze(1), [128, nd, 256]), ALU.mult), reads=[th_t, hB], writes=[th_t])
                k.op("dve", lambda e: e.tensor_tensor(fv, fv, bc(hA[:, :].unsqueeze(1), [128, nd, 256]), ALU.add), reads=[th_t, hA], writes=[th_t])
                k.op("dve", lambda e: e.tensor_scalar(th_ap, th_ap, TINY, None, ALU.max), reads=[th_t], writes=[th_t])
                k.op("dve", lambda e: e.tensor_scalar(key_ap, th_ap, -1.0, 1.0, ALU.mult, ALU.add), reads=[th_t], writes=[key_t])
                k.op("act", lambda e: e.activation(logf_ap, th_ap, AF.Ln), reads=[th_t], writes=[logf_t])

        p1buf = {}

        def pass1_prep(oi, tt, par):
            pb = B0
            proj(pb, tt, 512, 512)
            th, key, lg = S["thz"][par], S["key"][par], S["lg"][par]
            vt, kh, at, ex = S["vt"][par], S["kh"][par], S["aTsb"][par], S["fe"][par]
            yield
            k.op("act", lambda e: e.activation(th[:, 0:256], pb[:, 0:256], AF.Exp, scale=-1.0), reads=[pb], writes=[th])
            k.op("act", lambda e: e.activation(vt[:], pb[:, 256:512], AF.Copy), reads=[pb], writes=[vt])
            yield
            gate2(th[:, 0:256], th, key[:, 0:256], key, lg[:, 0:256], lg, 256)
            yield
            cb = B1
            k.mm(cb, [(cb[:, 0:256], [(C("lstrict"), lg[:, 0:256])])], reads=[cst, lg])
            chunk_decay2(lg[:, 0:256], lg, at, B2, 0)
            yield
            k.op("act", lambda e: e.activation(ex[:], cb[:, 0:256], AF.Exp), reads=[cb], writes=[ex])
            yield
            k.op("dve", lambda e: e.tensor_tensor(kh[:], key[:, 0:256], ex[:], ALU.mult), reads=[key, ex], writes=[kh])
            p1buf[oi] = (kh, vt, at)

        def pass1_chain(oi, tt):
            kh, vt, at = p1buf[oi]
            for ci in (1, 0):
                cur = tt * 2 + ci
                if ci == 1:
                    nx = cur - 1
                elif oi + 1 < len(order):
                    nx = order[oi + 1] * 2 + 1
                else:
                    nx = None
                if nx is None:
                    break
                ub = u_mm2(kh, vt, ci)
                yield
                s_update2(ub, at, ci)
                yield
                k.op("act", lambda e: e.activation(sball_[:, nx, :, :], sfst_[:], AF.Copy), reads=[sfst_], writes=[sball_])
                yield

        def pass1_all():
            gens = [pass1_prep(oi, tt, oi % 2) for oi, tt in enumerate(order)]
            for _ in gens[0]:
                pass
            for oi, tt in enumerate(order):
                lst = [pass1_chain(oi, tt)]
                if oi + 1 < len(order):
                    lst.append(gens[oi + 1])
                interleave(lst, depth=2)
        pass1_all()

        k.op("dve", lambda e: e.memset(sfst_[:], 0.0), writes=[sfst_])

        def FM(kind, d_, g):
            return kind * 4 + d_ * 2 + g

        p2buf = {}

        def pass2_prep(tt, par):
            p0, p1, p2 = B0, B1, B2
            proj(p0, tt, 0, 512)
            proj(p1, tt, 512, 512)
            proj(p2, tt, 1024, 256)
            thz, sq, tmp, key, lg = S["thz"][par], S["sq"][par], S["tmp"][par], S["key"][par], S["lg"][par]
            vt, sgt, at, fm, khf = S["vt"][2 + par], S["sgt"][par], S["aTsb"][2 + par], S["fm"][par], S["kh"][2 + par]
            yield
            k.op("act", lambda e: e.activation(thz[:, 0:256], p0[:, 256:512], AF.Exp, scale=-1.0), reads=[p0], writes=[thz])
            k.op("act", lambda e: e.activation(thz[:, 256:512], p1[:, 0:256], AF.Exp, scale=-1.0), reads=[p1], writes=[thz])
            k.op("act", lambda e: e.activation(tmp[:, 0:256], p0[:, 0:256], AF.Exp, scale=-1.0), reads=[p0], writes=[tmp])
            k.op("act", lambda e: e.activation(vt[:], p1[:, 256:512], AF.Copy), reads=[p1], writes=[vt])
            yield
            k.op("act", lambda e: e.activation(tmp[:, 0:256], tmp[:, 0:256], AF.Ln, bias=1.0), reads=[tmp], writes=[tmp])
            k.op("act", lambda e: e.activation(tmp[:, 0:256], tmp[:, 0:256], AF.Exp, scale=-1.0), reads=[tmp], writes=[tmp])
            k.op("dve", lambda e: e.tensor_tensor(sq[:, 0:256], tmp[:, 0:256], p0[:, 0:256], ALU.mult), reads=[tmp, p0], writes=[sq])
            gate2(thz[:, :], thz, key[:, :], key, lg[:, :], lg, 512)
            yield
            sqq = S["sqq"][par]
            k.op("act", lambda e: e.activation(sqq[:, 0:256], p2[:, 0:256], AF.Exp, scale=-1.0), reads=[p2], writes=[sqq])
            chunk_decay2(lg[:, 0:256], lg, at, B2, 256)
            yield
            k.op("act", lambda e: e.activation(sqq[:, 0:256], sqq[:, 0:256], AF.Ln, bias=1.0), reads=[sqq], writes=[sqq])
            k.op("act", lambda e: e.activation(sqq[:, 0:256], sqq[:, 0:256], AF.Exp, scale=-1.0), reads=[sqq], writes=[sqq])
            k.op("dve", lambda e: e.tensor_tensor(sgt[:], sqq[:, 0:256], p2[:, 0:256], ALU.mult), reads=[sqq, p2], writes=[sgt])
            for d_ in range(2):
                lgd = lg[:, d_ * 256:(d_ + 1) * 256]
                keyd = key[:, d_ * 256:(d_ + 1) * 256]
                m_c, m_s32, m_s64, m_b64 = (("l32incl", "u32strict", "ustrict", "lincl") if d_ == 0
                                            else ("u32incl", "l32strict", "lstrict", "uincl"))
                ca, cbk = B0, B1
                k.mm(ca, [(ca[:, 0:256], [(C(m_c), lgd)]), (ca[:, 256:512], [(C(m_s32), lgd)])], reads=[cst, lg])
                need64k = (d_ == 0)
                grp = [(cbk[:, 0:256], [(C(m_b64), lgd)])]
                if need64k:
                    grp.append((cbk[:, 256:512], [(C(m_s64), lgd)]))
                k.mm(cbk, grp, reads=[cst, lg])
                yield
                fe_ = S["fe"]
                e_c, e_nc, e_s32, e_b64, e_s64 = [fe_[(par * 2 + d_) * 5 + i] for i in range(5)]
                k.op("act", lambda e: e.activation(e_c[:], ca[:, 0:256], AF.Exp), reads=[ca], writes=[e_c])
                k.op("act", lambda e: e.activation(e_nc[:], ca[:, 0:256], AF.Exp, scale=-1.0), reads=[ca], writes=[e_nc])
                k.op("act", lambda e: e.activation(e_s32[:], ca[:, 256:512], AF.Exp), reads=[ca], writes=[e_s32])
                k.op("act", lambda e: e.activation(e_b64[:], cbk[:, 0:256], AF.Exp), reads=[cbk], writes=[e_b64])
                if need64k:
                    k.op("act", lambda e: e.activation(e_s64[:], cbk[:, 256:512], AF.Exp), reads=[cbk], writes=[e_s64])
                yield
                q8 = S["b1"][par * 2 + d_]
                k.op("dve", lambda e: e.tensor_tensor(q8[:, 0:256], sq[:, 0:256], e_c[:], ALU.mult), reads=[sq, e_c], writes=[q8])
                k.op("pool", lambda e: e.tensor_tensor(q8[:, 256:512], keyd, e_nc[:], ALU.mult), reads=[key, e_nc], writes=[q8])
                k.op("dve", lambda e: e.tensor_tensor(q8[:, 512:768], keyd, e_s32[:], ALU.mult), reads=[key, e_s32], writes=[q8])
                k.op("pool", lambda e: e.tensor_tensor(q8[:, 768:1024], sq[:, 0:256], e_b64[:], ALU.mult), reads=[sq, e_b64], writes=[q8])
                if need64k:
                    k.op("dve", lambda e: e.tensor_tensor(khf[:], keyd, e_s64[:], ALU.mult), reads=[key, e_s64], writes=[khf])
                yield
                pt = PT[0]
                k.transposes(pt, [(pt[:, i * 128:(i + 1) * 128], q8[:, i * 128:(i + 1) * 128]) for i in range(8)],
                             ident[:], reads=[q8, ident])
                yield
                k.op("act", lambda e: e.activation(fm[:, :, :].rearrange("p (kd d g) t -> p kd d g t", kd=4, d=2, g=2)[:, :, d_, :, :],
                                                   pt[:, :].rearrange("p (kd g t) -> p kd g t", kd=4, g=2), AF.Copy),
                     reads=[pt], writes=[fm])
            p2buf[tt] = (vt, sgt, at, fm, khf)

        state = {"sf_prev": None}

        def pass2_main(tt, par):
            vt, sgt, at, fm, khf = p2buf[tt]
            need_out = not (tt < 2 and l == LAYERS - 1 and LAYERS > 1)
            sf_in = [state["sf_prev"], None]
            ub = u_mm2(khf, vt, 0)
            s_update2(ub, at, 0)
            s1 = hr(sfbf_)
            k.op("act", lambda e: e.activation(s1[:], sfst_[:], AF.Copy), reads=[sfst_], writes=[s1])
            sf_in[1] = s1
            if tt < NT - 1:
                ub = u_mm2(khf, vt, 1)
                s_update2(ub, at, 1)
                s2 = hr(sfbf_)
                k.op("act", lambda e: e.activation(s2[:], sfst_[:], AF.Copy), reads=[sfst_], writes=[s2])
                state["sf_prev"] = s2
            yield
            if not need_out:
                return
            sm = S["sm"][par]
            fo = S["fo"][par]
            for hp in range(2):
                pr = slice(hp * 64, hp * 64 + 64)
                a1, a2 = B3, B4
                for (bk, kk) in ((a1, 1), (a2, 2)):
                    k.mm(bk, [(bk[:, (d_ * 2 + g) * 128:(d_ * 2 + g + 1) * 128],
                               [(fm[pr, FM(kk, d_, g), :], fm[pr, FM(0, d_, g), :])]) for d_ in range(2) for g in range(2)],
                         reads=[fm])
                yield
                am = S["am"][hp]
                for bi, (bk, masks) in enumerate(((a1, ("l32incl", "u32incl")), (a2, ("m2f", "m2b")))):
                    for d_ in range(2):
                        o_ = bi * 512 + d_ * 256
                        k.op("dve" if bi == 0 else "pool", lambda e: e.tensor_tensor(
                            am[:, o_:o_ + 256].rearrange("p (g t) -> p g t", g=2),
                            bk[:, d_ * 256:(d_ + 1) * 256].rearrange("p (g t) -> p g t", g=2),
                            bc(C(masks[d_]).unsqueeze(1), [128, 2, 128]), ALU.mult), reads=[bk, cst], writes=[am]) if bi == 0 else \
                        k.op("dve", lambda e: e.tensor_tensor(
                            am[:, o_:o_ + 256].rearrange("p (g t) -> p g t", g=2),
                            bk[:, d_ * 256:(d_ + 1) * 256].rearrange("p (g t) -> p g t", g=2),
                            bc(C(masks[d_]).unsqueeze(1), [128, 2, 128]), ALU.mult), reads=[bk, cst], writes=[am])
                yield
                ob = B5
                rd = [am, vt, fm, sball_]
                for g in range(2):
                    h = 2 * g + hp
                    k.mm(ob, [(ob[:, g * 64:(g + 1) * 64],
                               [(am[:, (bi * 4 + d_ * 2 + g) * 128:(bi * 4 + d_ * 2 + g + 1) * 128], vt[:, h * 64:(h + 1) * 64])
                                for bi in range(2) for d_ in range(2)])], reads=rd, stop=False)
                    groups = []
                    for ci in range(2):
                        chunk = tt * 2 + ci
                        tk = slice(ci * 64, ci * 64 + 64)
                        pairs = []
                        if sf_in[ci] is not None:
                            pairs.append((fm[pr, FM(3, 0, g), tk], sf_in[ci][pr, g, :]))
                            if sf_in[ci] not in rd:
                                rd.append(sf_in[ci])
                        if chunk != 3:
                            pairs.append((fm[pr, FM(3, 1, g), tk], sball_[pr, chunk, g, :]))
                        if pairs:
                            groups.append((ob[ci * 64:ci * 64 + 64, g * 64:(g + 1) * 64], pairs))
                    if groups:
                        k.mm(ob, groups, reads=rd, start=False)
                yield
                k.op("act", lambda e: e.activation(fo[:, 0:256].rearrange("p (g hp d) -> p g hp d", g=2, hp=2)[:, :, hp, :],
                                                   ob[:, 0:128].rearrange("p (g d) -> p g d", g=2), AF.Copy),
                     reads=[ob], writes=[fo])
                yield
            yield
            tq = S["fe"][(par * 2 + 1) * 5 + 4]
            k.op("dve", lambda e: e.tensor_tensor(tq[:, 0:256], fo[:, 0:256], fo[:, 0:256], ALU.mult), reads=[fo], writes=[tq])
            k.op("dve", lambda e: e.tensor_reduce(sm[:, 4:8], tq[:, 0:256].rearrange("p (h d) -> p h d", h=4), AX.X, ALU.add),
                 reads=[tq], writes=[sm])
            k.op("dve", lambda e: e.tensor_scalar(sm[:, 8:12], sm[:, 4:8], 1.0 / 64, EPS, ALU.mult, ALU.add), reads=[sm], writes=[sm])
            yield
            k.op("pool", lambda e: e.tensor_tensor(sm[:, 12:16], sm[:, 8:12], C("mhalf", 0, 4), ALU.pow), reads=[sm, cst], writes=[sm])
            yield
            fv2 = fo[:, 0:256].rearrange("p (h d) -> p h d", h=4)
            k.op("dve", lambda e: e.tensor_tensor(fv2, fv2, bc(sm[:, 12:16].unsqueeze(2), [128, 4, 64]), ALU.mult), reads=[fo, sm], writes=[fo])
            ycb = S["yc"][par]
            k.op("dve", lambda e: e.tensor_tensor(ycb[:], fo[:, 0:256], sgt[:], ALU.mult), reads=[fo, sgt], writes=[ycb])
            yield
            emit_y(ycb, 3, tt, PT[1])

        def pass2_all():
            gens = [pass2_prep(tt, tt % 2) for tt in range(NT)]
            for _ in gens[0]:
                pass
            for tt in range(NT):
                lst = [pass2_main(tt, tt % 2)]
                if tt + 1 < NT:
                    lst.append(gens[tt + 1])
                interleave(lst, depth=2)
        pass2_all()

    def x_src(l, j, tt):
        if l == 0:
            if tt < 2:
                return ctx_d, ctx_d[j, tt * 128:(tt + 1) * 128, :]
            return x_d, x_d[j, (tt - 2) * 128:(tt - 1) * 128, :]
        return xs_tok[j][tt], xs_d[j, tt * 128:(tt + 1) * 128, :]

    def phase_a(l, j):
        k.barrier()

        def body(tt, par):
            xt = xbuf3[par]
            st_, ap_ = x_src(l, j, tt)
            k.dma("sp", xt, xt[:], st_, ap_)
            yield
            yield from modulate_gen(xt, tabs["sc1c" if tt < 2 else "sc1"], tabs["sh1c" if tt < 2 else "sh1"], tt, par, PT[tt % 2], mul_eng="dve")

        load_tab("sc1c", 0, l, 4, 1024)
        load_tab("sh1c", 1, l, 4, 0)
        load_tab("sc1", 2, l, j, 1024)
        load_tab("sh1", 3, l, j, 0)
        interleave((body(tt, i % 3) for i, tt in enumerate(range(NT))), depth=3)

    def phase_c(l, j):
        k.barrier()
        k.dma("sp", wout_sb, wout_sb[:], wout_b[l], wout_b[l][:, :].rearrange("p (c n) -> p c n", c=8))
        t0 = 0 if (l < LAYERS - 1 or LAYERS == 1) else 2
        load_ln("ln1g", 3, l, 0)
        load_ln("ln1b", 4, l, 1)
        YB = [[P[0], P[1]], [P[2], P[3]], [P[4], P[5]]]

        def body(tt, par):
            xt, wa, st = xbuf3[par], wkA3[par], stt3[par]
            st_, ap_ = x_src(l, j, tt)
            k.dma("sp", xt, xt[:], st_, ap_)
            ybs = YB[par]
            for c in range(2):
                pb = ybs[c]
                k.mm(pb, [(pb[:, :], [(ycT[:, kc, tt * 128:(tt + 1) * 128], wout_sb[:, kc, c * 512:(c + 1) * 512]) for kc in range(8)])],
                     reads=[ycT_tok[m][tt] for m in range(4)] + [wout_sb])
            yield
            g1 = tabs["g1"]
            for c in range(2):
                k.op("dve", lambda e: e.tensor_tensor(wa[:, c * 512:(c + 1) * 512], ybs[c][:, :], g1[:, c * 512:(c + 1) * 512], ALU.mult),
                     reads=[ybs[c], g1], writes=[wa])
            k.op("dve", lambda e: e.scalar_tensor_tensor(wa[:], xt[:], ALPHA, wa[:], ALU.mult, ALU.add), reads=[xt, wa], writes=[wa])
            yield from ln_stats_gen(wa, st)
            yield
            k.op("act", lambda e: e.activation(wa[:], wa[:], AF.Identity, bias=st[:, 2:3], scale=st[:, 1:2]), reads=[wa, st], writes=[wa])
            yield
            k.op("pool", lambda e: e.tensor_tensor(wa[:], wa[:], tabs["ln1g"][:], ALU.mult), reads=[wa, tabs["ln1g"]], writes=[wa])
            yield
            k.op("dve", lambda e: e.tensor_tensor(xt[:], wa[:], tabs["ln1b"][:], ALU.add), reads=[wa, tabs["ln1b"]], writes=[xt])
            yield
            k.dma("sp", xm_tok[j][tt], xm_d[j, tt * 128:(tt + 1) * 128, :], xt, xt[:])
            yield from modulate_gen(xt, tabs["sc2"], tabs["sh2"], tt, par, PT[tt % 2])

        if t0 == 0:
            load_tab("g1", 0, l, 4, 2048)
            load_tab("sc2", 1, l, 4, 4096)
            load_tab("sh2", 2, l, 4, 3072)
            interleave((body(tt, tt % 3) for tt in range(0, 2)), depth=2)
        load_tab("g1", 0, l, j, 2048)
        load_tab("sc2", 1, l, j, 4096)
        load_tab("sh2", 2, l, j, 3072)
        interleave((body(tt, tt % 3) for tt in range(2, NT)), depth=3)

    hid_tok = [k.tok("hid%d" % i) for i in range(8)]

    def phase_d(l, j):
        k.barrier()
        groups = [(2 + 4 * i, 4) for i in range(4)]
        if l < LAYERS - 1 or LAYERS == 1:
            groups = [(0, 2)] + groups
        last = (l == LAYERS - 1)
        load_ln("ln2g", 3, l, 2)
        load_ln("ln2b", 4, l, 3)
        load_tab("g2c", 0, l, 4, 5120)
        load_tab("g2l", 1, l, j, 5120)
        PTF = [T(PT[i][:, :].bitcast(F32), "ptf%d" % i, psum=True) for i in range(2)]
        acc_banks = [P[0], P[1], P[2], P[3], P[4], P[5], PT[0], PT[1]]

        def acc_ap(bk):
            return bk[:, :].bitcast(F32) if bk in (PT[0], PT[1]) else bk[:, :]

        UPB = Rot([P[4], P[5]])

        def epilogue(tt, s, obs, par):
            xt, wa, st = xbuf[par], wkA[par], stt[par]
            k.dma("sp", xt, xt[:], xm_tok[j][tt], xm_d[j, tt * 128:(tt + 1) * 128, :])
            g2 = tabs["g2c" if tt < 2 else "g2l"]
            yield
            for cc in range(2):
                k.op("dve", lambda e: e.tensor_tensor(wa[:, cc * 512:(cc + 1) * 512], acc_ap(obs[(s, cc)]), g2[:, cc * 512:(cc + 1) * 512], ALU.mult),
                     reads=[obs[(s, cc)], g2], writes=[wa])
            k.op("dve", lambda e: e.scalar_tensor_tensor(wa[:], xt[:], ALPHA, wa[:], ALU.mult, ALU.add), reads=[xt, wa], writes=[wa])
            yield from ln_stats_gen(wa, st)
            yield
            k.op("act", lambda e: e.activation(wa[:], wa[:], AF.Identity, bias=st[:, 2:3], scale=st[:, 1:2]), reads=[wa, st], writes=[wa])
            yield
            k.op("dve", lambda e: e.tensor_tensor(wa[:], wa[:], tabs["ln2g"][:], ALU.mult), reads=[wa, tabs["ln2g"]], writes=[wa])
            yield
            k.op("dve", lambda e: e.tensor_tensor(xt[:], wa[:], tabs["ln2b"][:], ALU.add), reads=[wa, tabs["ln2b"]], writes=[xt])
            yield
            if last and tt < 2:
                pass
            elif last:
                k.dma("sp", out_d, out_d[j, (tt - 2) * 128:(tt - 1) * 128, :], xt, xt[:], is_output=True)
            else:
                k.dma("sp", xs_tok[j][tt], xs_d[j, tt * 128:(tt + 1) * 128, :], xt, xt[:])

        for (t0, ntile) in groups:
            ntok = ntile * 128
            tk = slice(t0 * 128, t0 * 128 + ntok)
            for c in range(8):
                wsb = wup_sb[c % 2]
                k.dma("sp", wsb, wsb[:], wup_b[l], wup_b[l][c].rearrange("p (c n) -> p c n", c=8))
                for jj in range(4):
                    jf = c * 4 + jj
                    pb = UPB()
                    k.mm(pb, [(pb[:, 0:ntok], [(wsb[:, kc, jj * 128:(jj + 1) * 128], hT[:, kc, tk]) for kc in range(8)])],
                         reads=[wsb] + [hT_tok[t] for t in range(t0, t0 + ntile)])
                    fr = fr_x[jf % 2]
                    k.op("act", lambda e: e.activation(fr[:, 0:ntok], pb[:, 0:ntok], AF.Relu), reads=[pb], writes=[fr])
                    k.op("dve", lambda e: e.tensor_tensor(hidT[:, jf, 0:ntok], fr[:, 0:ntok], fr[:, 0:ntok], ALU.mult),
                         reads=[fr], writes=[hid_tok[c]])
            obs = {}
            bi = 0
            for s_ in range(ntile):
                for cc in range(2):
                    obs[(s_, cc)] = acc_banks[bi]
                    bi += 1
            for c in range(8):
                wsb = wdn_sb[c % 2]
                k.dma("sp", wsb, wsb[:], wdn_b[l], wdn_b[l][c].rearrange("p (c n) -> p c n", c=4))
                for s_ in range(ntile):
                    for cc in range(2):
                        ob = obs[(s_, cc)]
                        k.mm(ob, [(acc_ap(ob), [(hidT[:, c * 4 + jj, s_ * 128:(s_ + 1) * 128], wsb[:, jj, cc * 512:(cc + 1) * 512]) for jj in range(4)])],
                             reads=[wsb, hid_tok[c]], start=(c == 0), stop=(c == 7))
            order_ = list(range(ntile))[::-1] if ntile == 4 else list(range(ntile))
            interleave((epilogue(t0 + s_, s_, obs, i % 2) for i, s_ in enumerate(order_)), depth=2)

    for l in range(LAYERS):
        if "S" in stages:
            layer_setup(l)
        for j in range(NB):
            if "A" in stages:
                phase_a(l, j)
            if "R" in stages:
                ret_mixer(l)
            if "G" in stages:
                attn_mixer(l, "gqa")
            if "W" in stages:
                attn_mixer(l, "swa")
            if "H" in stages:
                hgrn_mixer(l)
            if dbg and l == 0 and j == 0:
                k.barrier()
                for kc in range(8):
                    wa = nxt(wkA, "a")
                    for hh in range(0, TT, 1024):
                        w = min(1024, TT - hh)
                        srcT = hT if dbg == "hT" else ycT
                        k.op("dve", lambda e: e.tensor_copy(wa[:, 0:w], srcT[:, kc, hh:hh + w]),
                             reads=[ycT_tok[m][t] for m in range(4) for t in range(NT)] + hT_tok, writes=[wa])
                        k.dma("sp", dbg_d, dbg_d[:, kc * TT + hh:kc * TT + hh + w], wa, wa[:, 0:w], is_output=True)
            if "C" in stages:
                phase_c(l, j)
            if "D" in stages:
                phase_d(l, j)
    k.finish()
    return nc, k


def _perm_win_cols():
    perm = np.arange(NIN)
    for base in (1024, 1536):
        blk = perm[base:base + 256].reshape(4, 64)
        perm[base:base + 256] = blk[[0, 2, 1, 3]].reshape(-1)
    return perm


def prep_shared(inputs, LAYERS=2):
    f = lambda a: np.ascontiguousarray(a, dtype=np.float32)
    perm = _perm_win_cols()

    def pkn(w, kc):
        K, N = w.shape
        return np.ascontiguousarray(w.reshape(kc, 128, N).transpose(1, 0, 2).reshape(128, kc * N))

    sh = {}
    sh["cst"] = make_consts()
    sh["wada"] = f(np.stack([pkn(inputs["w_ada"][l], 8) for l in range(2)]))
    sh["bada"] = f(inputs["b_ada"]).reshape(2, 1, 6144)
    sh["win"] = f(np.stack([pkn(inputs["w_in"][l][:, perm], 8) for l in range(2)]))
    sh["wout"] = f(np.stack([pkn(inputs["w_out"][l], 8) for l in range(2)]))
    wup = []
    wdn = []
    for l in range(2):
        wu = inputs["w_up"][l]
        wup.append(np.stack([pkn(wu[:, c * 512:(c + 1) * 512], 8) for c in range(8)]))
        wd = inputs["w_down"][l]
        wdn.append(np.stack([pkn(wd[c * 512:(c + 1) * 512, :], 4) for c in range(8)]))
    sh["wup"] = f(np.stack(wup))
    sh["wdn"] = f(np.stack(wdn))
    sh["rdl"] = f(inputs["ret_decay_logit"]).reshape(2, 1, 8)
    sh["gain"] = f(np.concatenate([np.tile(inputs["gqa_q_gain"], (1, 4)), np.tile(inputs["gqa_k_gain"], (1, 2))], axis=1)).reshape(2, 1, 384)
    sh["sink"] = f(inputs["swa_sink"]).reshape(2, 1, 4)
    sh["hlb"] = f(inputs["hgrn_lb"]).reshape(2, 1, 256)
    sh["lnp"] = f(np.stack([inputs["ln1_g"], inputs["ln1_b"], inputs["ln2_g"], inputs["ln2_b"]], axis=1)).reshape(2, 4, 1, D)
    return sh


def core_inputs(inputs, sh, core, NB):
    b0 = core * NB
    m = dict(sh)
    m["x"] = np.ascontiguousarray(inputs["x"][b0:b0 + NB], dtype=np.float32)
    m["ctx"] = np.ascontiguousarray(inputs["ctx"][b0:b0 + NB], dtype=np.float32)
    cv = np.concatenate([inputs["c"][b0:b0 + NB], np.zeros((4 - NB, D), np.float32), inputs["c_ctx"][None, :]], axis=0)
    m["cT"] = np.ascontiguousarray(cv.reshape(5, 8, 128).transpose(2, 1, 0).reshape(128, 40), dtype=np.float32)
    return m


_CACHE = {}


def kernel(**inputs):
    inputs = {k_: np.asarray(v) for k_, v in inputs.items()}
    NB = 4
    if "nc" not in _CACHE:
        _CACHE["nc"] = build(NB=NB, LAYERS=2)[0]
    nc = _CACHE["nc"]
    sh = prep_shared(inputs)
    in_maps = [core_inputs(inputs, sh, c, NB) for c in range(8)]
    res = run_bass_kernel_spmd(nc, in_maps, core_ids=list(range(8)))
    out = np.concatenate([np.asarray(r["out"]) for r in res.results], axis=0)
    return out.astype(np.float32)
```

```python
from contextlib import ExitStack
import numpy as np
import concourse.bass as bass
import concourse.mybir as mybir
from concourse.bass_utils import run_bass_kernel_spmd

F32 = mybir.dt.float32
BF16 = mybir.dt.bfloat16
AF = mybir.ActivationFunctionType
ALU = mybir.AluOpType
AX = mybir.AxisListType

D = 1024
SEQ = 2048
LC = 256
NT = 18
TT = NT * 128
NIN = 3328
DFF = 4096
ALPHA = 4.0 ** 0.25
EPS = 1e-6
TINY = 1e-30
SM_SHIFT = 12.0


class T:
    __slots__ = ("ap", "name", "w", "r", "dsem", "dcnt", "psum", "is_dram")

    def __init__(self, ap, name, psum=False, is_dram=False):
        self.is_dram = is_dram
        self.ap = ap
        self.name = name
        self.w = None
        self.r = []
        self.dsem = None
        self.dcnt = 0
        self.psum = psum

    def __getitem__(self, k):
        return self.ap[k]


class KB:
    def __init__(self, nc):
        self.nc = nc
        self.es = ExitStack()
        self.eng = {"pe": nc.tensor, "act": nc.scalar, "dve": nc.vector, "pool": nc.gpsimd, "sp": nc.sync}
        self.sem = {}
        self.cnt = {}
        self.known = {}
        for e in self.eng:
            self.sem[e] = self.es.enter_context(nc.semaphore("s_" + e))
            self.cnt[e] = 0
            self.known[e] = {}
        self.out_events = []
        self.n_ins = 0
        self.n_wait = 0
        self.sb_bytes = 0
        self.dma_owners = []

    SB_BASE = 16512
    SB_END = 229376 - 1024

    def sbuf(self, name, shape, dtype=F32):
        n = 1
        for s in shape[1:]:
            n *= s
        nbytes = n * (4 if dtype == F32 else 2)
        nbytes = (nbytes + 31) // 32 * 32
        if not hasattr(self, "top"):
            arena = self.nc.alloc_sbuf_tensor("arena", [128, (self.SB_END - self.SB_BASE) // 4], F32)
            self.SB_BASE = self.nc.lookup_mloc(arena).addr
            self.SB_END = self.SB_BASE + (self.SB_END - 16512)
            self.top = self.SB_BASE
            self.nalloc = 0
        off = self.top
        self.top += nbytes
        assert self.top <= self.SB_END, ("SBUF overflow", name, self.top)
        self.nalloc += 1
        t = self.nc.alloc_sbuf_tensor_at("%s_%d" % (name, self.nalloc), list(shape), dtype, offset=off)
        return T(t, name)

    def barrier(self):
        evs = [(o.dsem, o.dcnt) for o in self.dma_owners if o.dcnt > 0]
        evs += [(self.sem[e], self.cnt[e]) for e in ("pe", "act", "dve", "pool") if self.cnt[e] > 0]
        self._wait("sp", evs)
        ins = self.nc.sync.nop()
        self.cnt["sp"] += 1
        ins.then_inc(self.sem["sp"], 1)
        self.n_ins += 1
        for e in ("pe", "act", "dve", "pool"):
            self._wait(e, [(self.sem["sp"], self.cnt["sp"])])

    def psum_bank(self, name, dtype=F32):
        n = 512 if dtype == F32 else 1024
        t = self.es.enter_context(self.nc.psum_tensor(name, [128, n], dtype))
        return T(t, name, psum=True)

    def dram(self, name, shape, dtype, kind):
        t = self.nc.dram_tensor(name, list(shape), dtype, kind=kind)
        return T(t.ap(), name, is_dram=True)

    def tok(self, name, like=None, is_dram=False):
        return T(like.ap if like is not None else None, name, psum=(like.psum if like is not None else False),
                 is_dram=(like.is_dram if like is not None else is_dram))

    def _wait(self, e, deps, skip_self=False):
        best = {}
        for (s, v) in deps:
            kk = id(s)
            if kk not in best or best[kk][1] < v:
                best[kk] = (s, v)
        kn = self.known[e]
        for kk, (s, v) in best.items():
            if skip_self and s is self.sem[e]:
                continue
            if kn.get(kk, 0) >= v:
                continue
            self.eng[e].wait_ge(s, v)
            self.n_wait += 1
            kn[kk] = v

    def _deps(self, reads, writes):
        deps = []
        for t in reads:
            if t.w is not None:
                deps.append(t.w)
        for t in writes:
            if t.w is not None:
                deps.append(t.w)
            deps.extend(t.r)
        return deps

    def _mark(self, ev, reads, writes):
        for t in reads:
            if t.psum:
                t.w = ev
                t.r = []
            else:
                t.r.append(ev)
                if len(t.r) > 16:
                    best = {}
                    for (s, v) in t.r:
                        kk = id(s)
                        if kk not in best or best[kk][1] < v:
                            best[kk] = (s, v)
                    t.r = list(best.values())
        for t in writes:
            t.w = ev
            t.r = []

    def op(self, e, fn, reads=(), writes=()):
        self._wait(e, self._deps(reads, writes), skip_self=(e == "pe"))
        ins = fn(self.eng[e])
        self.cnt[e] += 1
        ins.then_inc(self.sem[e], 1)
        self.n_ins += 1
        self._mark((self.sem[e], self.cnt[e]), reads, writes)
        return ins

    def mm(self, out_t, groups, reads, start=True, stop=True, extra_writes=(), stop_last_only=False):
        writes = [out_t] + list(extra_writes)
        self._wait("pe", self._deps(reads, writes), skip_self=True)
        ins = None
        ng = len(groups)
        for gi_, (out_ap, pairs) in enumerate(groups):
            n = len(pairs)
            for i, (l, r) in enumerate(pairs):
                st_ = stop and i == n - 1 and (not stop_last_only or gi_ == ng - 1)
                ins = self.nc.tensor.matmul(out_ap, l, r, start=(start and i == 0), stop=st_)
                self.n_ins += 1
        self.cnt["pe"] += 1
        ins.then_inc(self.sem["pe"], 1)
        self._mark((self.sem["pe"], self.cnt["pe"]), reads, writes)

    def transposes(self, out_t, items, ident_ap, reads):
        writes = [out_t]
        self._wait("pe", self._deps(reads, writes), skip_self=True)
        ins = None
        for (o, i) in items:
            ins = self.nc.tensor.transpose(o, i, ident_ap)
            self.n_ins += 1
        self.cnt["pe"] += 1
        ins.then_inc(self.sem["pe"], 1)
        self._mark((self.sem["pe"], self.cnt["pe"]), reads, writes)

    def dma(self, q, out_t, out_ap, in_t, in_ap, is_output=False, owner=None):
        if owner is None:
            owner = out_t
            if out_t.is_dram and not in_t.is_dram:
                owner = in_t
        if owner.dsem is None:
            owner.dsem = self.es.enter_context(self.nc.semaphore("d_" + owner.name))
            self.dma_owners.append(owner)
        self._wait(q, self._deps([in_t], [out_t]))
        ins = self.eng[q].dma_start(out=out_ap, in_=in_ap)
        owner.dcnt += 16
        ins.then_inc(owner.dsem, 16)
        self.n_ins += 1
        ev = (owner.dsem, owner.dcnt)
        self._mark(ev, [in_t], [out_t])
        if is_output:
            self.out_events.append(ev)
        return ins

    def finish(self):
        evs = list(self.out_events)
        evs += [(o.dsem, o.dcnt) for o in self.dma_owners]
        evs += [(self.sem[e], self.cnt[e]) for e in ("pe", "act", "dve", "pool") if self.cnt[e] > 0]
        self._wait("sp", evs)


def bc(ap, shape):
    return ap.broadcast_to(list(shape))


CST = {}
_off = 0
for _n, _w in (("ident", 128), ("lincl", 128), ("ustrict", 128), ("uincl", 128), ("lstrict", 128),
               ("chunkind", 2), ("maskP", 128), ("maskN", 128), ("pos", 128), ("neg", 128),
               ("indge", 128), ("indle", 128), ("tp1", 128), ("t128m", 128), ("pcol", 2),
               ("c64", 16 * 64), ("s32", 16 * 32), ("mhalf", 8),
               ("l32incl", 128), ("u32incl", 128), ("u32strict", 128), ("l32strict", 128), ("m2f", 128), ("m2b", 128)):
    CST[_n] = (_off, _w)
    _off += _w
NCST = _off


def make_consts():
    c = np.zeros((128, NCST), np.float32)
    p = np.arange(128)[:, None]
    t = np.arange(128)[None, :]

    def put(n, a):
        o, w = CST[n]
        c[:, o:o + w] = a

    same = (p // 64) == (t // 64)
    put("ident", (p == t))
    put("lincl", same & (p <= t))
    put("ustrict", same & (p > t))
    put("uincl", same & (p >= t))
    put("lstrict", same & (p < t))
    same32 = (p // 32) == (t // 32)
    put("l32incl", same32 & (p <= t))
    put("u32incl", same32 & (p >= t))
    put("u32strict", same32 & (p > t))
    put("l32strict", same32 & (p < t))
    put("m2f", same & ((p % 64) < 32) & ((t % 64) >= 32))
    put("m2b", same & ((p % 64) >= 32) & ((t % 64) < 32))
    put("chunkind", np.stack([(np.arange(128) < 64), (np.arange(128) >= 64)], axis=1))
    put("maskP", (p >= t))
    put("maskN", (t >= p))
    put("pos", np.maximum(t - p, 0))
    put("neg", np.maximum(p - t, 0))
    put("indge", (t - p >= 0))
    put("indle", (t - p <= 0))
    put("tp1", np.broadcast_to(t + 1, (128, 128)))
    put("t128m", np.broadcast_to(128 - t, (128, 128)))
    put("pcol", np.stack([127 - np.arange(128), np.arange(128)], axis=1))
    put("mhalf", np.full((128, 8), -0.5))
    inv = (np.float32(10000.0) ** (-np.arange(16, dtype=np.float32) / np.float32(16))).astype(np.float32)
    tok = (np.arange(16)[None, :] * 128 + np.arange(128)[:, None]).astype(np.int64)
    row = (tok // 64).astype(np.float32)
    col = (tok % 64).astype(np.float32)
    ang = np.stack([row[..., None] * inv, col[..., None] * inv], axis=2).astype(np.float32)
    cos = np.cos(ang).astype(np.float32)
    sin = np.sin(ang).astype(np.float32)
    c64 = np.broadcast_to(cos[:, :, :, None, :], (128, 16, 2, 2, 16)).reshape(128, 16 * 64)
    put("c64", c64)
    put("s32", sin.reshape(128, 16 * 32))
    return c


def build(NB=4, LAYERS=2, dbg=False, stages="SARGWHCD"):
    nc = bass.Bass("TRN2", target_bir_lowering=False)
    k = KB(nc)
    IN, INT, OUT = "ExternalInput", "Internal", "ExternalOutput"

    x_d = k.dram("x", [NB, SEQ, D], F32, IN)
    ctx_d = k.dram("ctx", [NB, LC, D], F32, IN)
    cT_d = k.dram("cT", [128, 8 * 5], F32, IN)
    cst_d = k.dram("cst", [128, NCST], F32, IN)
    wada_d = k.dram("wada", [2, 128, 8 * 6144], F32, IN)
    bada_d = k.dram("bada", [2, 1, 6144], F32, IN)
    win_d = k.dram("win", [2, 128, 8 * NIN], F32, IN)
    wout_d = k.dram("wout", [2, 128, 8 * D], F32, IN)
    wup_d = k.dram("wup", [2, 8, 128, 8 * 512], F32, IN)
    wdn_d = k.dram("wdn", [2, 8, 128, 4 * D], F32, IN)
    rdl_d = k.dram("rdl", [2, 1, 8], F32, IN)
    gain_d = k.dram("gain", [2, 1, 384], F32, IN)
    sink_d = k.dram("sink", [2, 1, 4], F32, IN)
    hlb_d = k.dram("hlb", [2, 1, 256], F32, IN)
    lnp_d = k.dram("lnp", [2, 4, 1, D], F32, IN)
    out_d = k.dram("out", [NB, SEQ, D], F32, OUT)
    dbg_d = k.dram("dbg", [128, 8 * TT], F32, OUT) if dbg else None
    dbg2_d = k.dram("dbg2", [128, 8192], F32, OUT) if dbg else None
    dd = {"off": 0, "names": []}

    def dump(name, t, ap, n):
        if not dbg:
            return
        k.dma("sp", dbg2_d, dbg2_d[:, dd["off"]:dd["off"] + n], t, ap, is_output=True)
        dd["names"].append((name, dd["off"], n))
        dd["off"] += n
    k.dd = dd

    wada_b = [k.dram("wada_b%d" % l, [128, 8 * 6144], BF16, INT) for l in range(2)]
    win_b = [k.dram("win_b%d" % l, [128, 8 * NIN], BF16, INT) for l in range(2)]
    wout_b = [k.dram("wout_b%d" % l, [128, 8 * D], BF16, INT) for l in range(2)]
    wup_b = [k.dram("wup_b%d" % l, [8, 128, 8 * 512], BF16, INT) for l in range(2)]
    wdn_b = [k.dram("wdn_b%d" % l, [8, 128, 4 * D], BF16, INT) for l in range(2)]
    modrows = [k.dram("modrows%d" % l, [5, 6144], F32, INT) for l in range(2)]
    xs_d = k.dram("xs", [NB, TT, D], F32, INT)
    xm_d = k.dram("xm", [NB, TT, D], F32, INT)
    xs_tok = [[k.tok("xs%d_%d" % (j, t), is_dram=True) for t in range(NT)] for j in range(NB)]
    xm_tok = [[k.tok("xm%d_%d" % (j, t), is_dram=True) for t in range(NT)] for j in range(NB)]

    cst = k.sbuf("cst_sb", [128, NCST])
    ident = k.sbuf("ident_bf", [128, 128], BF16)
    HT_OFF = k.top
    hT = k.sbuf("hT", [128, 8, TT], BF16)
    ycT = k.sbuf("ycT", [128, 8, TT], BF16)
    hT_tok = [k.tok("hT%d" % t) for t in range(NT)]
    ycT_tok = [[k.tok("ycT%d_%d" % (m, t)) for t in range(NT)] for m in range(4)]
    gaintab = k.sbuf("gaintab", [128, 384])
    rdl = k.sbuf("rdl_sb", [128, 8])
    lgt = k.sbuf("lgt", [128, 8])
    lgcol = k.sbuf("lgcol", [128, 4])
    a128c = k.sbuf("a128c", [128, 4])
    dk = k.sbuf("dk", [128, 8])
    dq = k.sbuf("dq", [128, 4, 128])
    mT = k.sbuf("mT", [128, 4, 128])
    sinkE = k.sbuf("sinkE", [128, 4])
    hA = k.sbuf("hA", [128, 256])
    hB = k.sbuf("hB", [128, 256])
    small = k.sbuf("small", [128, 64])
    small2 = k.sbuf("small2", [128, 64])

    P = [k.psum_bank("ps%d" % i) for i in range(6)]
    PT = [k.psum_bank("pst%d" % i, BF16) for i in range(2)]
    prot = [0]
    ptrot = [0]

    def nb():
        prot[0] = (prot[0] + 1) % 6
        return P[prot[0]]

    def nbt():
        ptrot[0] = (ptrot[0] + 1) % 2
        return PT[ptrot[0]]

    def C(name, lo=0, hi=None):
        o, w = CST[name]
        return cst[:, o + lo:o + (w if hi is None else hi)]

    k.dma("sp", cst, cst[:], cst_d, cst_d[:])
    k.op("dve", lambda e: e.tensor_copy(ident[:], C("ident")), reads=[cst], writes=[ident])

    def cast_dram(dst, dst_ap2d, src, src_ap2d, n):
        step = 8192
        for c0 in range(0, n, step):
            c1 = min(n, c0 + step)
            k.dma("pool", dst, dst_ap2d[:, c0:c1], src, src_ap2d[:, c0:c1])

    for l in range(LAYERS):
        cast_dram(wada_b[l], wada_b[l][:, :], wada_d, wada_d[l], 8 * 6144)
        cast_dram(win_b[l], win_b[l][:, :], win_d, win_d[l], 8 * NIN)
        cast_dram(wout_b[l], wout_b[l][:, :], wout_d, wout_d[l], 8 * D)
        for c in range(8):
            cast_dram(wup_b[l], wup_b[l][c], wup_d, wup_d[l, c], 8 * 512)
        for c in range(8):
            cast_dram(wdn_b[l], wdn_b[l][c], wdn_d, wdn_d[l, c], 4 * D)

    def layer_norm_stats(xt, xt_tok, st, st_tok):
        k.op("dve", lambda e: e.bn_stats(st[:, 8:14], xt[:, 0:512]), reads=[xt_tok], writes=[st_tok])
        k.op("dve", lambda e: e.bn_stats(st[:, 14:20], xt[:, 512:1024]), reads=[xt_tok], writes=[st_tok])
        k.op("dve", lambda e: e.bn_aggr(st[:, 0:2], st[:, 8:20]), reads=[st_tok], writes=[st_tok])
        k.op("dve", lambda e: e.tensor_scalar(st[:, 3:4], st[:, 1:2], EPS, None, ALU.add), reads=[st_tok], writes=[st_tok])
        k.op("pool", lambda e: e.tensor_tensor(st[:, 1:2], st[:, 3:4], C("mhalf", 0, 1), ALU.pow),
             reads=[st_tok, cst], writes=[st_tok])
        k.op("dve", lambda e: e.scalar_tensor_tensor(st[:, 2:3], st[:, 0:1], -1.0, st[:, 1:2], ALU.mult, ALU.mult),
             reads=[st_tok], writes=[st_tok])

    cT = k.sbuf("cT_sb", [128, 40])
    cTb = k.sbuf("cT_bf", [128, 40], BF16)
    M0 = k.top
    slots = [k.sbuf("tabslot%d" % i, [128, D]) for i in range(5)]
    xbuf = [k.sbuf("xbuf%d" % i, [128, D]) for i in range(2)]
    wkA = [k.sbuf("wkA%d" % i, [128, D]) for i in range(2)]
    wkB = [k.sbuf("wkB%d" % i, [128, D], BF16) for i in range(2)]
    stt = [k.sbuf("stt%d" % i, [128, 24]) for i in range(2)]
    fr_x = [k.sbuf("fr_x%d" % i, [128, 512]) for i in range(2)]
    modsb, badasb = fr_x[0], fr_x[1]
    M1 = k.top
    wout_sb = k.sbuf("wout_sb", [128, 8, D], BF16)
    xbuf3 = xbuf + [k.sbuf("xbuf2", [128, D])]
    wkA3 = wkA + [k.sbuf("wkA2", [128, D])]
    wkB3 = wkB + [k.sbuf("wkB2", [128, D], BF16)]
    stt3 = stt + [k.sbuf("stt2", [128, 24])]
    k.top = M1
    hidT = k.sbuf("hidT", [128, 32, 512], BF16)
    wup_sb = [k.sbuf("wup_sb%d" % i, [128, 8, 512], BF16) for i in range(2)]
    wdn_sb = [k.sbuf("wdn_sb%d" % i, [128, 4, D], BF16) for i in range(2)]
    print("SBUF set X top", k.top)
    tabs = {}
    k.dma("sp", cT, cT[:], cT_d, cT_d[:])
    k.op("act", lambda e: e.activation(small[:, 0:40], cT[:], AF.Exp, scale=-1.0), reads=[cT], writes=[small])
    k.op("dve", lambda e: e.tensor_scalar(small[:, 0:40], small[:, 0:40], 1.0, None, ALU.add), reads=[small], writes=[small])
    k.op("dve", lambda e: e.reciprocal(small[:, 0:40], small[:, 0:40]), reads=[small], writes=[small])
    k.op("dve", lambda e: e.tensor_tensor(cTb[:], small[:, 0:40], cT[:], ALU.mult), reads=[small, cT], writes=[cTb])

    rot = {"x": 0, "a": 0, "b": 0, "s": 0}

    def nxt(lst, key):
        rot[key] = (rot[key] + 1) % len(lst)
        return lst[rot[key]]

    def modulate_to_hT(xt, sc, sh, tt):
        st = nxt(stt, "s")
        layer_norm_stats(xt, xt, st, st)
        wa = nxt(wkA, "a")
        wb = nxt(wkB, "b")
        k.op("act", lambda e: e.activation(wa[:], xt[:], AF.Identity, bias=st[:, 2:3], scale=st[:, 1:2]),
             reads=[xt, st], writes=[wa])
        k.op("pool", lambda e: e.tensor_tensor(wa[:], wa[:], sc[:], ALU.mult), reads=[wa, sc], writes=[wa])
        k.op("dve", lambda e: e.tensor_tensor(wb[:], wa[:], sh[:], ALU.add), reads=[wa, sh], writes=[wb])
        pt = nbt()
        k.transposes(pt, [(pt[:, kc * 128:(kc + 1) * 128], wb[:, kc * 128:(kc + 1) * 128]) for kc in range(8)],
                     ident[:], reads=[wb, ident])
        k.op("act", lambda e: e.activation(hT[:, :, tt * 128:(tt + 1) * 128],
                                           pt[:, :].rearrange("p (c t) -> p c t", c=8), AF.Copy),
             reads=[pt], writes=[hT_tok[tt]])

    def ln_stats_gen(xt, st):
        k.op("dve", lambda e: e.bn_stats(st[:, 8:14], xt[:, 0:512]), reads=[xt], writes=[st])
        k.op("dve", lambda e: e.bn_stats(st[:, 14:20], xt[:, 512:1024]), reads=[xt], writes=[st])
        k.op("dve", lambda e: e.bn_aggr(st[:, 0:2], st[:, 8:20]), reads=[st], writes=[st])
        k.op("dve", lambda e: e.tensor_scalar(st[:, 3:4], st[:, 1:2], EPS, None, ALU.add), reads=[st], writes=[st])
        yield
        k.op("pool", lambda e: e.tensor_tensor(st[:, 1:2], st[:, 3:4], C("mhalf", 0, 1), ALU.pow), reads=[st, cst], writes=[st])
        yield
        k.op("dve", lambda e: e.scalar_tensor_tensor(st[:, 2:3], st[:, 0:1], -1.0, st[:, 1:2], ALU.mult, ALU.mult), reads=[st], writes=[st])

    def modulate_gen(xt, sc, sh, tt, par, pt, mul_eng="pool"):
        st, wa, wb = stt3[par], wkA3[par], wkB3[par]
        yield from ln_stats_gen(xt, st)
        yield
        k.op("act", lambda e: e.activation(wa[:], xt[:], AF.Identity, bias=st[:, 2:3], scale=st[:, 1:2]), reads=[xt, st], writes=[wa])
        yield
        k.op(mul_eng, lambda e: e.tensor_tensor(wa[:], wa[:], sc[:], ALU.mult), reads=[wa, sc], writes=[wa])
        yield
        k.op("dve", lambda e: e.tensor_tensor(wb[:], wa[:], sh[:], ALU.add), reads=[wa, sh], writes=[wb])
        yield
        k.transposes(pt, [(pt[:, kc * 128:(kc + 1) * 128], wb[:, kc * 128:(kc + 1) * 128]) for kc in range(8)], ident[:], reads=[wb, ident])
        yield
        k.op("act", lambda e: e.activation(hT[:, :, tt * 128:(tt + 1) * 128], pt[:, :].rearrange("p (c t) -> p c t", c=8), AF.Copy),
             reads=[pt], writes=[hT_tok[tt]])

    def load_tab(name, slot, l, row, col0):
        t = slots[slot]
        tabs[name] = t
        k.dma("sp", t, t[:], modrows[l], modrows[l][row:row + 1, col0:col0 + D].partition_broadcast(128))

    def load_ln(name, slot, l, i):
        t = slots[slot]
        tabs[name] = t
        k.dma("sp", t, t[:], lnp_d, lnp_d[l, i].partition_broadcast(128))

    wada_sb = wup_sb

    def layer_setup(l):
        for ct in range(12):
            wsb = wada_sb[ct % 2]
            k.dma("sp", badasb, badasb[0:5, :], bada_d, bada_d[l][:, ct * 512:(ct + 1) * 512].partition_broadcast(5))
            k.dma("sp", wsb, wsb[:], wada_b[l],
                  wada_b[l][:, :].rearrange("p (c n) -> p c n", c=8)[:, :, ct * 512:(ct + 1) * 512])
            pb = nb()
            k.mm(pb, [(pb[0:5, 0:512], [(cTb[:, kc * 5:(kc + 1) * 5], wsb[:, kc, :]) for kc in range(8)])],
                 reads=[cTb, wsb])
            k.op("dve", lambda e: e.tensor_tensor(modsb[0:5, :], pb[0:5, 0:512], badasb[0:5, :], ALU.add),
                 reads=[pb, badasb], writes=[modsb])
            if ct in (2, 3, 8, 9):
                k.op("dve", lambda e: e.tensor_scalar(modsb[0:5, :], modsb[0:5, :], 1.0, None, ALU.add), reads=[modsb], writes=[modsb])
            k.dma("sp", modrows[l], modrows[l][:, ct * 512:(ct + 1) * 512], modsb, modsb[0:5, :])
        k.dma("sp", gaintab, gaintab[:], gain_d, gain_d[l].partition_broadcast(128))
        k.dma("sp", rdl, rdl[:], rdl_d, rdl_d[l].partition_broadcast(128))
        k.op("act", lambda e: e.activation(small[:, 0:8], rdl[:], AF.Exp, scale=-1.0), reads=[rdl], writes=[small])
        k.op("dve", lambda e: e.tensor_scalar(small[:, 0:8], small[:, 0:8], 1.0, None, ALU.add), reads=[small], writes=[small])
        k.op("act", lambda e: e.activation(small[:, 8:16], small[:, 0:8], AF.Ln), reads=[small], writes=[small])
        k.op("dve", lambda e: e.tensor_scalar(lgt[:], small[:, 8:16], -1.0, None, ALU.mult), reads=[small], writes=[lgt])
        for d_ in range(2):
            for g in range(2):
                j = d_ * 2 + g
                k.op("dve", lambda e: e.tensor_copy(lgcol[0:64, j:j + 1], lgt[0:64, d_ * 4 + 2 * g:d_ * 4 + 2 * g + 1]),
                     reads=[lgt], writes=[lgcol])
                k.op("dve", lambda e: e.tensor_copy(lgcol[64:128, j:j + 1], lgt[64:128, d_ * 4 + 2 * g + 1:d_ * 4 + 2 * g + 2]),
                     reads=[lgt], writes=[lgcol])
        k.op("act", lambda e: e.activation(a128c[:], lgcol[:], AF.Exp, scale=128.0), reads=[lgcol], writes=[a128c])
        for d_ in range(2):
            k.op("dve", lambda e: e.tensor_scalar(small[:, 16 + 4 * d_:20 + 4 * d_], lgt[:, 4 * d_:4 * d_ + 4],
                                                  C("pcol", d_, d_ + 1), None, ALU.mult),
                 reads=[lgt, cst], writes=[small])
        k.op("act", lambda e: e.activation(small[:, 24:32], small[:, 16:24], AF.Exp), reads=[small], writes=[small])
        k.op("dve", lambda e: e.tensor_scalar(dk[:], small[:, 24:32], 0.125, None, ALU.mult), reads=[small], writes=[dk])
        for d_ in range(2):
            for g in range(2):
                j = d_ * 2 + g
                src = C("tp1") if d_ == 0 else C("t128m")
                k.op("act", lambda e: e.activation(dq[:, j, :], src, AF.Exp, scale=lgcol[:, j:j + 1]),
                     reads=[cst, lgcol], writes=[dq])
        for h in range(4):
            k.op("act", lambda e: e.activation(hA[:, 0:128], C("pos"), AF.Exp, scale=lgt[:, h:h + 1]),
                 reads=[cst, lgt], writes=[hA])
            k.op("dve", lambda e: e.tensor_tensor(hA[:, 0:128], hA[:, 0:128], C("indge"), ALU.mult), reads=[hA, cst], writes=[hA])
            k.op("act", lambda e: e.activation(hA[:, 128:256], C("neg"), AF.Exp, scale=lgt[:, 4 + h:5 + h]),
                 reads=[cst, lgt], writes=[hA])
            k.op("dve", lambda e: e.tensor_tensor(hA[:, 128:256], hA[:, 128:256], C("indle"), ALU.mult), reads=[hA, cst], writes=[hA])
            k.op("dve", lambda e: e.tensor_tensor(hA[:, 0:128], hA[:, 0:128], hA[:, 128:256], ALU.add), reads=[hA], writes=[hA])
            k.op("dve", lambda e: e.tensor_scalar(mT[:, h, :], hA[:, 0:128], 0.125, None, ALU.mult), reads=[hA], writes=[mT])
        k.dma("sp", small2, small2[:, 0:4], sink_d, sink_d[l].partition_broadcast(128))
        k.op("act", lambda e: e.activation(sinkE[:], small2[:, 0:4], AF.Exp, bias=-SM_SHIFT), reads=[small2], writes=[sinkE])
        if l == 0:
            k.op("dve", lambda e: e.memset(hA[:], 0.0), writes=[hA])
            k.op("dve", lambda e: e.memset(hB[:], 1.0), writes=[hB])
        else:
            k.dma("sp", hA, hA[:], hlb_d, hlb_d[1].partition_broadcast(128))
            k.dma("sp", hB, hB[:], hlb_d, hlb_d[0].partition_broadcast(128))
            k.op("dve", lambda e: e.tensor_tensor(hB[:], hB[:], hA[:], ALU.subtract), reads=[hA, hB], writes=[hB])
            k.op("act", lambda e: e.activation(hB[:], hB[:], AF.Exp), reads=[hB], writes=[hB])
            k.op("dve", lambda e: e.tensor_scalar(hB[:], hB[:], 1.0, None, ALU.add), reads=[hB], writes=[hB])
            k.op("dve", lambda e: e.reciprocal(hA[:], hB[:]), reads=[hB], writes=[hA])
            k.op("dve", lambda e: e.tensor_scalar(hB[:], hA[:], -1.0, 1.0, ALU.mult, ALU.add), reads=[hA], writes=[hB])

    qkT_tok = [k.tok("qkT%d" % t) for t in range(NT)]
    tm_tok = [k.tok("tm%d" % t) for t in range(NT)]

    def alloc_set(pfx, base, spec):
        k.top = base
        S = {}
        for (key, shape, dtype, cnt) in spec:
            if cnt == 0:
                S[key] = k.sbuf(pfx + key, shape, dtype)
            else:
                S[key] = [k.sbuf(pfx + key + str(i), shape, dtype) for i in range(cnt)]
        return S

    fspec = lambda names: [(n, [128, 512], F32, 0) for n in names]
    small_spec = [("b2", [128, 512], BF16, 2), ("b3", [128, 512], BF16, 2), ("yc", [128, 256], BF16, 2),
                  ("sfst", [128, 2, 64], F32, 0), ("sfbf", [128, 2, 64], BF16, 2)]
    tm_spec = [(n, [128, NT, 256], BF16, 0) for n in ("tmA", "tmB", "tmC", "tmD")]
    SET_RET = alloc_set("r_", M0, [("qkT", [128, 8, TT], BF16, 0)] + tm_spec + [("sball", [128, 18, 2, 64], BF16, 0),
                        ("win_sb", [128, 8, 1024], BF16, 0), ("f1", [128, 512], F32, 2)] + fspec(["f2", "f3", "f4", "f5"]) + small_spec)
    print("SBUF set RET top", k.top)
    _save_top = k.top
    k.top = k.nc.lookup_mloc(SET_RET["sball"].ap).addr
    SET_RET["f3b"] = k.sbuf("r_f3b", [128, 512])
    SET_RET["f4b"] = k.sbuf("r_f4b", [128, 512])
    assert k.top <= k.nc.lookup_mloc(SET_RET["win_sb"].ap).addr, "ret overlay overflow"
    k.top = k.nc.lookup_mloc(SET_RET["f4"].ap).addr
    SET_RET["sfbf2"] = [k.sbuf("r_sfbf_x%d" % i, [128, 2, 64], BF16) for i in range(2)]
    k.top = _save_top
    SET_ATT = alloc_set("a_", M0, [("qkT", [128, 2, TT], BF16, 0), ("kz", [128, 2, TT], BF16, 0), ("vaug2", [128, NT, 2, 2, 128], BF16, 0),
                        ("win_sb", [128, 8, 512], BF16, 0), ("fa", [128, 512], F32, 3), ("f2", [128, 512], F32, 3),
                        ("f3", [128, 512], F32, 3), ("f4", [128, 512], F32, 3), ("qb", [128, 512], BF16, 3),
                        ("pT", [128, 512], BF16, 8), ("rec", [128, 512], F32, 4), ("rec2", [128, 512], F32, 4),
                        ("sm", [128, 16], F32, 3), ("msk", [128, 6, 512], BF16, 0)])
    print("SBUF set ATT top", k.top)
    SET_HG = alloc_set("h_", M0, [("win_sb", [128, 8, 1280], BF16, 0), ("thz", [128, 512], F32, 2), ("sq", [128, 256], F32, 2),
                       ("tmp", [128, 256], F32, 2), ("key", [128, 512], F32, 2), ("lg", [128, 512], F32, 2), ("sqq", [128, 256], F32, 2),
                       ("fo", [128, 256], F32, 2), ("fe", [128, 256], F32, 20), ("b1", [128, 1024], BF16, 4), ("kh", [128, 256], BF16, 4),
                       ("vt", [128, 256], BF16, 4), ("sgt", [128, 256], BF16, 2), ("fm", [128, 16, 128], BF16, 2),
                       ("am", [128, 1024], BF16, 2), ("yc", [128, 256], BF16, 2), ("sfst", [128, 2, 64], F32, 0),
                       ("sfbf", [128, 2, 64], BF16, 3), ("sball", [128, 36, 2, 64], BF16, 0), ("aTsb", [128, 16], F32, 4),
                       ("sm", [128, 16], F32, 2)])
    print("SBUF set HG top", k.top)
    qkT = tmA = tmB = tmC = tmD = vaug = sball = win_sb = f1 = f2 = f3 = f4 = f5 = f6 = f7 = None
    b1 = b2 = b3 = yc = sfst = sfbf = aTsb = None

    def use_set(S):
        nonlocal qkT, tmA, tmB, tmC, tmD, vaug, sball, win_sb, f1, f2, f3, f4, f5, f6, f7, b1, b2, b3, yc, sfst, sfbf, aTsb
        qkT = S.get("qkT"); tmA = S.get("tmA"); tmB = S.get("tmB"); tmC = S.get("tmC"); tmD = S.get("tmD")
        vaug = S.get("vaug"); sball = S.get("sball"); win_sb = S.get("win_sb")
        f1 = S.get("f1"); f2 = S.get("f2"); f3 = S.get("f3"); f4 = S.get("f4"); f5 = S.get("f5"); f6 = S.get("f6"); f7 = S.get("f7")
        b1 = S.get("b1"); b2 = S.get("b2"); b3 = S.get("b3"); yc = S.get("yc")
        sfst = S.get("sfst"); sfbf = S.get("sfbf"); aTsb = S.get("aTsb")
    rr = {"f1": 0, "b1": 0, "b2": 0, "b3": 0, "yc": 0, "sf": 0}

    def rn(lst, key):
        rr[key] = (rr[key] + 1) % len(lst)
        return lst[rr[key]]

    MIX_COLS = {"ret": (0, 1024), "gqa": (1024, 512), "swa": (1536, 512), "hgrn": (2048, 1280)}

    def load_win(l, mixer):
        c0, n = MIX_COLS[mixer]
        k.dma("sp", win_sb, win_sb[:, :, 0:n], win_b[l],
              win_b[l][:, :].rearrange("p (c n) -> p c n", c=8)[:, :, c0:c0 + n])

    def project(tt, n):
        res = []
        for c0 in range(0, n, 512):
            w = min(512, n - c0)
            pb = nb()
            k.mm(pb, [(pb[:, 0:w], [(hT[:, kc, tt * 128:(tt + 1) * 128], win_sb[:, kc, c0:c0 + w]) for kc in range(8)])],
                 reads=[hT_tok[tt], win_sb])
            res.append((pb, w))
        return res

    def rope(dst_bf, src, nh, tt, tmp1, tmp2):
        n = nh * 64
        if tt < 2:
            k.op("pool", lambda e: e.tensor_copy(dst_bf[:, 0:n], src[:, 0:n]), reads=[src], writes=[dst_bf])
            return
        li = tt - 2
        c64 = C("c64", li * 64, (li + 1) * 64)
        s32 = C("s32", li * 32, (li + 1) * 32)
        xv = src[:, 0:n].rearrange("p (h a f e) -> p h a f e", h=nh, a=2, f=2, e=16)
        t1 = tmp1[:, 0:n]
        k.op("dve", lambda e: e.tensor_tensor(t1.rearrange("p (h c) -> p h c", h=nh),
                                              src[:, 0:n].rearrange("p (h c) -> p h c", h=nh),
                                              bc(c64.unsqueeze(1), [128, nh, 64]), ALU.mult),
             reads=[src, cst], writes=[tmp1])
        sv = bc(s32.rearrange("p (a e) -> p a e", a=2).unsqueeze(1), [128, nh, 2, 16])
        u = tmp2[:, 0:n // 2].rearrange("p (h a e) -> p h a e", h=nh, a=2, e=16)
        w_ = tmp2[:, n // 2:n].rearrange("p (h a e) -> p h a e", h=nh, a=2, e=16)
        k.op("pool", lambda e: e.tensor_tensor(u, xv[:, :, :, 1, :], sv, ALU.mult), reads=[src, cst], writes=[tmp2])
        k.op("dve", lambda e: e.tensor_tensor(w_, xv[:, :, :, 0, :], sv, ALU.mult), reads=[src, cst], writes=[tmp2])
        t1v = t1.rearrange("p (h a f e) -> p h a f e", h=nh, a=2, f=2, e=16)
        dv = dst_bf[:, 0:n].rearrange("p (h a f e) -> p h a f e", h=nh, a=2, f=2, e=16)
        k.op("dve", lambda e: e.tensor_tensor(dv[:, :, :, 0, :], t1v[:, :, :, 0, :], u, ALU.subtract),
             reads=[tmp1, tmp2], writes=[dst_bf])
        k.op("dve", lambda e: e.tensor_tensor(dv[:, :, :, 1, :], t1v[:, :, :, 1, :], w_, ALU.add),
             reads=[tmp1, tmp2], writes=[dst_bf])

    def to_fm(src_bf, nblk, tt, slot0, reads):
        pt = nbt()
        k.transposes(pt, [(pt[:, i * 128:(i + 1) * 128], src_bf[:, i * 128:(i + 1) * 128]) for i in range(nblk)],
                     ident[:], reads=reads + [ident])
        k.op("act", lambda e: e.activation(qkT[:, slot0:slot0 + nblk, tt * 128:(tt + 1) * 128],
                                           pt[:, 0:nblk * 128].rearrange("p (c t) -> p c t", c=nblk), AF.Copy),
             reads=[pt], writes=[qkT_tok[tt]])

    def emit_y(ycb, m, tt, pt=None):
        if pt is None:
            pt = nbt()
        k.transposes(pt, [(pt[:, i * 128:(i + 1) * 128], ycb[:, i * 128:(i + 1) * 128]) for i in range(2)],
                     ident[:], reads=[ycb, ident])
        k.op("act", lambda e: e.activation(ycT[:, 2 * m:2 * m + 2, tt * 128:(tt + 1) * 128],
                                           pt[:, 0:256].rearrange("p (c t) -> p c t", c=2), AF.Copy),
             reads=[pt], writes=[ycT_tok[m][tt]])

    def silu_to(dst_bf_ap, dst_tok, src_ap, src_tok, tmp, n):
        k.op("act", lambda e: e.activation(tmp[:, 0:n], src_ap, AF.Exp, scale=-1.0), reads=[src_tok], writes=[tmp])
        k.op("dve", lambda e: e.tensor_scalar(tmp[:, 0:n], tmp[:, 0:n], 1.0, None, ALU.add), reads=[tmp], writes=[tmp])
        k.op("dve", lambda e: e.reciprocal(tmp[:, 0:n], tmp[:, 0:n]), reads=[tmp], writes=[tmp])
        k.op("dve", lambda e: e.tensor_tensor(dst_bf_ap, tmp[:, 0:n], src_ap, ALU.mult), reads=[tmp, src_tok], writes=[dst_tok])

    def interleave(gens, depth=2):
        active = []
        it_ = iter(gens)
        while True:
            if len(active) < depth:
                g_ = next(it_, None)
                if g_ is not None:
                    active.append(g_)
            if not active:
                break
            for g_ in list(active):
                try:
                    next(g_)
                except StopIteration:
                    active.remove(g_)

    class Rot:
        def __init__(self, lst):
            self.lst = list(lst)
            self.i = -1

        def __call__(self):
            self.i = (self.i + 1) % len(self.lst)
            return self.lst[self.i]

    def attn_mixer(l, which):
        m = 1 if which == "gqa" else 2
        k.barrier()
        S = SET_ATT
        qkT_, vaug2, win = S["qkT"], S["vaug2"], S["win_sb"]
        k.op("pool", lambda e: e.memset(vaug2[:], 1.0), writes=[vaug2] + tm_tok)
        kz = S["kz"]
        k.op("pool", lambda e: e.memset(kz[:], 0.0), writes=[kz] + qkT_tok)
        c0, n = MIX_COLS[which]
        k.dma("sp", win, win[:, :, 0:n], win_b[l], win_b[l][:, :].rearrange("p (c n) -> p c n", c=8)[:, :, c0:c0 + n])
        msk = S["msk"]
        if which == "swa":
            k.op("pool", lambda e: e.memset(msk[:], 0.0), writes=[msk])
            for r in range(-1, 5):
                for b in range(4):
                    dlt = r - b
                    if dlt not in (-1, 0, 1):
                        continue
                    src = {-1: C("maskP"), 0: None, 1: C("maskN")}[dlt]
                    if src is None:
                        k.op("pool", lambda e: e.memset(msk[:, r + 1, b * 128:(b + 1) * 128], 1.0), writes=[msk])
                    else:
                        k.op("dve", lambda e: e.tensor_copy(msk[:, r + 1, b * 128:(b + 1) * 128], src), reads=[cst], writes=[msk])
        PB = Rot([P[0], P[1], P[2]])

        def prep(tt, par):
            pb = PB()
            k.mm(pb, [(pb[:, 0:512], [(hT[:, kc, tt * 128:(tt + 1) * 128], win[:, kc, 0:512]) for kc in range(8)])],
                 reads=[hT_tok[tt], win])
            fa, f2_, f3_, f4_, qb, sm = S["fa"][par], S["f2"][par], S["f3"][par], S["f4"][par], S["qb"][par], S["sm"][par]
            yield
            k.op("act", lambda e: e.activation(fa[:, 0:384], pb[:, 0:384], AF.Copy), reads=[pb], writes=[fa])
            for kv in range(2):
                k.op("act", lambda e: e.activation(vaug2[:, tt, kv, 0, 0:64], pb[:, 384 + kv * 64:448 + kv * 64], AF.Copy),
                     reads=[pb], writes=[tm_tok[tt]])
                k.op("dve", lambda e: e.tensor_copy(vaug2[:, tt, kv, 1, 64:128], pb[:, 384 + kv * 64:448 + kv * 64]),
                     reads=[pb], writes=[tm_tok[tt]])
            yield
            if which == "gqa":
                k.op("dve", lambda e: e.tensor_tensor(f2_[:, 0:384], fa[:, 0:384], fa[:, 0:384], ALU.mult), reads=[fa], writes=[f2_])
                k.op("dve", lambda e: e.tensor_reduce(sm[:, 0:6], f2_[:, 0:384].rearrange("p (h d) -> p h d", h=6), AX.X, ALU.add),
                     reads=[f2_], writes=[sm])
                k.op("dve", lambda e: e.tensor_scalar(sm[:, 0:6], sm[:, 0:6], 1.0 / 64, EPS, ALU.mult, ALU.add), reads=[sm], writes=[sm])
                yield
                k.op("pool", lambda e: e.tensor_tensor(sm[:, 8:14], sm[:, 0:6], C("mhalf", 0, 6), ALU.pow), reads=[sm, cst], writes=[sm])
                yield
                k.op("dve", lambda e: e.tensor_tensor(fa[:, 0:384].rearrange("p (h d) -> p h d", h=6),
                                                      fa[:, 0:384].rearrange("p (h d) -> p h d", h=6),
                                                      bc(sm[:, 8:14].unsqueeze(2), [128, 6, 64]), ALU.mult), reads=[fa, sm], writes=[fa])
                k.op("dve", lambda e: e.tensor_tensor(fa[:, 0:384], fa[:, 0:384], gaintab[:], ALU.mult), reads=[fa, gaintab], writes=[fa])
                yield
            if tt < 2:
                k.op("dve", lambda e: e.tensor_copy(qb[:, 0:384], fa[:, 0:384]), reads=[fa], writes=[qb])
            else:
                li = tt - 2
                c64 = C("c64", li * 64, (li + 1) * 64)
                s32 = C("s32", li * 32, (li + 1) * 32)
                nh = 6
                nn = 384
                xv = fa[:, 0:nn].rearrange("p (h a f e) -> p h a f e", h=nh, a=2, f=2, e=16)
                t1 = f3_[:, 0:nn]
                k.op("dve", lambda e: e.tensor_tensor(t1.rearrange("p (h c) -> p h c", h=nh), fa[:, 0:nn].rearrange("p (h c) -> p h c", h=nh),
                                                      bc(c64.unsqueeze(1), [128, nh, 64]), ALU.mult), reads=[fa, cst], writes=[f3_])
                sv = bc(s32.rearrange("p (a e) -> p a e", a=2).unsqueeze(1), [128, nh, 2, 16])
                u = f4_[:, 0:nn // 2].rearrange("p (h a e) -> p h a e", h=nh, a=2, e=16)
                w_ = f4_[:, nn // 2:nn].rearrange("p (h a e) -> p h a e", h=nh, a=2, e=16)
                k.op("pool", lambda e: e.tensor_tensor(u, xv[:, :, :, 1, :], sv, ALU.mult), reads=[fa, cst], writes=[f4_])
                k.op("dve", lambda e: e.tensor_tensor(w_, xv[:, :, :, 0, :], sv, ALU.mult), reads=[fa, cst], writes=[f4_])
                yield
                t1v = t1.rearrange("p (h a f e) -> p h a f e", h=nh, a=2, f=2, e=16)
                dv = qb[:, 0:nn].rearrange("p (h a f e) -> p h a f e", h=nh, a=2, f=2, e=16)
                k.op("dve", lambda e: e.tensor_tensor(dv[:, :, :, 0, :], t1v[:, :, :, 0, :], u, ALU.subtract), reads=[f3_, f4_], writes=[qb])
                k.op("dve", lambda e: e.tensor_tensor(dv[:, :, :, 1, :], t1v[:, :, :, 1, :], w_, ALU.add), reads=[f3_, f4_], writes=[qb])
            yield
            pt = nbt()
            k.transposes(pt, [(pt[:, i * 128:(i + 1) * 128], qb[:, i * 128:(i + 1) * 128]) for i in range(3)], ident[:], reads=[qb, ident])
            yield
            k.op("act", lambda e: e.activation(qkT_[:, 0:2, tt * 128:(tt + 1) * 128],
                                               pt[:, 0:256].rearrange("p (c t) -> p c t", c=2), AF.Copy), reads=[pt], writes=[qkT_tok[tt]])
            k.op("dve", lambda e: e.tensor_copy(kz[0:64, 0, tt * 128:(tt + 1) * 128], pt[0:64, 256:384]), reads=[pt], writes=[qkT_tok[tt]])
            k.op("act", lambda e: e.activation(kz[64:128, 1, tt * 128:(tt + 1) * 128], pt[64:128, 256:384], AF.Copy), reads=[pt], writes=[qkT_tok[tt]])

        interleave((prep(tt, i % 3) for i, tt in enumerate(range(NT))), depth=3)

        ob_free = [P[5], P[0], P[1], PT[0]]
        sb_free = [P[2], P[3], P[4], PT[1]]

        def BA(bk):
            return bk[:, :].bitcast(F32) if bk in (PT[0], PT[1]) else bk[:, :]
        pt_free = list(S["pT"])
        rec_free = list(zip(S["rec"], S["rec2"]))

        def core(q0, nq, keys, g, hp, par):
            head = 2 * hp + g
            pr = slice(hp * 64, hp * 64 + 64)
            nqt = nq * 128
            qs = slice(q0 * 128, q0 * 128 + nqt)
            ob = ob_free.pop(0)
            qtoks = [qkT_tok[t] for t in range(q0, q0 + nq)]
            for ki, (kt, mi) in enumerate(keys):
                sbk = sb_free.pop(0)
                k.mm(sbk, [(BA(sbk)[:, 0:nqt], [(kz[:, hp, kt * 128:(kt + 1) * 128], qkT_[:, g, qs])])], reads=[qkT_tok[kt]] + qtoks)
                yield
                pT = pt_free.pop(0)
                k.op("act", lambda e: e.activation(pT[:, 0:nqt], BA(sbk)[:, 0:nqt], AF.Exp, scale=0.125, bias=-SM_SHIFT), reads=[sbk], writes=[pT])
                sb_free.append(sbk)
                if mi is not None:
                    k.op("dve", lambda e: e.tensor_tensor(pT[:, 0:nqt], pT[:, 0:nqt], msk[:, mi, 0:nqt], ALU.mult), reads=[pT, msk], writes=[pT])
                yield
                k.mm(ob, [(BA(ob)[:, 0:nqt], [(vaug2[:, kt, hp, g, :], pT[:, 0:nqt])])], reads=[pT, tm_tok[kt]],
                     start=(ki == 0), stop=(ki == len(keys) - 1))
                pt_free.append(pT)
            yield
            orow = slice(g * 64, g * 64 + 64)
            drow = slice((1 - g) * 64, (1 - g) * 64 + 64)
            rec, rec2 = rec_free.pop(0)
            if which == "swa":
                k.op("dve", lambda e: e.tensor_scalar(rec[drow, 0:nqt], BA(ob)[drow, 0:nqt], sinkE[drow, head:head + 1], None, ALU.add),
                     reads=[ob, sinkE], writes=[rec])
                k.op("dve", lambda e: e.reciprocal(rec[drow, 0:nqt], rec[drow, 0:nqt]), reads=[rec], writes=[rec])
            else:
                k.op("dve", lambda e: e.reciprocal(rec[drow, 0:nqt], BA(ob)[drow, 0:nqt]), reads=[ob], writes=[rec])
            yield
            k.op("act", lambda e: e.activation(rec2[orow, 0:nqt], rec[drow, 0:nqt], AF.Copy), reads=[rec], writes=[rec2])
            yield
            kc = 2 * m + hp
            k.op("dve", lambda e: e.tensor_tensor(ycT[orow, kc, qs], BA(ob)[orow, 0:nqt], rec2[orow, 0:nqt], ALU.mult),
                 reads=[ob, rec2], writes=[ycT_tok[m][t] for t in range(q0, q0 + nq)])
            ob_free.append(ob)
            rec_free.append((rec, rec2))

        units = []
        ui = 0
        qgroups = [(2 + 4 * i, 4) for i in range(4)]
        if l < LAYERS - 1 or LAYERS == 1:
            qgroups = qgroups + [(0, 2)]
        for (q0, nq) in qgroups:
            if q0 == 0:
                keys = [(0, None), (1, None)]
            elif which == "gqa":
                keys = [(kt, None) for kt in range(NT)]
            else:
                keys = [(0, None), (1, None)]
                for r in range(-1, 5):
                    j = q0 + r
                    if 2 <= j < NT:
                        keys.append((j, r + 1))
            for g in range(2):
                for hp in range(2):
                    units.append(core(q0, nq, keys, g, hp, ui % 2))
                    ui += 1
        interleave(units, depth=4)

    def ret_mixer(l):
        k.barrier()
        use_set(SET_RET)
        load_win(l, "ret")
        ktf, ktb, vr, sg = tmA, tmB, tmC, tmD
        RB = [[P[0], P[1]], [P[2], P[3]]]
        rtmp = [(f2, f3, f4), (f5, SET_RET["f3b"], SET_RET["f4b"])]

        def prep(tt, par):
            p0, p1 = RB[par]
            for (pb, c0) in ((p0, 0), (p1, 512)):
                k.mm(pb, [(pb[:, 0:512], [(hT[:, kc, tt * 128:(tt + 1) * 128], win_sb[:, kc, c0:c0 + 512]) for kc in range(8)])],
                     reads=[hT_tok[tt], win_sb])
            fa, qb = f1[par], b2[par]
            t2, t3, t4 = rtmp[par]
            yield
            k.op("act", lambda e: e.activation(fa[:], p0[:, 0:512], AF.Copy), reads=[p0], writes=[fa])
            k.op("act", lambda e: e.activation(vr[:, tt, :], p1[:, 0:256], AF.Copy), reads=[p1], writes=[tm_tok[tt]])
            k.op("act", lambda e: e.activation(t2[:, 0:256], p1[:, 256:512], AF.Exp, scale=-1.0), reads=[p1], writes=[t2])
            yield
            k.op("act", lambda e: e.activation(t2[:, 0:256], t2[:, 0:256], AF.Ln, bias=1.0), reads=[t2], writes=[t2])
            k.op("act", lambda e: e.activation(t2[:, 0:256], t2[:, 0:256], AF.Exp, scale=-1.0), reads=[t2], writes=[t2])
            k.op("dve", lambda e: e.tensor_tensor(sg[:, tt, :], t2[:, 0:256], p1[:, 256:512], ALU.mult), reads=[t2, p1], writes=[tm_tok[tt]])
            yield
            rope(qb, fa, 8, tt, t3, t4)
            yield
            k.op("dve", lambda e: e.tensor_tensor(ktf[:, tt, :].rearrange("p (h d) -> p h d", h=4),
                                                  qb[:, 256:512].rearrange("p (h d) -> p h d", h=4),
                                                  bc(dk[:, 0:4].unsqueeze(2), [128, 4, 64]), ALU.mult), reads=[qb, dk], writes=[tm_tok[tt]])
            k.op("pool", lambda e: e.tensor_tensor(ktb[:, tt, :].rearrange("p (h d) -> p h d", h=4),
                                                   qb[:, 256:512].rearrange("p (h d) -> p h d", h=4),
                                                   bc(dk[:, 4:8].unsqueeze(2), [128, 4, 64]), ALU.mult), reads=[qb, dk], writes=[tm_tok[tt]])
            pt = PT[par]
            k.transposes(pt, [(pt[:, i * 128:(i + 1) * 128], qb[:, i * 128:(i + 1) * 128]) for i in range(4)], ident[:], reads=[qb, ident])
            yield
            k.op("act", lambda e: e.activation(qkT[:, 0:4, tt * 128:(tt + 1) * 128],
                                               pt[:, 0:512].rearrange("p (c t) -> p c t", c=4), AF.Copy), reads=[pt], writes=[qkT_tok[tt]])

        interleave((prep(tt, tt % 2) for tt in range(NT)), depth=2)
        k.barrier()
        allq = qkT_tok
        for d_ in range(2):
            for g in range(2):
                j = d_ * 2 + g
                k.op("dve" if g == 0 else "pool",
                     lambda e: e.tensor_tensor(qkT[:, 4 + j, :].rearrange("p (c t) -> p c t", c=NT),
                                               qkT[:, g, :].rearrange("p (c t) -> p c t", c=NT),
                                               bc(dq[:, j, :].unsqueeze(1), [128, NT, 128]), ALU.mult),
                     reads=allq + [dq], writes=allq)

        def u_mm(src, tt):
            ub = nb()
            k.mm(ub, [(ub[hp * 64:hp * 64 + 64, g * 64:g * 64 + 64],
                       [(src[:, tt, (2 * g + hp) * 64:(2 * g + hp) * 64 + 64], vr[:, tt, (2 * g + hp) * 64:(2 * g + hp) * 64 + 64])])
                      for g in range(2) for hp in range(2)], reads=[tm_tok[tt]])
            return ub

        def s_update(ub, d_):
            for g in range(2):
                k.op("dve", lambda e: e.scalar_tensor_tensor(sfst[:, g, :], sfst[:, g, :], a128c[:, d_ * 2 + g:d_ * 2 + g + 1],
                                                             ub[:, g * 64:g * 64 + 64], ALU.mult, ALU.add),
                     reads=[sfst, a128c, ub], writes=[sfst])

        chain = [1, 0] + list(range(NT - 1, 1, -1))
        k.op("dve", lambda e: e.memset(sfst[:], 0.0), writes=[sfst])
        for ci in range(len(chain) - 1):
            cur, nx = chain[ci], chain[ci + 1]
            ub = u_mm(ktb, cur)
            s_update(ub, 1)
            k.op("act", lambda e: e.activation(sball[:, nx, :, :], sfst[:], AF.Copy), reads=[sfst], writes=[sball])
        k.op("dve", lambda e: e.memset(sfst[:], 0.0), writes=[sfst])
        sf_free = list(sfbf) + list(SET_RET["sfbf2"])
        st_ = {"cur": None}
        am_sets = [[b3[0], b3[1]], [b2[0], b2[1]]]
        sq_tmp = [f5, f2]
        smalls = [small, small2]

        def r3(tt, par):
            need_out = not (tt < 2 and l == LAYERS - 1 and LAYERS > 1)
            incoming = st_["cur"]
            if tt < NT - 1:
                ubT = PT[par]
                ub = ubT[:, :].bitcast(F32)
                k.mm(ubT, [(ub[hp * 64:hp * 64 + 64, g * 64:g * 64 + 64],
                            [(ktf[:, tt, (2 * g + hp) * 64:(2 * g + hp) * 64 + 64], vr[:, tt, (2 * g + hp) * 64:(2 * g + hp) * 64 + 64])])
                           for g in range(2) for hp in range(2)], reads=[tm_tok[tt]])
                for g in range(2):
                    k.op("dve", lambda e: e.scalar_tensor_tensor(sfst[:, g, :], sfst[:, g, :], a128c[:, g:g + 1],
                                                                 ub[:, g * 64:g * 64 + 64], ALU.mult, ALU.add),
                         reads=[sfst, a128c, ubT], writes=[sfst])
                outb = sf_free.pop(0)
                k.op("act", lambda e: e.activation(outb[:], sfst[:], AF.Copy), reads=[sfst], writes=[outb])
                st_["cur"] = outb
            yield
            if not need_out:
                if incoming is not None:
                    sf_free.append(incoming)
                return
            sm = smalls[par]
            fo = f1[par]
            ab = P[par * 3 + 0]
            for hp in range(2):
                pr = slice(hp * 64, hp * 64 + 64)
                k.mm(ab, [(ab[:, g * 128:(g + 1) * 128],
                           [(qkT[pr, 2 + g, tt * 128:(tt + 1) * 128], qkT[pr, g, tt * 128:(tt + 1) * 128])]) for g in range(2)],
                     reads=[qkT_tok[tt]])
                yield
                am = am_sets[par][hp]
                k.op("dve", lambda e: e.tensor_tensor(am[:, 0:256].rearrange("p (g t) -> p g t", g=2),
                                                      ab[:, 0:256].rearrange("p (g t) -> p g t", g=2),
                                                      mT[:, hp::2, :], ALU.mult), reads=[ab, mT], writes=[am])
                yield
                ob = P[par * 3 + 1 + hp]
                groups = []
                rd = [am, tm_tok[tt], qkT_tok[tt]]
                for g in range(2):
                    h = 2 * g + hp
                    pairs = [(am[:, g * 128:(g + 1) * 128], vr[:, tt, h * 64:(h + 1) * 64])]
                    if tt != 0:
                        pairs.append((qkT[pr, 4 + g, tt * 128:(tt + 1) * 128], incoming[pr, g, :]))
                    if tt != 1:
                        pairs.append((qkT[pr, 6 + g, tt * 128:(tt + 1) * 128], sball[pr, tt, g, :]))
                    groups.append((ob[:, g * 64:(g + 1) * 64], pairs))
                if tt != 0:
                    rd.append(incoming)
                if tt != 1:
                    rd.append(sball)
                k.mm(ob, groups, reads=rd)
                yield
                k.op("act", lambda e: e.activation(fo[:, 0:256].rearrange("p (g hp d) -> p g hp d", g=2, hp=2)[:, :, hp, :],
                                                   ob[:, 0:128].rearrange("p (g d) -> p g d", g=2), AF.Copy), reads=[ob], writes=[fo])
            if incoming is not None:
                sf_free.append(incoming)
            yield
            tq = sq_tmp[par]
            k.op("dve", lambda e: e.tensor_reduce(sm[:, 0:4], fo[:, 0:256].rearrange("p (h d) -> p h d", h=4), AX.X, ALU.add), reads=[fo], writes=[sm])
            k.op("dve", lambda e: e.tensor_tensor(tq[:, 0:256], fo[:, 0:256], fo[:, 0:256], ALU.mult), reads=[fo], writes=[tq])
            k.op("dve", lambda e: e.tensor_reduce(sm[:, 4:8], tq[:, 0:256].rearrange("p (h d) -> p h d", h=4), AX.X, ALU.add), reads=[tq], writes=[sm])
            k.op("dve", lambda e: e.tensor_scalar(sm[:, 0:8], sm[:, 0:8], 1.0 / 64, None, ALU.mult), reads=[sm], writes=[sm])
            k.op("dve", lambda e: e.tensor_tensor(sm[:, 8:12], sm[:, 0:4], sm[:, 0:4], ALU.mult), reads=[sm], writes=[sm])
            k.op("dve", lambda e: e.tensor_tensor(sm[:, 8:12], sm[:, 4:8], sm[:, 8:12], ALU.subtract), reads=[sm], writes=[sm])
            k.op("dve", lambda e: e.tensor_scalar(sm[:, 8:12], sm[:, 8:12], EPS, None, ALU.add), reads=[sm], writes=[sm])
            yield
            k.op("pool", lambda e: e.tensor_tensor(sm[:, 12:16], sm[:, 8:12], C("mhalf", 0, 4), ALU.pow), reads=[sm, cst], writes=[sm])
            yield
            fv = fo[:, 0:256].rearrange("p (h d) -> p h d", h=4)
            k.op("dve", lambda e: e.tensor_tensor(fv, fv, bc(sm[:, 0:4].unsqueeze(2), [128, 4, 64]), ALU.subtract), reads=[fo, sm], writes=[fo])
            k.op("dve", lambda e: e.tensor_tensor(fv, fv, bc(sm[:, 12:16].unsqueeze(2), [128, 4, 64]), ALU.mult), reads=[fo, sm], writes=[fo])
            ycb = yc[par]
            k.op("dve", lambda e: e.tensor_tensor(ycb[:], fo[:, 0:256], sg[:, tt, :], ALU.mult), reads=[fo, tm_tok[tt]], writes=[ycb])
            yield
            emit_y(ycb, 0, tt, PT[par])

        interleave((r3(tt, tt % 2) for tt in range(NT)), depth=2)

    def hgrn_mixer(l):
        k.barrier()
        S = SET_HG
        win = S["win_sb"]
        sfst_, sfbf_, sball_ = S["sfst"], S["sfbf"], S["sball"]
        c0, n = MIX_COLS["hgrn"]
        k.dma("sp", win, win[:, :, 0:n], win_b[l], win_b[l][:, :].rearrange("p (c n) -> p c n", c=8)[:, :, c0:c0 + n])
        hrot = {}

        def hr(lst):
            key_ = id(lst[0])
            hrot[key_] = hrot.get(key_, -1) + 1
            return lst[hrot[key_] % len(lst)]

        def proj(pb, tt, lo, w):
            k.mm(pb, [(pb[:, 0:w], [(hT[:, kc, tt * 128:(tt + 1) * 128], win[:, kc, lo:lo + w]) for kc in range(8)])],
                 reads=[hT_tok[tt], win])

        def gate(th_ap, th_t, key_ap, key_t, logf_ap, logf_t, ncol):
            nd = ncol // 256
            fv = th_ap.rearrange("p (a d) -> p a d", a=nd)
            k.op("dve", lambda e: e.tensor_tensor(fv, fv, bc(hB[:, :].unsqueeze(1), [128, nd, 256]), ALU.mult), reads=[th_t, hB], writes=[th_t])
            k.op("pool", lambda e: e.tensor_tensor(fv, fv, bc(hA[:, :].unsqueeze(1), [128, nd, 256]), ALU.add), reads=[th_t, hA], writes=[th_t])
            k.op("pool", lambda e: e.tensor_scalar(th_ap, th_ap, TINY, None, ALU.max), reads=[th_t], writes=[th_t])
            k.op("dve", lambda e: e.tensor_scalar(key_ap, th_ap, -1.0, 1.0, ALU.mult, ALU.add), reads=[th_t], writes=[key_t])
            k.op("act", lambda e: e.activation(logf_ap, th_ap, AF.Ln), reads=[th_t], writes=[logf_t])

        def u_mm(kh_t, v_t, ci):
            ub = nb()
            rows = slice(ci * 64, ci * 64 + 64)
            k.mm(ub, [(ub[hp * 64:hp * 64 + 64, g * 64:g * 64 + 64],
                       [(kh_t[rows, (2 * g + hp) * 64:(2 * g + hp) * 64 + 64], v_t[rows, (2 * g + hp) * 64:(2 * g + hp) * 64 + 64])])
                      for g in range(2) for hp in range(2)], reads=[kh_t, v_t])
            return ub

        def s_update(ub, a_t, ci):
            for g in range(2):
                k.op("dve", lambda e: e.scalar_tensor_tensor(sfst_[:, g, :], sfst_[:, g, :], a_t[:, g * 2 + ci:g * 2 + ci + 1],
                                                             ub[:, g * 64:g * 64 + 64], ALU.mult, ALU.add),
                     reads=[sfst_, a_t, ub], writes=[sfst_])

        def chunk_decay(logf_ap, logf_t, a_t):
            pa = nb()
            k.mm(pa, [(pa[:, g * 2:g * 2 + 2], [(logf_ap[:, g * 128:(g + 1) * 128], C("chunkind"))]) for g in range(2)],
                 reads=[logf_t, cst])
            k.op("act", lambda e: e.activation(a_t[:, 0:4], pa[:, 0:4], AF.Exp), reads=[pa], writes=[a_t])

        k.op("dve", lambda e: e.memset(sfst_[:], 0.0), writes=[sfst_])
        order = [1, 0] + list(range(NT - 1, 1, -1))
        B0, B1, B2, B3, B4, B5 = P

        def chunk_decay2(logf_ap, logf_t, a_t, pa, c0_):
            k.mm(pa, [(pa[:, c0_ + g * 2:c0_ + g * 2 + 2], [(logf_ap[:, g * 128:(g + 1) * 128], C("chunkind"))]) for g in range(2)],
                 reads=[logf_t, cst])
            k.op("act", lambda e: e.activation(a_t[:, 0:4], pa[:, c0_:c0_ + 4], AF.Exp), reads=[pa], writes=[a_t])

        def u_mm2(kh_t, v_t, ci):
            ub = B5
            rows = slice(ci * 64, ci * 64 + 64)
            uo = 256 + ci * 128
            k.mm(ub, [(ub[hp * 64:hp * 64 + 64, uo + g * 64:uo + g * 64 + 64],
                       [(kh_t[rows, (2 * g + hp) * 64:(2 * g + hp) * 64 + 64], v_t[rows, (2 * g + hp) * 64:(2 * g + hp) * 64 + 64])])
                      for g in range(2) for hp in range(2)], reads=[kh_t, v_t])
            return ub, uo

        def s_update2(ubo, a_t, ci):
            ub, uo = ubo
            for g in range(2):
                k.op("dve", lambda e: e.scalar_tensor_tensor(sfst_[:, g, :], sfst_[:, g, :], a_t[:, g * 2 + ci:g * 2 + ci + 1],
                                                             ub[:, uo + g * 64:uo + g * 64 + 64], ALU.mult, ALU.add),
                     reads=[sfst_, a_t, ub], writes=[sfst_])

        def gate2(th_ap, th_t, key_ap, key_t, logf_ap, logf_t, ncol):
            nd = ncol // 256
            fv = th_ap.rearrange("p (a d) -> p a d", a=nd)
            k.op("dve", lambda e: e.tensor_scalar(th_ap, th_ap, 1e18, None, ALU.min), reads=[th_t], writes=[th_t])
            k.op("act", lambda e: e.activation(logf_ap, th_ap, AF.Ln, bias=1.0), reads=[th_t], writes=[logf_t])
            k.op("act", lambda e: e.activation(th_ap, logf_ap, AF.Exp, scale=-1.0), reads=[logf_t], writes=[th_t])
            if l == 0:
                k.op("dve", lambda e: e.tensor_scalar(logf_ap, logf_ap, -1.0, -69.07755278982137, ALU.mult, ALU.max), reads=[logf_t], writes=[logf_t])
                k.op("dve", lambda e: e.tensor_scalar(key_ap, th_ap, -1.0, 1.0, ALU.mult, ALU.add), reads=[th_t], writes=[key_t])
            else:
                k.op("dve", lambda e: e.tensor_tensor(fv, fv, bc(hB[:, :].unsqueeze(1), [128, nd, 256]), ALU.mult), reads=[th_t, hB], writes=[th_t])
                k.op("dve", lambda e: e.tensor_tensor(fv, fv, bc(hA[:, :].unsqueeze(1), [128, nd, 256]), ALU.add), reads=[th_t, hA], writes=[th_t])
                k.op("dve", lambda e: e.tensor_scalar(th_ap, th_ap, TINY, None, ALU.max), reads=[th_t], writes=[th_t])
                k.op("dve", lambda e: e.tensor_scalar(key_ap, th_ap, -1.0, 1.0, ALU.mult, ALU.add), reads=[th_t], writes=[key_t])
                k.op("act", lambda e: e.activation(logf_ap, th_ap, AF.Ln), reads=[th_t], writes=[logf_t])

        p1buf = {}

        def pass1_prep(oi, tt, par):
            pb = B0
            proj(pb, tt, 512, 512)
            th, key, lg = S["thz"][par], S["key"][par], S["lg"][par]
            vt, kh, at, ex = S["vt"][par], S["kh"][par], S["aTsb"][par], S["fe"][par]
            yield
            k.op("act", lambda e: e.activation(th[:, 0:256], pb[:, 0:256], AF.Exp, scale=-1.0), reads=[pb], writes=[th])
            k.op("act", lambda e: e.activation(vt[:], pb[:, 256:512], AF.Copy), reads=[pb], writes=[vt])
            yield
            gate2(th[:, 0:256], th, key[:, 0:256], key, lg[:, 0:256], lg, 256)
            yield
            cb = B1
            k.mm(cb, [(cb[:, 0:256], [(C("lstrict"), lg[:, 0:256])])], reads=[cst, lg])
            chunk_decay2(lg[:, 0:256], lg, at, B2, 0)
            yield
            k.op("act", lambda e: e.activation(ex[:], cb[:, 0:256], AF.Exp), reads=[cb], writes=[ex])
            yield
            k.op("dve", lambda e: e.tensor_tensor(kh[:], key[:, 0:256], ex[:], ALU.mult), reads=[key, ex], writes=[kh])
            p1buf[oi] = (kh, vt, at)

        def pass1_chain(oi, tt):
            kh, vt, at = p1buf[oi]
            for ci in (1, 0):
                cur = tt * 2 + ci
                if ci == 1:
                    nx = cur - 1
                elif oi + 1 < len(order):
                    nx = order[oi + 1] * 2 + 1
                else:
                    nx = None
                if nx is None:
                    break
                ub = u_mm2(kh, vt, ci)
                yield
                s_update2(ub, at, ci)
                yield
                k.op("act", lambda e: e.activation(sball_[:, nx, :, :], sfst_[:], AF.Copy), reads=[sfst_], writes=[sball_])
                yield

        def pass1_all():
            gens = [pass1_prep(oi, tt, oi % 2) for oi, tt in enumerate(order)]
            for _ in gens[0]:
                pass
            for oi, tt in enumerate(order):
                lst = [pass1_chain(oi, tt)]
                if oi + 1 < len(order):
                    lst.append(gens[oi + 1])
                interleave(lst, depth=2)
        pass1_all()

        k.op("dve", lambda e: e.memset(sfst_[:], 0.0), writes=[sfst_])

        def FM(kind, d_, g):
            return kind * 4 + d_ * 2 + g

        p2buf = {}

        def pass2_prep(tt, par):
            p0, p1, p2 = B0, B1, B2
            proj(p0, tt, 0, 512)
            proj(p1, tt, 512, 512)
            proj(p2, tt, 1024, 256)
            thz, sq, tmp, key, lg = S["thz"][par], S["sq"][par], S["tmp"][par], S["key"][par], S["lg"][par]
            vt, sgt, at, fm, khf = S["vt"][2 + par], S["sgt"][par], S["aTsb"][2 + par], S["fm"][par], S["kh"][2 + par]
            yield
            k.op("act", lambda e: e.activation(thz[:, 0:256], p0[:, 256:512], AF.Exp, scale=-1.0), reads=[p0], writes=[thz])
            k.op("act", lambda e: e.activation(thz[:, 256:512], p1[:, 0:256], AF.Exp, scale=-1.0), reads=[p1], writes=[thz])
            k.op("act", lambda e: e.activation(tmp[:, 0:256], p0[:, 0:256], AF.Exp, scale=-1.0), reads=[p0], writes=[tmp])
            k.op("act", lambda e: e.activation(vt[:], p1[:, 256:512], AF.Copy), reads=[p1], writes=[vt])
            yield
            k.op("act", lambda e: e.activation(tmp[:, 0:256], tmp[:, 0:256], AF.Ln, bias=1.0), reads=[tmp], writes=[tmp])
            k.op("act", lambda e: e.activation(tmp[:, 0:256], tmp[:, 0:256], AF.Exp, scale=-1.0), reads=[tmp], writes=[tmp])
            k.op("dve", lambda e: e.tensor_tensor(sq[:, 0:256], tmp[:, 0:256], p0[:, 0:256], ALU.mult), reads=[tmp, p0], writes=[sq])
            gate2(thz[:, :], thz, key[:, :], key, lg[:, :], lg, 512)
            yield
            sqq = S["sqq"][par]
            k.op("act", lambda e: e.activation(sqq[:, 0:256], p2[:, 0:256], AF.Exp, scale=-1.0), reads=[p2], writes=[sqq])
            chunk_decay2(lg[:, 0:256], lg, at, B2, 256)
            yield
            k.op("act", lambda e: e.activation(sqq[:, 0:256], sqq[:, 0:256], AF.Ln, bias=1.0), reads=[sqq], writes=[sqq])
            k.op("act", lambda e: e.activation(sqq[:, 0:256], sqq[:, 0:256], AF.Exp, scale=-1.0), reads=[sqq], writes=[sqq])
            k.op("dve", lambda e: e.tensor_tensor(sgt[:], sqq[:, 0:256], p2[:, 0:256], ALU.mult), reads=[sqq, p2], writes=[sgt])
            for d_ in range(2):
                lgd = lg[:, d_ * 256:(d_ + 1) * 256]
                keyd = key[:, d_ * 256:(d_ + 1) * 256]
                m_c, m_s32, m_s64, m_b64 = (("l32incl", "u32strict", "ustrict", "lincl") if d_ == 0
                                            else ("u32incl", "l32strict", "lstrict", "uincl"))
                ca, cbk = B0, B1
                k.mm(ca, [(ca[:, 0:256], [(C(m_c), lgd)]), (ca[:, 256:512], [(C(m_s32), lgd)])], reads=[cst, lg])
                need64k = (d_ == 0)
                grp = [(cbk[:, 0:256], [(C(m_b64), lgd)])]
                if need64k:
                    grp.append((cbk[:, 256:512], [(C(m_s64), lgd)]))
                k.mm(cbk, grp, reads=[cst, lg])
                yield
                fe_ = S["fe"]
                e_c, e_nc, e_s32, e_b64, e_s64 = [fe_[(par * 2 + d_) * 5 + i] for i in range(5)]
                k.op("act", lambda e: e.activation(e_c[:], ca[:, 0:256], AF.Exp), reads=[ca], writes=[e_c])
                k.op("act", lambda e: e.activation(e_nc[:], ca[:, 0:256], AF.Exp, scale=-1.0), reads=[ca], writes=[e_nc])
                k.op("act", lambda e: e.activation(e_s32[:], ca[:, 256:512], AF.Exp), reads=[ca], writes=[e_s32])
                k.op("act", lambda e: e.activation(e_b64[:], cbk[:, 0:256], AF.Exp), reads=[cbk], writes=[e_b64])
                if need64k:
                    k.op("act", lambda e: e.activation(e_s64[:], cbk[:, 256:512], AF.Exp), reads=[cbk], writes=[e_s64])
                yield
                q8 = S["b1"][par * 2 + d_]
                k.op("dve", lambda e: e.tensor_tensor(q8[:, 0:256], sq[:, 0:256], e_c[:], ALU.mult), reads=[sq, e_c], writes=[q8])
                k.op("pool", lambda e: e.tensor_tensor(q8[:, 256:512], keyd, e_nc[:], ALU.mult), reads=[key, e_nc], writes=[q8])
                k.op("dve", lambda e: e.tensor_tensor(q8[:, 512:768], keyd, e_s32[:], ALU.mult), reads=[key, e_s32], writes=[q8])
                k.op("pool", lambda e: e.tensor_tensor(q8[:, 768:1024], sq[:, 0:256], e_b64[:], ALU.mult), reads=[sq, e_b64], writes=[q8])
                if need64k:
                    k.op("dve", lambda e: e.tensor_tensor(khf[:], keyd, e_s64[:], ALU.mult), reads=[key, e_s64], writes=[khf])
                yield
                pt = PT[0]
                k.transposes(pt, [(pt[:, i * 128:(i + 1) * 128], q8[:, i * 128:(i + 1) * 128]) for i in range(8)],
                             ident[:], reads=[q8, ident])
                yield
                k.op("act", lambda e: e.activation(fm[:, :, :].rearrange("p (kd d g) t -> p kd d g t", kd=4, d=2, g=2)[:, :, d_, :, :],
                                                   pt[:, :].rearrange("p (kd g t) -> p kd g t", kd=4, g=2), AF.Copy),
                     reads=[pt], writes=[fm])
            p2buf[tt] = (vt, sgt, at, fm, khf)

        state = {"sf_prev": None}

        def pass2_main(tt, par):
            vt, sgt, at, fm, khf = p2buf[tt]
            need_out = not (tt < 2 and l == LAYERS - 1 and LAYERS > 1)
            sf_in = [state["sf_prev"], None]
            ub = u_mm2(khf, vt, 0)
            s_update2(ub, at, 0)
            s1 = hr(sfbf_)
            k.op("act", lambda e: e.activation(s1[:], sfst_[:], AF.Copy), reads=[sfst_], writes=[s1])
            sf_in[1] = s1
            if tt < NT - 1:
                ub = u_mm2(khf, vt, 1)
                s_update2(ub, at, 1)
                s2 = hr(sfbf_)
                k.op("act", lambda e: e.activation(s2[:], sfst_[:], AF.Copy), reads=[sfst_], writes=[s2])
                state["sf_prev"] = s2
            yield
            if not need_out:
                return
            sm = S["sm"][par]
            fo = S["fo"][par]
            for hp in range(2):
                pr = slice(hp * 64, hp * 64 + 64)
                a1, a2 = B3, B4
                for (bk, kk) in ((a1, 1), (a2, 2)):
                    k.mm(bk, [(bk[:, (d_ * 2 + g) * 128:(d_ * 2 + g + 1) * 128],
                               [(fm[pr, FM(kk, d_, g), :], fm[pr, FM(0, d_, g), :])]) for d_ in range(2) for g in range(2)],
                         reads=[fm])
                yield
                am = S["am"][hp]
                for bi, (bk, masks) in enumerate(((a1, ("l32incl", "u32incl")), (a2, ("m2f", "m2b")))):
                    for d_ in range(2):
                        o_ = bi * 512 + d_ * 256
                        k.op("dve" if bi == 0 else "pool", lambda e: e.tensor_tensor(
                            am[:, o_:o_ + 256].rearrange("p (g t) -> p g t", g=2),
                            bk[:, d_ * 256:(d_ + 1) * 256].rearrange("p (g t) -> p g t", g=2),
                            bc(C(masks[d_]).unsqueeze(1), [128, 2, 128]), ALU.mult), reads=[bk, cst], writes=[am]) if bi == 0 else \
                        k.op("dve", lambda e: e.tensor_tensor(
                            am[:, o_:o_ + 256].rearrange("p (g t) -> p g t", g=2),
                            bk[:, d_ * 256:(d_ + 1) * 256].rearrange("p (g t) -> p g t", g=2),
                            bc(C(masks[d_]).unsqueeze(1), [128, 2, 128]), ALU.mult), reads=[bk, cst], writes=[am])
                yield
                ob = B5
                rd = [am, vt, fm, sball_]
                for g in range(2):
                    h = 2 * g + hp
                    k.mm(ob, [(ob[:, g * 64:(g + 1) * 64],
                               [(am[:, (bi * 4 + d_ * 2 + g) * 128:(bi * 4 + d_ * 2 + g + 1) * 128], vt[:, h * 64:(h + 1) * 64])
                                for bi in range(2) for d_ in range(2)])], reads=rd, stop=False)
                    groups = []
                    for ci in range(2):
                        chunk = tt * 2 + ci
                        tk = slice(ci * 64, ci * 64 + 64)
                        pairs = []
                        if sf_in[ci] is not None:
                            pairs.append((fm[pr, FM(3, 0, g), tk], sf_in[ci][pr, g, :]))
                            if sf_in[ci] not in rd:
                                rd.append(sf_in[ci])
                        if chunk != 3:
                            pairs.append((fm[pr, FM(3, 1, g), tk], sball_[pr, chunk, g, :]))
                        if pairs:
                            groups.append((ob[ci * 64:ci * 64 + 64, g * 64:(g + 1) * 64], pairs))
                    if groups:
                        k.mm(ob, groups, reads=rd, start=False)
                yield
                k.op("act", lambda e: e.activation(fo[:, 0:256].rearrange("p (g hp d) -> p g hp d", g=2, hp=2)[:, :, hp, :],
                                                   ob[:, 0:128].rearrange("p (g d) -> p g d", g=2), AF.Copy),
                     reads=[ob], writes=[fo])
                yield
            yield
            tq = S["fe"][(par * 2 + 1) * 5 + 4]
            k.op("dve", lambda e: e.tensor_tensor(tq[:, 0:256], fo[:, 0:256], fo[:, 0:256], ALU.mult), reads=[fo], writes=[tq])
            k.op("dve", lambda e: e.tensor_reduce(sm[:, 4:8], tq[:, 0:256].rearrange("p (h d) -> p h d", h=4), AX.X, ALU.add),
                 reads=[tq], writes=[sm])
            k.op("dve", lambda e: e.tensor_scalar(sm[:, 8:12], sm[:, 4:8], 1.0 / 64, EPS, ALU.mult, ALU.add), reads=[sm], writes=[sm])
            yield
            k.op("pool", lambda e: e.tensor_tensor(sm[:, 12:16], sm[:, 8:12], C("mhalf", 0, 4), ALU.pow), reads=[sm, cst], writes=[sm])
            yield
            fv2 = fo[:, 0:256].rearrange("p (h d) -> p h d", h=4)
            k.op("dve", lambda e: e.tensor_tensor(fv2, fv2, bc(sm[:, 12:16].unsqueeze(2), [128, 4, 64]), ALU.mult), reads=[fo, sm], writes=[fo])
            ycb = S["yc"][par]
            k.op("dve", lambda e: e.tensor_tensor(ycb[:], fo[:, 0:256], sgt[:], ALU.mult), reads=[fo, sgt], writes=[ycb])
            yield
            emit_y(ycb, 3, tt, PT[1])

        def pass2_all():
            gens = [pass2_prep(tt, tt % 2) for tt in range(NT)]
            for _ in gens[0]:
                pass
            for tt in range(NT):
                lst = [pass2_main(tt, tt % 2)]
                if tt + 1 < NT:
                    lst.append(gens[tt + 1])
                interleave(lst, depth=2)
        pass2_all()

    def x_src(l, j, tt):
        if l == 0:
            if tt < 2:
                return ctx_d, ctx_d[j, tt * 128:(tt + 1) * 128, :]
            return x_d, x_d[j, (tt - 2) * 128:(tt - 1) * 128, :]
        return xs_tok[j][tt], xs_d[j, tt * 128:(tt + 1) * 128, :]

    def phase_a(l, j):
        k.barrier()

        def body(tt, par):
            xt = xbuf3[par]
            st_, ap_ = x_src(l, j, tt)
            k.dma("sp", xt, xt[:], st_, ap_)
            yield
            yield from modulate_gen(xt, tabs["sc1c" if tt < 2 else "sc1"], tabs["sh1c" if tt < 2 else "sh1"], tt, par, PT[tt % 2], mul_eng="dve")

        load_tab("sc1c", 0, l, 4, 1024)
        load_tab("sh1c", 1, l, 4, 0)
        load_tab("sc1", 2, l, j, 1024)
        load_tab("sh1", 3, l, j, 0)
        interleave((body(tt, i % 3) for i, tt in enumerate(range(NT))), depth=3)

    def phase_c(l, j):
        k.barrier()
        k.dma("sp", wout_sb, wout_sb[:], wout_b[l], wout_b[l][:, :].rearrange("p (c n) -> p c n", c=8))
        t0 = 0 if (l < LAYERS - 1 or LAYERS == 1) else 2
        load_ln("ln1g", 3, l, 0)
        load_ln("ln1b", 4, l, 1)
        YB = [[P[0], P[1]], [P[2], P[3]], [P[4], P[5]]]

        def body(tt, par):
            xt, wa, st = xbuf3[par], wkA3[par], stt3[par]
            st_, ap_ = x_src(l, j, tt)
            k.dma("sp", xt, xt[:], st_, ap_)
            ybs = YB[par]
            for c in range(2):
                pb = ybs[c]
                k.mm(pb, [(pb[:, :], [(ycT[:, kc, tt * 128:(tt + 1) * 128], wout_sb[:, kc, c * 512:(c + 1) * 512]) for kc in range(8)])],
                     reads=[ycT_tok[m][tt] for m in range(4)] + [wout_sb])
            yield
            g1 = tabs["g1"]
            for c in range(2):
                k.op("dve", lambda e: e.tensor_tensor(wa[:, c * 512:(c + 1) * 512], ybs[c][:, :], g1[:, c * 512:(c + 1) * 512], ALU.mult),
                     reads=[ybs[c], g1], writes=[wa])
            k.op("dve", lambda e: e.scalar_tensor_tensor(wa[:], xt[:], ALPHA, wa[:], ALU.mult, ALU.add), reads=[xt, wa], writes=[wa])
            yield from ln_stats_gen(wa, st)
            yield
            k.op("act", lambda e: e.activation(wa[:], wa[:], AF.Identity, bias=st[:, 2:3], scale=st[:, 1:2]), reads=[wa, st], writes=[wa])
            yield
            k.op("dve", lambda e: e.tensor_tensor(wa[:], wa[:], tabs["ln1g"][:], ALU.mult), reads=[wa, tabs["ln1g"]], writes=[wa])
            yield
            k.op("dve", lambda e: e.tensor_tensor(xt[:], wa[:], tabs["ln1b"][:], ALU.add), reads=[wa, tabs["ln1b"]], writes=[xt])
            yield
            k.dma("sp", xm_tok[j][tt], xm_d[j, tt * 128:(tt + 1) * 128, :], xt, xt[:])
            yield from modulate_gen(xt, tabs["sc2"], tabs["sh2"], tt, par, PT[tt % 2])

        if t0 == 0:
            load_tab("g1", 0, l, 4, 2048)
            load_tab("sc2", 1, l, 4, 4096)
            load_tab("sh2", 2, l, 4, 3072)
            interleave((body(tt, tt % 3) for tt in range(0, 2)), depth=2)
        load_tab("g1", 0, l, j, 2048)
        load_tab("sc2", 1, l, j, 4096)
        load_tab("sh2", 2, l, j, 3072)
        interleave((body(tt, tt % 3) for tt in range(2, NT)), depth=3)

    hid_tok = [k.tok("hid%d" % i) for i in range(8)]

    def phase_d(l, j):
        k.barrier()
        groups = [(2 + 4 * i, 4) for i in range(4)]
        if l < LAYERS - 1 or LAYERS == 1:
            groups = [(0, 2)] + groups
        last = (l == LAYERS - 1)
        load_ln("ln2g", 3, l, 2)
        load_ln("ln2b", 4, l, 3)
        load_tab("g2c", 0, l, 4, 5120)
        load_tab("g2l", 1, l, j, 5120)
        PTF = [T(PT[i][:, :].bitcast(F32), "ptf%d" % i, psum=True) for i in range(2)]
        acc_banks = [P[0], P[1], P[2], P[3], P[4], P[5], PT[0], PT[1]]

        def acc_ap(bk):
            return bk[:, :].bitcast(F32) if bk in (PT[0], PT[1]) else bk[:, :]

        UPB = Rot([P[4], P[5]])

        def epilogue(tt, s, obs, par):
            xt, wa, st = xbuf[par], wkA[par], stt[par]
            k.dma("sp", xt, xt[:], xm_tok[j][tt], xm_d[j, tt * 128:(tt + 1) * 128, :])
            g2 = tabs["g2c" if tt < 2 else "g2l"]
            yield
            for cc in range(2):
                k.op("dve", lambda e: e.tensor_tensor(wa[:, cc * 512:(cc + 1) * 512], acc_ap(obs[(s, cc)]), g2[:, cc * 512:(cc + 1) * 512], ALU.mult),
                     reads=[obs[(s, cc)], g2], writes=[wa])
            k.op("dve", lambda e: e.scalar_tensor_tensor(wa[:], xt[:], ALPHA, wa[:], ALU.mult, ALU.add), reads=[xt, wa], writes=[wa])
            yield from ln_stats_gen(wa, st)
            yield
            k.op("act", lambda e: e.activation(wa[:], wa[:], AF.Identity, bias=st[:, 2:3], scale=st[:, 1:2]), reads=[wa, st], writes=[wa])
            yield
            k.op("dve", lambda e: e.tensor_tensor(wa[:], wa[:], tabs["ln2g"][:], ALU.mult), reads=[wa, tabs["ln2g"]], writes=[wa])
            yield
            k.op("dve", lambda e: e.tensor_tensor(xt[:], wa[:], tabs["ln2b"][:], ALU.add), reads=[wa, tabs["ln2b"]], writes=[xt])
            yield
            if last and tt < 2:
                pass
            elif last:
                k.dma("sp", out_d, out_d[j, (tt - 2) * 128:(tt - 1) * 128, :], xt, xt[:], is_output=True)
            else:
                k.dma("sp", xs_tok[j][tt], xs_d[j, tt * 128:(tt + 1) * 128, :], xt, xt[:])

        for (t0, ntile) in groups:
            ntok = ntile * 128
            tk = slice(t0 * 128, t0 * 128 + ntok)
            for c in range(8):
                wsb = wup_sb[c % 2]
                k.dma("sp", wsb, wsb[:], wup_b[l], wup_b[l][c].rearrange("p (c n) -> p c n", c=8))
                for jj in range(4):
                    jf = c * 4 + jj
                    pb = UPB()
                    k.mm(pb, [(pb[:, 0:ntok], [(wsb[:, kc, jj * 128:(jj + 1) * 128], hT[:, kc, tk]) for kc in range(8)])],
                         reads=[wsb] + [hT_tok[t] for t in range(t0, t0 + ntile)])
                    fr = fr_x[jf % 2]
                    k.op("act", lambda e: e.activation(fr[:, 0:ntok], pb[:, 0:ntok], AF.Relu), reads=[pb], writes=[fr])
                    k.op("dve", lambda e: e.tensor_tensor(hidT[:, jf, 0:ntok], fr[:, 0:ntok], fr[:, 0:ntok], ALU.mult),
                         reads=[fr], writes=[hid_tok[c]])
            obs = {}
            bi = 0
            for s_ in range(ntile):
                for cc in range(2):
                    obs[(s_, cc)] = acc_banks[bi]
                    bi += 1
            for c in range(8):
                wsb = wdn_sb[c % 2]
                k.dma("sp", wsb, wsb[:], wdn_b[l], wdn_b[l][c].rearrange("p (c n) -> p c n", c=4))
                for s_ in range(ntile):
                    for cc in range(2):
                        ob = obs[(s_, cc)]
                        k.mm(ob, [(acc_ap(ob), [(hidT[:, c * 4 + jj, s_ * 128:(s_ + 1) * 128], wsb[:, jj, cc * 512:(cc + 1) * 512]) for jj in range(4)])],
                             reads=[wsb, hid_tok[c]], start=(c == 0), stop=(c == 7))
            order_ = list(range(ntile))[::-1] if ntile == 4 else list(range(ntile))
            interleave((epilogue(t0 + s_, s_, obs, i % 2) for i, s_ in enumerate(order_)), depth=2)

    for l in range(LAYERS):
        if "S" in stages:
            layer_setup(l)
        for j in range(NB):
            if "A" in stages:
                phase_a(l, j)
            if "R" in stages:
                ret_mixer(l)
            if "G" in stages:
                attn_mixer(l, "gqa")
            if "W" in stages:
                attn_mixer(l, "swa")
            if "H" in stages:
                hgrn_mixer(l)
            if dbg and l == 0 and j == 0:
                k.barrier()
                for kc in range(8):
                    wa = nxt(wkA, "a")
                    for hh in range(0, TT, 1024):
                        w = min(1024, TT - hh)
                        srcT = hT if dbg == "hT" else ycT
                        k.op("dve", lambda e: e.tensor_copy(wa[:, 0:w], srcT[:, kc, hh:hh + w]),
                             reads=[ycT_tok[m][t] for m in range(4) for t in range(NT)] + hT_tok, writes=[wa])
                        k.dma("sp", dbg_d, dbg_d[:, kc * TT + hh:kc * TT + hh + w], wa, wa[:, 0:w], is_output=True)
            if "C" in stages:
                phase_c(l, j)
            if "D" in stages:
                phase_d(l, j)
    k.finish()
    return nc, k


def _perm_win_cols():
    perm = np.arange(NIN)
    for base in (1024, 1536):
        blk = perm[base:base + 256].reshape(4, 64)
        perm[base:base + 256] = blk[[0, 2, 1, 3]].reshape(-1)
    return perm


def prep_shared(inputs, LAYERS=2):
    f = lambda a: np.ascontiguousarray(a, dtype=np.float32)
    perm = _perm_win_cols()

    def pkn(w, kc):
        K, N = w.shape
        return np.ascontiguousarray(w.reshape(kc, 128, N).transpose(1, 0, 2).reshape(128, kc * N))

    sh = {}
    sh["cst"] = make_consts()
    sh["wada"] = f(np.stack([pkn(inputs["w_ada"][l], 8) for l in range(2)]))
    sh["bada"] = f(inputs["b_ada"]).reshape(2, 1, 6144)
    sh["win"] = f(np.stack([pkn(inputs["w_in"][l][:, perm], 8) for l in range(2)]))
    sh["wout"] = f(np.stack([pkn(inputs["w_out"][l], 8) for l in range(2)]))
    wup = []
    wdn = []
    for l in range(2):
        wu = inputs["w_up"][l]
        wup.append(np.stack([pkn(wu[:, c * 512:(c + 1) * 512], 8) for c in range(8)]))
        wd = inputs["w_down"][l]
        wdn.append(np.stack([pkn(wd[c * 512:(c + 1) * 512, :], 4) for c in range(8)]))
    sh["wup"] = f(np.stack(wup))
    sh["wdn"] = f(np.stack(wdn))
    sh["rdl"] = f(inputs["ret_decay_logit"]).reshape(2, 1, 8)
    sh["gain"] = f(np.concatenate([np.tile(inputs["gqa_q_gain"], (1, 4)), np.tile(inputs["gqa_k_gain"], (1, 2))], axis=1)).reshape(2, 1, 384)
    sh["sink"] = f(inputs["swa_sink"]).reshape(2, 1, 4)
    sh["hlb"] = f(inputs["hgrn_lb"]).reshape(2, 1, 256)
    sh["lnp"] = f(np.stack([inputs["ln1_g"], inputs["ln1_b"], inputs["ln2_g"], inputs["ln2_b"]], axis=1)).reshape(2, 4, 1, D)
    return sh


def core_inputs(inputs, sh, core, NB):
    b0 = core * NB
    m = dict(sh)
    m["x"] = np.ascontiguousarray(inputs["x"][b0:b0 + NB], dtype=np.float32)
    m["ctx"] = np.ascontiguousarray(inputs["ctx"][b0:b0 + NB], dtype=np.float32)
    cv = np.concatenate([inputs["c"][b0:b0 + NB], np.zeros((4 - NB, D), np.float32), inputs["c_ctx"][None, :]], axis=0)
    m["cT"] = np.ascontiguousarray(cv.reshape(5, 8, 128).transpose(2, 1, 0).reshape(128, 40), dtype=np.float32)
    return m


_CACHE = {}


def kernel(**inputs):
    inputs = {k_: np.asarray(v) for k_, v in inputs.items()}
    NB = 4
    if "nc" not in _CACHE:
        _CACHE["nc"] = build(NB=NB, LAYERS=2)[0]
    nc = _CACHE["nc"]
    sh = prep_shared(inputs)
    in_maps = [core_inputs(inputs, sh, c, NB) for c in range(8)]
    res = run_bass_kernel_spmd(nc, in_maps, core_ids=list(range(8)))
    out = np.concatenate([np.asarray(r["out"]) for r in res.results], axis=0)
    return out.astype(np.float32)
```
